# Optimizing a Trainium2 kernel written in Bass

```python
import math
import jax
import jax.numpy as jnp
from jax import lax
import numpy as np

D_MODEL = 1024
BATCH = 2
SEQ = 8192
DEPTH = 2

D_MIX = D_MODEL
POOL_WIDTH = D_MIX // 4
POOL_GROUPS = 4
POOL_GC = POOL_WIDTH // POOL_GROUPS
POOL_WINDOWS = (2, 4, 8, 16)
HEAD_DIM = 64
NSA_WIDTH = D_MIX // 2
NSA_HEADS = NSA_WIDTH // HEAD_DIM
NSA_KV_HEADS = 2
NSA_REP = NSA_HEADS // NSA_KV_HEADS
KV_WIDTH = NSA_KV_HEADS * HEAD_DIM
N_BRANCH = 3
GATE_WIDTH = NSA_HEADS * N_BRANCH
S5_WIDTH = D_MIX - POOL_WIDTH - NSA_WIDTH
S5_H = 16
S5_GROUPS = S5_WIDTH // S5_H
S5_P = 64
CMP_STRIDE = 16
CMP_LEN = 32
CMP_HIDDEN = 128
SLC_BLOCK = 64
N_SELECT = 16
WINDOW = 512
Q_BLOCK = 128
FORCE_BONUS = 1000.0
ROT_DIM = HEAD_DIM // 4
ROPE_THETA = 500000.0
EPS = 1e-6
NEG_INF = -1e30
D_FF = 2816
N_MOD = 9
N_IN = POOL_WIDTH + NSA_WIDTH + 6 * KV_WIDTH + GATE_WIDTH + S5_WIDTH
IN_OFFSETS = (POOL_WIDTH, POOL_WIDTH + NSA_WIDTH, POOL_WIDTH + NSA_WIDTH + 6 * KV_WIDTH,
              POOL_WIDTH + NSA_WIDTH + 6 * KV_WIDTH + GATE_WIDTH)
OUT_OFFSETS = (POOL_WIDTH, POOL_WIDTH + NSA_WIDTH)

kernel_name = 'hybrid_pool_nsa_s5_macaron'


def _rms_norm(x, g):
    xf = x.astype(jnp.float32)
    y = xf * lax.rsqrt(jnp.mean(xf * xf, axis=-1, keepdims=True) + EPS)
    return (y * g.astype(jnp.float32)).astype(x.dtype)


def _modulate(h, shift, scale):
    return h * (1.0 + scale[:, None, :]) + shift[:, None, :]


def _swiglu(h, w_in, w_out):
    gate, up = jnp.split(h @ w_in, 2, axis=-1)
    return (jax.nn.silu(gate) * up) @ w_out


def _rope_partial(x, pos):
    half = ROT_DIM // 2
    inv_freq = jnp.exp(-math.log(ROPE_THETA) * jnp.arange(half, dtype=jnp.float32) * (2.0 / ROT_DIM))
    ang = pos[:, None] * inv_freq[None, :]
    cos, sin = jnp.cos(ang), jnp.sin(ang)
    xf = x.astype(jnp.float32)
    x1, x2 = xf[..., :half], xf[..., half:ROT_DIM]
    out = jnp.concatenate([x1 * cos - x2 * sin, x2 * cos + x1 * sin, xf[..., ROT_DIM:]], axis=-1)
    return out.astype(x.dtype)


def _masked_softmax(s, mask):
    return jax.nn.softmax(jnp.where(mask, s.astype(jnp.float32), NEG_INF), axis=-1)


def _pool_mixer(u, pool_w, pool_b, pool_scale):
    B, S, _ = u.shape
    ug = u.reshape(B, S, POOL_GROUPS, POOL_GC)
    count = jnp.arange(1, S + 1, dtype=jnp.float32)[None, :, None]
    outs = []
    for gi, w in enumerate(POOL_WINDOWS):
        v = ug[:, :, gi].astype(jnp.float32)
        cs = jnp.cumsum(v, axis=1)
        lag = jnp.pad(cs[:, :S - w], ((0, 0), (w, 0), (0, 0)))
        outs.append((cs - lag) / jnp.minimum(count, float(w)) - v)
    pooled = jnp.stack(outs, axis=2).astype(u.dtype)
    y = jnp.einsum('bsgc,gcd->bsgd', pooled, pool_w) + pool_b
    return y.reshape(B, S, POOL_WIDTH) * pool_scale


def _nsa_compress(a, pe, w1, w2):
    B, S, G, HD = a.shape
    ch = a.reshape(B, S // CMP_STRIDE, CMP_STRIDE, G, HD)
    blk = jnp.concatenate([ch[:, :-1], ch[:, 1:]], axis=2) + pe[None, None, :, None, :]
    blk = blk.transpose(0, 1, 3, 2, 4).reshape(B, S // CMP_STRIDE - 1, G, CMP_LEN * HD)
    return (jax.nn.gelu(blk @ w1) @ w2).transpose(0, 2, 1, 3)


def _nsa_mixer(u_q, u_kv, u_gate, q_norm, k_norm, cmp_pe, cmp_k_w1, cmp_k_w2, cmp_v_w1, cmp_v_w2):
    B, S, _ = u_q.shape
    G, R, HD = NSA_KV_HEADS, NSA_REP, HEAD_DIM
    dt = u_q.dtype
    pos = jnp.arange(S, dtype=jnp.float32)
    q = u_q.reshape(B, S, G, R, HD).transpose(0, 2, 3, 1, 4)
    q = _rope_partial(_rms_norm(q, q_norm), pos)
    kc, vc, ks, vs, kw, vw = [a.reshape(B, S, G, HD) for a in jnp.split(u_kv, 6, axis=-1)]

    n_cmp = S // CMP_STRIDE - 1
    cmp_end = jnp.arange(n_cmp) * CMP_STRIDE + CMP_LEN - 1
    k_cmp = _rope_partial(_rms_norm(_nsa_compress(kc, cmp_pe[0], cmp_k_w1, cmp_k_w2), k_norm[0]),
                          cmp_end.astype(jnp.float32))
    v_cmp = _nsa_compress(vc, cmp_pe[1], cmp_v_w1, cmp_v_w2)
    k_slc = _rope_partial(_rms_norm(ks.transpose(0, 2, 1, 3), k_norm[1]), pos)
    v_slc = vs.transpose(0, 2, 1, 3)
    k_win = _rope_partial(_rms_norm(kw.transpose(0, 2, 1, 3), k_norm[2]), pos)
    v_win = vw.transpose(0, 2, 1, 3)
    gates = jax.nn.sigmoid(u_gate.astype(jnp.float32).reshape(B, S, G, R, N_BRANCH)
                           .transpose(0, 2, 3, 1, 4))

    n_slc = S // SLC_BLOCK
    n_top = min(N_SELECT, n_slc)
    k_blocks = k_slc.reshape(B, G, n_slc, SLC_BLOCK, HD)
    v_blocks = v_slc.reshape(B, G, n_slc, SLC_BLOCK, HD)
    c_start = jnp.arange(n_cmp)[:, None] * CMP_STRIDE
    s_start = jnp.arange(n_slc)[None, :] * SLC_BLOCK
    overlap = jnp.clip(jnp.minimum(c_start + CMP_LEN, s_start + SLC_BLOCK) - jnp.maximum(c_start, s_start),
                       0, None).astype(jnp.float32) / CMP_LEN
    k_win_pad = jnp.pad(k_win, ((0, 0), (0, 0), (WINDOW, 0), (0, 0)))
    v_win_pad = jnp.pad(v_win, ((0, 0), (0, 0), (WINDOW, 0), (0, 0)))
    bi = jnp.arange(B)[:, None, None, None]
    gi = jnp.arange(G)[None, :, None, None]
    blk_ids = jnp.arange(n_slc)
    scale = HEAD_DIM ** -0.5

    def one_block(b):
        start = b * Q_BLOCK
        t = start + jnp.arange(Q_BLOCK)
        qb = lax.dynamic_slice_in_dim(q, start, Q_BLOCK, axis=3)
        m_cmp = cmp_end[None, :] <= t[:, None]
        s = jnp.einsum('bgrqd,bgnd->bgrqn', qb, k_cmp) * scale
        p_cmp = _masked_softmax(s, m_cmp) * jnp.any(m_cmp, axis=-1)[:, None].astype(jnp.float32)
        o_cmp = jnp.einsum('bgrqn,bgnd->bgrqd', p_cmp.astype(dt), v_cmp)
        imp = jnp.einsum('bgrqn,nj->bgqj', p_cmp, overlap)
        cur = t // SLC_BLOCK
        forced = (blk_ids[None, :] == 0) | (blk_ids[None, :] == cur[:, None]) | (blk_ids[None, :] == cur[:, None] - 1)
        valid = blk_ids[None, :] * SLC_BLOCK <= t[:, None]
        imp = jnp.where(valid, imp + jnp.where(forced, FORCE_BONUS, 0.0), NEG_INF)
        _, idx = lax.top_k(imp, n_top)
        k_sel = k_blocks[bi, gi, idx].reshape(B, G, Q_BLOCK, n_top * SLC_BLOCK, HD)
        v_sel = v_blocks[bi, gi, idx].reshape(B, G, Q_BLOCK, n_top * SLC_BLOCK, HD)
        tok = (idx[..., None] * SLC_BLOCK + jnp.arange(SLC_BLOCK)).reshape(B, G, Q_BLOCK, n_top * SLC_BLOCK)
        m_slc = (tok <= t[:, None])[:, :, None]
        s = jnp.einsum('bgrqd,bgqkd->bgrqk', qb, k_sel) * scale
        o_slc = jnp.einsum('bgrqk,bgqkd->bgrqd', _masked_softmax(s, m_slc).astype(dt), v_sel)
        k_w = lax.dynamic_slice_in_dim(k_win_pad, start, WINDOW + Q_BLOCK, axis=2)
        v_w = lax.dynamic_slice_in_dim(v_win_pad, start, WINDOW + Q_BLOCK, axis=2)
        kp = start - WINDOW + jnp.arange(WINDOW + Q_BLOCK)
        m_win = (kp[None, :] <= t[:, None]) & (kp[None, :] > t[:, None] - WINDOW) & (kp[None, :] >= 0)
        s = jnp.einsum('bgrqd,bgkd->bgrqk', qb, k_w) * scale
        o_win = jnp.einsum('bgrqk,bgkd->bgrqd', _masked_softmax(s, m_win).astype(dt), v_w)
        gb = lax.dynamic_slice_in_dim(gates, start, Q_BLOCK, axis=3)
        o = gb[..., 0:1] * o_cmp + gb[..., 1:2] * o_slc + gb[..., 2:3] * o_win
        return o.astype(dt)

    o = lax.map(one_block, jnp.arange(S // Q_BLOCK))
    return o.transpose(1, 0, 4, 2, 3, 5).reshape(B, S, NSA_WIDTH)


def _s5_mixer(u, lam_re, lam_im, log_dt, b_re, b_im, c_re, c_im, d_skip, glu_w, glu_b):
    B, S, _ = u.shape
    f32 = jnp.float32
    uf = u.astype(f32).reshape(B, S, S5_GROUPS, S5_H)
    lam = lax.complex(lam_re.astype(f32), lam_im.astype(f32))
    step = jnp.exp(log_dt.astype(f32))[:, None]
    lam_bar = jnp.exp(lam * step)
    b_bar = lax.complex(b_re.astype(f32), b_im.astype(f32)) * ((lam_bar - 1.0) / lam)[..., None]
    bu = jnp.einsum('gph,bsgh->bsgp', b_bar, uf.astype(jnp.complex64))
    a = jnp.broadcast_to(lam_bar, bu.shape)

    def combine(e1, e2):
        return (e2[0] * e1[0], e2[0] * e1[1] + e2[1])

    _, states = lax.associative_scan(combine, (a, bu), axis=1)
    c_mat = lax.complex(c_re.astype(f32), c_im.astype(f32))
    y = jnp.einsum('ghp,bsgp->bsgh', c_mat, states).real + d_skip.astype(f32) * uf
    y = jax.nn.gelu(y.reshape(B, S, S5_WIDTH))
    y = y * jax.nn.sigmoid(y @ glu_w.astype(f32) + glu_b.astype(f32))
    return y.astype(u.dtype)


def _hybrid_layer(x, c, ada_w, ada_b, norm_ffn1, ffn1_w_in, ffn1_w_out, norm_mix, w_in, w_out, out_norm,
                  pool_w, pool_b, pool_scale, q_norm, k_norm, cmp_pe, cmp_k_w1, cmp_k_w2, cmp_v_w1, cmp_v_w2,
                  s5_lam_re, s5_lam_im, s5_log_dt, s5_b_re, s5_b_im, s5_c_re, s5_c_im, s5_d, glu_w, glu_b,
                  norm_ffn2, ffn2_w_in, ffn2_w_out):
    B = x.shape[0]
    mod = (jax.nn.silu(c) @ ada_w + ada_b).reshape(B, N_MOD, D_MODEL)
    sh1, sc1, g1, sh2, sc2, g2, sh3, sc3, g3 = [mod[:, i] for i in range(N_MOD)]
    h = _modulate(_rms_norm(x, norm_ffn1), sh1, sc1)
    x = x + 0.5 * g1[:, None, :] * _swiglu(h, ffn1_w_in, ffn1_w_out)
    h = _modulate(_rms_norm(x, norm_mix), sh2, sc2)
    u_pool, u_q, u_kv, u_gate, u_s5 = jnp.split(h @ w_in, IN_OFFSETS, axis=-1)
    n_pool, n_nsa, n_s5 = jnp.split(out_norm, OUT_OFFSETS)
    y_pool = _rms_norm(_pool_mixer(u_pool, pool_w, pool_b, pool_scale), n_pool)
    y_nsa = _rms_norm(_nsa_mixer(u_q, u_kv, u_gate, q_norm, k_norm, cmp_pe, cmp_k_w1, cmp_k_w2,
                                 cmp_v_w1, cmp_v_w2), n_nsa)
    y_s5 = _rms_norm(_s5_mixer(u_s5, s5_lam_re, s5_lam_im, s5_log_dt, s5_b_re, s5_b_im, s5_c_re, s5_c_im,
                               s5_d, glu_w, glu_b), n_s5)
    y = jnp.concatenate([y_pool, y_nsa, y_s5], axis=-1) @ w_out
    x = x + g2[:, None, :] * y
    h = _modulate(_rms_norm(x, norm_ffn2), sh3, sc3)
    x = x + 0.5 * g3[:, None, :] * _swiglu(h, ffn2_w_in, ffn2_w_out)
    return x


def setup_inputs(seed: int = 0) -> dict:
    key = jax.random.key(seed)
    keys = iter(jax.random.split(key, 64))
    L, D = DEPTH, D_MODEL

    def nrm(shape, std):
        return std * jax.random.normal(next(keys), shape, jnp.float32)

    def gain(shape):
        return 1.0 + nrm(shape, 0.02)

    return {
        'x': nrm((BATCH, SEQ, D), 1.0),
        'c': nrm((BATCH, D), 1.0),
        'ada_w': nrm((L, D, N_MOD * D), 0.5 * D ** -0.5),
        'ada_b': nrm((L, N_MOD * D), 0.01),
        'norm_ffn1': gain((L, D)),
        'ffn1_w_in': nrm((L, D, 2 * D_FF), D ** -0.5),
        'ffn1_w_out': nrm((L, D_FF, D), D_FF ** -0.5),
        'norm_mix': gain((L, D)),
        'w_in': nrm((L, D, N_IN), D ** -0.5),
        'w_out': nrm((L, D_MIX, D), D_MIX ** -0.5),
        'out_norm': gain((L, D_MIX)),
        'pool_w': nrm((L, POOL_GROUPS, POOL_GC, POOL_GC), POOL_GC ** -0.5),
        'pool_b': nrm((L, POOL_GROUPS, POOL_GC), 0.01),
        'pool_scale': gain((L, POOL_WIDTH)),
        'q_norm': gain((L, HEAD_DIM)),
        'k_norm': gain((L, N_BRANCH, HEAD_DIM)),
        'cmp_pe': nrm((L, 2, CMP_LEN, HEAD_DIM), 0.02),
        'cmp_k_w1': nrm((L, CMP_LEN * HEAD_DIM, CMP_HIDDEN), (CMP_LEN * HEAD_DIM) ** -0.5),
        'cmp_k_w2': nrm((L, CMP_HIDDEN, HEAD_DIM), CMP_HIDDEN ** -0.5),
        'cmp_v_w1': nrm((L, CMP_LEN * HEAD_DIM, CMP_HIDDEN), (CMP_LEN * HEAD_DIM) ** -0.5),
        'cmp_v_w2': nrm((L, CMP_HIDDEN, HEAD_DIM), CMP_HIDDEN ** -0.5),
        's5_lam_re': -0.5 + nrm((L, S5_GROUPS, S5_P), 0.01),
        's5_lam_im': math.pi * jnp.arange(S5_P, dtype=jnp.float32) + nrm((L, S5_GROUPS, S5_P), 0.01),
        's5_log_dt': jax.random.uniform(next(keys), (L, S5_GROUPS), jnp.float32,
                                        math.log(1e-3), math.log(1e-1)),
        's5_b_re': nrm((L, S5_GROUPS, S5_P, S5_H), (2 * S5_H) ** -0.5),
        's5_b_im': nrm((L, S5_GROUPS, S5_P, S5_H), (2 * S5_H) ** -0.5),
        's5_c_re': nrm((L, S5_GROUPS, S5_H, S5_P), S5_P ** -0.5),
        's5_c_im': nrm((L, S5_GROUPS, S5_H, S5_P), S5_P ** -0.5),
        's5_d': nrm((L, S5_GROUPS, S5_H), 1.0),
        'glu_w': nrm((L, S5_WIDTH, S5_WIDTH), S5_WIDTH ** -0.5),
        'glu_b': nrm((L, S5_WIDTH), 0.01),
        'norm_ffn2': gain((L, D)),
        'ffn2_w_in': nrm((L, D, 2 * D_FF), D ** -0.5),
        'ffn2_w_out': nrm((L, D_FF, D), D_FF ** -0.5),
    }


def reference(x, c, ada_w, ada_b, norm_ffn1, ffn1_w_in, ffn1_w_out, norm_mix, w_in, w_out, out_norm,
              pool_w, pool_b, pool_scale, q_norm, k_norm, cmp_pe, cmp_k_w1, cmp_k_w2, cmp_v_w1, cmp_v_w2,
              s5_lam_re, s5_lam_im, s5_log_dt, s5_b_re, s5_b_im, s5_c_re, s5_c_im, s5_d, glu_w, glu_b,
              norm_ffn2, ffn2_w_in, ffn2_w_out):
    for l in range(DEPTH):
        x = _hybrid_layer(x, c, ada_w[l], ada_b[l], norm_ffn1[l], ffn1_w_in[l], ffn1_w_out[l], norm_mix[l],
                          w_in[l], w_out[l], out_norm[l], pool_w[l], pool_b[l], pool_scale[l], q_norm[l],
                          k_norm[l], cmp_pe[l], cmp_k_w1[l], cmp_k_w2[l], cmp_v_w1[l], cmp_v_w2[l],
                          s5_lam_re[l], s5_lam_im[l], s5_log_dt[l], s5_b_re[l], s5_b_im[l], s5_c_re[l],
                          s5_c_im[l], s5_d[l], glu_w[l], glu_b[l], norm_ffn2[l], ffn2_w_in[l], ffn2_w_out[l])
    return x
```

```python
import math
from contextlib import ExitStack
import numpy as np
import concourse.bass as bass
import concourse.mybir as mybir
from concourse.bass_utils import run_bass_kernel_spmd

F32 = mybir.dt.float32
BF16 = mybir.dt.bfloat16
I32 = mybir.dt.int32
AF = mybir.ActivationFunctionType
ALU = mybir.AluOpType
AX = mybir.AxisListType

D = 1024
DFF = 2816
NTOK = 2048
NT = NTOK // 128
S = 8192
EPS = 1e-6
N_IN = 1816
N_DMA_SEMS = 16


class KB:
    def __init__(self, nc):
        self.nc = nc
        self.eng = {'pe': nc.tensor, 'act': nc.scalar, 'dve': nc.vector, 'pool': nc.gpsimd, 'sp': nc.sync}
        self.sem = {e: nc.alloc_semaphore('s_' + e) for e in self.eng}
        self.cnt = {e: 0 for e in self.eng}
        self.seen = {e: {} for e in self.eng}
        self.dsem = [nc.alloc_semaphore('d%d' % i) for i in range(N_DMA_SEMS)]
        self.dcnt = 0
        self.lastw = {}
        self.reads = {}
        self.n_inst = 0

    def _wait(self, e, tok):
        key, sem, val = tok
        if self.seen[e].get(key, 0) >= val:
            return
        self.seen[e][key] = val
        self.eng[e].wait_ge(sem, val)

    def _deps(self, e, reads, writes, pe_acc=False):
        toks = []
        for b in reads:
            t = self.lastw.get(b)
            if t is not None:
                toks.append(t)
        for b in writes:
            t = self.lastw.get(b)
            if t is not None:
                toks.append(t)
            toks.extend(self.reads.get(b, []))
        for t in toks:
            if pe_acc and t[0] == 'pe':
                continue
            self._wait(e, t)

    def _commit(self, tok, reads, writes):
        for b in reads:
            self.reads.setdefault(b, []).append(tok)
        for b in writes:
            self.lastw[b] = tok
            self.reads[b] = []

    def op(self, e, inst_fn, reads=(), writes=(), pe_acc=False):
        self._deps(e, reads, writes, pe_acc)
        inst = inst_fn(self.eng[e])
        self.cnt[e] += 1
        inst.then_inc(self.sem[e], 1)
        tok = (e, self.sem[e], self.cnt[e])
        self._commit(tok, reads, writes)
        self.n_inst += 1
        return inst

    def dma(self, e, out, in_, reads=(), writes=(), **kw):
        i = self.dcnt
        self.dcnt += 1
        s = self.dsem[i % N_DMA_SEMS]
        kk = i // N_DMA_SEMS
        key = 'd%d' % (i % N_DMA_SEMS)
        if kk > 0:
            self._wait(e, (key, s, 16 * kk))
        self._deps(e, reads, writes)
        inst = self.eng[e].dma_start(out=out, in_=in_, **kw)
        inst.then_inc(s, 16)
        tok = (key, s, 16 * (kk + 1))
        self._commit(tok, reads, writes)
        self.n_inst += 1
        return tok

    def finish(self, toks):
        for t in toks:
            self._wait('sp', t)


class Ctx:
    def __init__(self):
        self.nc = bass.Bass("TRN2", target_bir_lowering=False)
        self.es = ExitStack()
        self.k = KB(self.nc)
        self.out_toks = []

    def din(self, name, shape, dt=F32):
        return self.nc.dram_tensor(name, list(shape), dt, kind="ExternalInput").ap()

    def dout(self, name, shape, dt=F32):
        return self.nc.dram_tensor(name, list(shape), dt, kind="ExternalOutput").ap()

    def sb(self, name, shape, dt=F32):
        return self.es.enter_context(self.nc.sbuf_tensor("s_" + name, list(shape), dt))

    def ps(self, name, shape, dt=F32):
        return self.es.enter_context(self.nc.psum_tensor("p_" + name, list(shape), dt))

    def done(self):
        self.k.finish(self.out_toks)
        self.es.close()
        return self.nc


def emit_consts(cx):
    k = cx.k
    ident_f = cx.sb("ident_f", [128, 128], F32)
    ident = cx.sb("ident", [128, 128], BF16)
    ones_r = cx.sb("ones_r", [1, 128], BF16)
    k.op('pool', lambda e: e.memset(ident_f[:], 1.0), writes=['ident_f'])
    k.op('pool', lambda e: e.affine_select(out=ident_f[:], in_=ident_f[:], pattern=[[-1, 128]],
                                           compare_op=ALU.is_equal, fill=0.0, base=0, channel_multiplier=1),
         reads=['ident_f'], writes=['ident_f'])
    k.op('dve', lambda e: e.tensor_copy(ident[:], ident_f[:]), reads=['ident_f'], writes=['ident'])
    k.op('dve', lambda e: e.memset(ones_r[:], 1.0), writes=['ones_r'])
    cx.ident = ident
    cx.ident_f = ident_f
    cx.ones_r = ones_r


def emit_mod(cx, cT_d, adaw_d, adab_d, nmod, ps_keys, ps_tiles):
    k = cx.k
    mod_b = cx.sb("mod_b", [128, nmod, 1024], F32)
    cT = cx.sb("cT", [128, 8], F32)
    sig = cx.sb("csig", [128, 8], F32)
    crep = cx.sb("crep", [128, 8, 128], BF16)
    adab = cx.sb("adab", [1, nmod * 1024], BF16)
    wch = [cx.sb("adaw_ch%d" % i, [128, 8, 256], BF16) for i in range(2)]
    k.dma('sp', cT[:], cT_d[:, :], writes=['cT'])
    k.dma('pool', adab[:], adab_d[:, :], writes=['adab'])
    k.op('act', lambda e: e.activation(out=sig[:], in_=cT[:], func=AF.Sigmoid), reads=['cT'], writes=['csig'])
    k.op('dve', lambda e: e.tensor_tensor(out=sig[:], in0=sig[:], in1=cT[:], op=ALU.mult),
         reads=['csig', 'cT'], writes=['csig'])
    k.op('dve', lambda e: e.tensor_copy(crep[:], sig[:].unsqueeze(2).to_broadcast([128, 8, 128])),
         reads=['csig'], writes=['crep'])
    nch = nmod * 4
    for ch in range(nch):
        w = wch[ch % 2]
        wk = 'adaw_ch%d' % (ch % 2)
        k.dma('pool', w[:], adaw_d[:, ch * 256:(ch + 1) * 256].rearrange("(kc p) n -> p kc n", p=128),
              writes=[wk])
        pk = ps_keys[ch % 2]
        pt = ps_tiles[ch % 2]
        for kc in range(8):
            k.op('pe', lambda e, kc=kc: e.matmul(pt[:, 0:256], crep[:, kc, :], w[:, kc, :], start=(kc == 0), stop=False),
                 reads=['crep', wk], writes=[pk], pe_acc=(kc > 0))
        k.op('pe', lambda e: e.matmul(pt[:, 0:256], cx.ones_r[:, :], adab[:, ch * 256:(ch + 1) * 256], start=False, stop=True),
             reads=['ones_r', 'adab'], writes=[pk], pe_acc=True)
        k.op('act', lambda e: e.copy(out=mod_b[:, ch // 4, (ch % 4) * 256:(ch % 4 + 1) * 256], in_=pt[:, 0:256]),
             reads=[pk], writes=['mod_b'])
    return mod_b


def emit_norm_mod_T(cx, x_t, xkey, gs_b, sh_b, hT, hTkey, col0, pT, pTkey, tag):
    k = cx.k
    W = cx.work
    ss, rstd, hf, hb = W['ss'], W['rstd'], W['hf'], W['hb']
    k.op('act', lambda e: e.activation(out=hf[:], in_=x_t[:], func=AF.Square, accum_out=ss[:, 0:1]),
         reads=[xkey], writes=['hf', 'ss'])
    k.op('act', lambda e: e.activation(out=rstd[:, 0:1], in_=ss[:, 0:1], func=AF.Sqrt, scale=1.0 / D, bias=W['eps'][:, 0:1]),
         reads=['ss', 'eps'], writes=['rstd'])
    k.op('dve', lambda e: e.reciprocal(out=rstd[:, 0:1], in_=rstd[:, 0:1]), reads=['rstd'], writes=['rstd'])
    k.op('dve', lambda e: e.scalar_tensor_tensor(out=hf[:], in0=x_t[:], scalar=rstd[:, 0:1], in1=gs_b,
                                                 op0=ALU.mult, op1=ALU.mult),
         reads=[xkey, 'rstd', 'modd'], writes=['hf'])
    k.op('pool', lambda e: e.tensor_tensor(out=hb[:], in0=hf[:], in1=sh_b, op=ALU.add),
         reads=['hf', 'modd'], writes=['hb'])
    for c in range(8):
        k.op('pe', lambda e, c=c: e.transpose(pT[:, c, :], hb[:, c * 128:(c + 1) * 128], cx.ident[:]),
             reads=['hb', 'ident'], writes=[pTkey], pe_acc=(c > 0))
    k.op('act', lambda e: e.copy(out=hT[:, :, col0:col0 + 128], in_=pT[:, :, :]), reads=[pTkey], writes=[hTkey])


def emit_work(cx):
    W = {}
    W['ss'] = cx.sb("ss", [128, 1], F32)
    W['rstd'] = cx.sb("rstd", [128, 1], F32)
    W['hf'] = cx.sb("hf", [128, 1024], F32)
    W['hb'] = cx.sb("hb", [128, 1024], BF16)
    W['eps'] = cx.sb("eps", [128, 1], F32)
    cx.k.op('dve', lambda e: e.memset(W['eps'][:], EPS), writes=['eps'])
    cx.work = W


GRP = 256


def build_F():
    cx = Ctx()
    k = cx.k
    x_d = cx.din("x", [NTOK, D])
    cT_d = cx.din("cT", [128, 8])
    adaw_d = cx.din("adaw", [D, 3 * D])
    adab_d = cx.din("adab", [1, 3 * D])
    nrm_d = cx.din("nrm", [1, D])
    win_d = cx.din("w_in", [D, 2 * DFF])
    wout_d = cx.din("w_out", [DFF, D])
    xo_d = cx.dout("xo", [NTOK, D])

    emit_consts(cx)
    emit_work(cx)
    Win = cx.sb("Win", [128, 8, 2 * DFF], BF16)
    Wout = cx.sb("Wout", [128, 22, D], BF16)
    nrm_b = cx.sb("nrm_b", [128, D], F32)
    hT = cx.sb("hT", [128, 8, GRP], BF16)
    actT = cx.sb("actT", [128, 22, GRP], BF16)
    sg = [cx.sb("sg%d" % i, [128, GRP], F32) for i in range(2)]
    xt = [cx.sb("xt%d" % i, [128, D], F32) for i in range(2)]
    xo = [cx.sb("xo%d" % i, [128, D], F32) for i in range(2)]
    pT = cx.ps("pT", [128, 8, 128], BF16)
    pg = [cx.ps("pg%d" % i, [128, 512], F32) for i in range(2)]
    pu = [cx.ps("pu%d" % i, [128, 512], F32) for i in range(2)]
    py = [cx.ps("py%d" % i, [128, 512], F32) for i in range(2)]

    k.dma('sp', nrm_b[:], nrm_d[0:1, :].to_broadcast([128, D]), writes=['nrm_b'])
    mod_b = emit_mod(cx, cT_d, adaw_d, adab_d, 3, ['pg0', 'pg1'], pg)
    for kc in range(8):
        k.dma('pool', Win[:, kc, :], win_d[kc * 128:(kc + 1) * 128, :], writes=['Win'])
    for kc in range(22):
        k.dma('pool', Wout[:, kc, :], wout_d[kc * 128:(kc + 1) * 128, :], writes=['Wout'])
    k.op('dve', lambda e: e.scalar_tensor_tensor(out=mod_b[:, 1, :], in0=mod_b[:, 1, :], scalar=1.0, in1=nrm_b[:],
                                                 op0=ALU.add, op1=ALU.mult),
         reads=['mod_b', 'nrm_b'], writes=['modd'])
    k.op('dve', lambda e: e.tensor_scalar(out=mod_b[:, 2, :], in0=mod_b[:, 2, :], scalar1=0.5, scalar2=None, op0=ALU.mult),
         reads=['mod_b', 'modd'], writes=['modd'])
    sh_b, gs_b, hg_b = mod_b[:, 0, :], mod_b[:, 1, :], mod_b[:, 2, :]

    ngrp = NTOK // GRP
    tpg = GRP // 128
    for g in range(ngrp):
        for t in range(tpg):
            ti = g * tpg + t
            k.dma('sp', xt[t][:], x_d[ti * 128:(ti + 1) * 128, :], writes=['xt%d' % t])
            emit_norm_mod_T(cx, xt[t], 'xt%d' % t, gs_b, sh_b, hT, 'hT', t * 128, pT, 'pT', 'f')
        for i in range(22):
            b = i % 2
            for kc in range(8):
                k.op('pe', lambda e, kc=kc: e.matmul(pg[b][:, 0:GRP], Win[:, kc, i * 128:(i + 1) * 128], hT[:, kc, :],
                                                     start=(kc == 0), stop=(kc == 7)),
                     reads=['Win', 'hT'], writes=['pg%d' % b], pe_acc=(kc > 0))
            for kc in range(8):
                k.op('pe', lambda e, kc=kc: e.matmul(pu[b][:, 0:GRP], Win[:, kc, DFF + i * 128:DFF + (i + 1) * 128],
                                                     hT[:, kc, :], start=(kc == 0), stop=(kc == 7)),
                     reads=['Win', 'hT'], writes=['pu%d' % b], pe_acc=(kc > 0))
            k.op('act', lambda e: e.activation(out=sg[b][:], in_=pg[b][:, 0:GRP], func=AF.Silu),
                 reads=['pg%d' % b], writes=['sg%d' % b])
            k.op('dve', lambda e: e.tensor_tensor(out=actT[:, i, :], in0=sg[b][:], in1=pu[b][:, 0:GRP], op=ALU.mult),
                 reads=['sg%d' % b, 'pu%d' % b], writes=['actT'])
        for t in range(tpg):
            ti = g * tpg + t
            for h in range(2):
                for i in range(22):
                    k.op('pe', lambda e, i=i: e.matmul(py[h][:, :], actT[:, i, t * 128:(t + 1) * 128],
                                                       Wout[:, i, h * 512:(h + 1) * 512], start=(i == 0), stop=(i == 21)),
                         reads=['actT', 'Wout'], writes=['py%d' % h], pe_acc=(i > 0))
            for h in range(2):
                k.op('dve', lambda e: e.tensor_tensor(out=xo[t][:, h * 512:(h + 1) * 512], in0=py[h][:, :],
                                                      in1=hg_b[:, h * 512:(h + 1) * 512], op=ALU.mult),
                     reads=['py%d' % h, 'modd'], writes=['xo%d' % t])
            k.op('pool', lambda e: e.tensor_tensor(out=xo[t][:], in0=xo[t][:], in1=xt[t][:], op=ALU.add),
                 reads=['xo%d' % t, 'xt%d' % t], writes=['xo%d' % t])
            cx.out_toks.append(k.dma('sp', xo_d[ti * 128:(ti + 1) * 128, :], xo[t][:], reads=['xo%d' % t]))
    return cx.done()


def emit_headnorm_rope(cx, U, ukey, c0, nh, gain_b, cos_b, sin_b, tmp, sq):
    k = cx.k
    v = U[:, c0:c0 + nh * 64].rearrange("p (h d) -> p h d", d=64)
    t = tmp[:, 0:nh * 64].rearrange("p (h d) -> p h d", d=64)
    ssq = sq[:, 0:nh]
    rs = sq[:, 8:8 + nh]
    k.op('dve', lambda e: e.tensor_tensor(out=t, in0=v, in1=v, op=ALU.mult), reads=[ukey], writes=['hn_tmp'])
    k.op('dve', lambda e: e.tensor_reduce(out=ssq, in_=t, axis=AX.X, op=ALU.add), reads=['hn_tmp'], writes=['hn_sq'])
    k.op('act', lambda e: e.activation(out=rs, in_=ssq, func=AF.Sqrt, scale=1.0 / 64, bias=cx.work['eps'][:, 0:1]),
         reads=['hn_sq', 'eps'], writes=['hn_sq'])
    k.op('dve', lambda e: e.reciprocal(out=rs, in_=rs), reads=['hn_sq'], writes=['hn_sq'])
    k.op('dve', lambda e: e.tensor_tensor(out=v, in0=v, in1=rs.unsqueeze(2).to_broadcast([128, nh, 64]), op=ALU.mult),
         reads=[ukey, 'hn_sq'], writes=[ukey])
    k.op('dve', lambda e: e.tensor_tensor(out=v, in0=v, in1=gain_b.unsqueeze(1).to_broadcast([128, nh, 64]), op=ALU.mult),
         reads=[ukey, 'gains'], writes=[ukey])
    x1 = v[:, :, 0:8]
    x2 = v[:, :, 8:16]
    cb = cos_b.unsqueeze(1).to_broadcast([128, nh, 8])
    sb_ = sin_b.unsqueeze(1).to_broadcast([128, nh, 8])
    a, b, c, d = t[:, :, 0:8], t[:, :, 8:16], t[:, :, 16:24], t[:, :, 24:32]
    k.op('dve', lambda e: e.tensor_tensor(out=a, in0=x1, in1=cb, op=ALU.mult), reads=[ukey, 'rope'], writes=['hn_tmp'])
    k.op('dve', lambda e: e.tensor_tensor(out=b, in0=x2, in1=sb_, op=ALU.mult), reads=[ukey, 'rope'], writes=['hn_tmp'])
    k.op('dve', lambda e: e.tensor_tensor(out=c, in0=x2, in1=cb, op=ALU.mult), reads=[ukey, 'rope'], writes=['hn_tmp'])
    k.op('dve', lambda e: e.tensor_tensor(out=d, in0=x1, in1=sb_, op=ALU.mult), reads=[ukey, 'rope'], writes=['hn_tmp'])
    k.op('dve', lambda e: e.tensor_tensor(out=x1, in0=a, in1=b, op=ALU.subtract), reads=['hn_tmp'], writes=[ukey])
    k.op('dve', lambda e: e.tensor_tensor(out=x2, in0=c, in1=d, op=ALU.add), reads=['hn_tmp'], writes=[ukey])


def build_P():
    cx = Ctx()
    k = cx.k
    x_d = cx.din("x", [NTOK, D])
    cT_d = cx.din("cT", [128, 8])
    adaw_d = cx.din("adaw", [D, 2 * D])
    adab_d = cx.din("adab", [1, 2 * D])
    nrm_d = cx.din("nrm", [1, D])
    win_d = cx.din("w_in", [D, N_IN])
    gains_d = cx.din("gains", [1, 4 * 64])
    rope_d = cx.din("rope", [NTOK, 16])
    u_d = cx.dout("u", [NTOK, N_IN])

    emit_consts(cx)
    emit_work(cx)
    Win = cx.sb("Win", [128, 8, N_IN], BF16)
    nrm_b = cx.sb("nrm_b", [128, D], F32)
    gains_b = cx.sb("gains_b", [128, 256], F32)
    hT = cx.sb("hT", [128, 8, 128], BF16)
    xt = [cx.sb("xt%d" % i, [128, D], F32) for i in range(2)]
    U = [cx.sb("U%d" % i, [128, N_IN], F32) for i in range(2)]
    rope = [cx.sb("rope%d" % i, [128, 16], F32) for i in range(2)]
    tmp = cx.sb("hn_tmp", [128, 512], F32)
    sq = cx.sb("hn_sq", [128, 16], F32)
    pT = cx.ps("pT", [128, 8, 128], BF16)
    pu = [cx.ps("pu%d" % i, [128, 512], F32) for i in range(4)]

    k.dma('sp', nrm_b[:], nrm_d[0:1, :].to_broadcast([128, D]), writes=['nrm_b'])
    k.dma('sp', gains_b[:], gains_d[0:1, :].to_broadcast([128, 256]), writes=['gains'])
    k.op('dve', lambda e: e.tensor_scalar(out=gains_b[:, 0:64], in0=gains_b[:, 0:64], scalar1=0.125, scalar2=None, op0=ALU.mult),
         reads=['gains'], writes=['gains'])
    mod_b = emit_mod(cx, cT_d, adaw_d, adab_d, 2, ['pu0', 'pu1'], pu)
    for kc in range(8):
        k.dma('pool', Win[:, kc, :], win_d[kc * 128:(kc + 1) * 128, :], writes=['Win'])
    k.op('dve', lambda e: e.scalar_tensor_tensor(out=mod_b[:, 1, :], in0=mod_b[:, 1, :], scalar=1.0, in1=nrm_b[:],
                                                 op0=ALU.add, op1=ALU.mult),
         reads=['mod_b', 'nrm_b'], writes=['modd'])
    sh_b, gs_b = mod_b[:, 0, :], mod_b[:, 1, :]
    chunks = [(0, 512), (512, 512), (1024, 512), (1536, 280)]
    for ti in range(NT):
        b = ti % 2
        k.dma('sp', xt[b][:], x_d[ti * 128:(ti + 1) * 128, :], writes=['xt%d' % b])
        k.dma('sp', rope[b][:], rope_d[ti * 128:(ti + 1) * 128, :], writes=['rope'])
        emit_norm_mod_T(cx, xt[b], 'xt%d' % b, gs_b, sh_b, hT, 'hT', 0, pT, 'pT', 'p')
        ukey = 'U%d' % b
        for ci, (c0, cw) in enumerate(chunks):
            for kc in range(8):
                k.op('pe', lambda e, kc=kc: e.matmul(pu[ci][:, 0:cw], hT[:, kc, :], Win[:, kc, c0:c0 + cw],
                                                     start=(kc == 0), stop=(kc == 7)),
                     reads=['hT', 'Win'], writes=['pu%d' % ci], pe_acc=(kc > 0))
            k.op('act', lambda e: e.copy(out=U[b][:, c0:c0 + cw], in_=pu[ci][:, 0:cw]), reads=['pu%d' % ci], writes=[ukey])
        cos_b, sin_b = rope[b][:, 0:8], rope[b][:, 8:16]
        emit_headnorm_rope(cx, U[b], ukey, 256, 8, gains_b[:, 0:64], cos_b, sin_b, tmp, sq)
        emit_headnorm_rope(cx, U[b], ukey, 1024, 2, gains_b[:, 128:192], cos_b, sin_b, tmp, sq)
        emit_headnorm_rope(cx, U[b], ukey, 1280, 2, gains_b[:, 192:256], cos_b, sin_b, tmp, sq)
        k.op('act', lambda e: e.activation(out=U[b][:, 1536:1560], in_=U[b][:, 1536:1560], func=AF.Sigmoid),
             reads=[ukey], writes=[ukey])
        cx.out_toks.append(k.dma('sp', u_d[ti * 128:(ti + 1) * 128, :], U[b][:], reads=[ukey]))
    return cx.done()


NB_NEG = -30000.0


def b2_tables():
    p = np.arange(128)
    caus = (p[:, None] <= p[None, :]).astype(np.float32)
    wlow = (p[:, None] > p[None, :]).astype(np.float32)
    f = np.floor((p - 31) / 16.0)
    mc = np.zeros((128, 17, 128), np.float32)
    for i in range(17):
        mc[:, i, :] = ((p[:, None] - 8 * i) <= f[None, :])
    n = np.arange(512)
    ovl = np.zeros((512, 128), np.float32)
    c_start = n[:, None] * 16
    s_start = np.arange(128)[None, :] * 64
    ovl = np.clip(np.minimum(c_start + 32, s_start + 64) - np.maximum(c_start, s_start), 0, None).astype(np.float32) / 32
    ovl[511, :] = 0
    T = np.zeros((128, 254), np.float32)
    cr = (p >= 64).astype(np.int64)
    m = np.arange(254) - 126
    T[(m[None, :] == cr[:, None]) | (m[None, :] == cr[:, None] - 1)] = 1000.0
    T[m[None, :] > cr[:, None]] = -1e30
    c = np.arange(8192)
    eslot = (np.arange(64)[:, None] == ((c // 64) % 64)[None, :]).astype(np.float32)
    pos = (np.arange(512) * 16 + 31).astype(np.float32)
    inv = np.exp(-math.log(500000.0) * np.arange(8, dtype=np.float32) * (2.0 / 16)).astype(np.float32)
    ang = pos[:, None] * inv[None, :]
    ropec = np.concatenate([np.cos(ang), np.sin(ang)], 1).astype(np.float32)
    return dict(caus=caus, wlow=wlow, mc=mc.reshape(128, 17 * 128), ovl=ovl, T=T, eslot=eslot, ropec=ropec)


def build_B2():
    cx = Ctx()
    k = cx.k
    qT_d = cx.din("qT", [64, 64, 512])
    ksT_d = cx.din("ksT", [64, S])
    kwT_d = cx.din("kwT", [64, S])
    vs_d = cx.din("vs", [S, 64])
    vw_d = cx.din("vw", [S, 64])
    kcT_d = cx.din("kcT", [64, S])
    vcT_d = cx.din("vcT", [64, S])
    gat_d = cx.din("gat", [S, 6])
    w1k_d = cx.din("w1k", [64, 32 * 128])
    w1v_d = cx.din("w1v", [64, 32 * 128])
    pek_d = cx.din("pek", [64, 32])
    pev_d = cx.din("pev", [64, 32])
    w2k_d = cx.din("w2k", [128, 64])
    w2v_d = cx.din("w2v", [128, 64])
    kn0_d = cx.din("kn0", [1, 64])
    caus_d = cx.din("caus", [128, 128])
    wlow_d = cx.din("wlow", [128, 128])
    mc_d = cx.din("mc", [128, 17 * 128])
    ovl_d = cx.din("ovl", [512, 128])
    T_d = cx.din("T", [128, 254])
    eslot_d = cx.din("eslot", [64, S])
    ropec_d = cx.din("ropec", [512, 16])
    o_d = cx.dout("o", [S, 128])

    emit_consts(cx)
    emit_work(cx)
    ks_aug = cx.sb("ks_aug", [128, S], BF16)
    kwT = cx.sb("kwT", [64, S], BF16)
    vs1 = cx.sb("vs1", [128, 64, 65], BF16)
    vw1 = cx.sb("vw1", [128, 64, 65], BF16)
    xk = cx.sb("xk", [64, S + 16], BF16)
    xv = cx.sb("xv", [64, S + 16], BF16)
    w1k = cx.sb("w1k", [64, 32, 128], BF16)
    w1v = cx.sb("w1v", [64, 32, 128], BF16)
    pek = cx.sb("pek", [64, 32], BF16)
    pev = cx.sb("pev", [64, 32], BF16)
    w2k = cx.sb("w2k", [128, 64], BF16)
    w2v = cx.sb("w2v", [128, 64], BF16)
    kcT = cx.sb("kcT", [64, 512], BF16)
    V1c = cx.sb("V1c", [128, 4, 193], BF16)
    gat = cx.sb("gat", [128, 64, 6], F32)
    caus = cx.sb("caus", [128, 128], BF16)
    wlow = cx.sb("wlow", [128, 128], BF16)
    mc = cx.sb("mc", [128, 17, 128], BF16)
    Tt = cx.sb("Tt", [128, 254], F32)
    kn0_b = cx.sb("kn0_b", [128, 64], F32)
    ropec = cx.sb("ropec", [128, 4, 16], F32)
    cbias = cx.sb("cbias", [128, 1], F32)
    xb = cx.sb("xb", [128, 512], F32)
    gt = cx.sb("gt", [128, 512], F32)
    hid = cx.sb("hid", [128, 512], BF16)
    kcf = cx.sb("kcf", [128, 64], F32)
    kcb = cx.sb("kcb", [128, 64], BF16)
    hn_tmp = cx.sb("hn_tmp", [128, 64], F32)
    hn_sq = cx.sb("hn_sq", [128, 16], F32)
    Qa = [[cx.sb("Qa%d_%d" % (i, j), [128, 512], BF16) for j in range(2)] for i in range(2)]
    pt = [cx.sb("pt%d" % i, [128, 512], BF16) for i in range(3)]
    oacc = [cx.sb("oacc%d" % i, [128, 2, 64], F32) for i in range(2)]
    imp = cx.sb("imp", [128, 128], F32)
    imp2 = cx.sb("imp2", [128, 128], F32)
    m8 = cx.sb("m8", [128, 16], F32)
    lr = cx.sb("lr", [128, 8], F32)
    nbsh = [cx.sb("nbsh%d" % i, [128, 128], BF16) for i in range(2)]
    ps_s = [cx.ps("ps_s%d" % i, [128, 512], F32) for i in range(2)]
    po = [cx.ps("po%d" % i, [128, 512], F32) for i in range(4)]
    pm = cx.ps("pm", [128, 1024], BF16)

    for c4 in range(4):
        sl = slice(c4 * 2048, (c4 + 1) * 2048)
        k.dma('pool', ks_aug[0:64, sl], ksT_d[:, sl], writes=['ks_aug'])
        k.dma('pool', ks_aug[64:128, sl], eslot_d[:, sl], writes=['ks_aug'])
        k.dma('pool', kwT[:, sl], kwT_d[:, sl], writes=['kwT'])
        k.dma('pool', xk[:, sl], kcT_d[:, sl], writes=['xk'])
        k.dma('pool', xv[:, sl], vcT_d[:, sl], writes=['xv'])
    k.op('dve', lambda e: e.memset(xk[:, S:S + 16], 0.0), writes=['xk'])
    k.op('dve', lambda e: e.memset(xv[:, S:S + 16], 0.0), writes=['xv'])
    k.op('dve', lambda e: e.memset(vs1[:, :, 64:65], 1.0), writes=['vs1'])
    k.op('dve', lambda e: e.memset(vw1[:, :, 64:65], 1.0), writes=['vw1'])
    k.op('dve', lambda e: e.memset(V1c[:, :, 64:65], 1.0), writes=['V1c'])
    for c4 in range(4):
        k.dma('pool', vs1[:, c4 * 16:(c4 + 1) * 16, 0:64],
              vs_d[c4 * 2048:(c4 + 1) * 2048, :].rearrange("(kt p) d -> p kt d", p=128), writes=['vs1'])
        k.dma('pool', vw1[:, c4 * 16:(c4 + 1) * 16, 0:64],
              vw_d[c4 * 2048:(c4 + 1) * 2048, :].rearrange("(kt p) d -> p kt d", p=128), writes=['vw1'])
        k.dma('sp', gat[:, c4 * 16:(c4 + 1) * 16, :],
              gat_d[c4 * 2048:(c4 + 1) * 2048, :].rearrange("(jb p) c -> p jb c", p=128), writes=['gat'])
    k.dma('pool', w1k[:].rearrange("p i h -> p (i h)"), w1k_d[:, :], writes=['w1k'])
    k.dma('pool', w1v[:].rearrange("p i h -> p (i h)"), w1v_d[:, :], writes=['w1v'])
    k.dma('pool', pek[:], pek_d[:, :], writes=['pek'])
    k.dma('pool', pev[:], pev_d[:, :], writes=['pev'])
    k.dma('pool', w2k[:], w2k_d[:, :], writes=['w2k'])
    k.dma('pool', w2v[:], w2v_d[:, :], writes=['w2v'])
    k.dma('pool', caus[:], caus_d[:, :], writes=['caus'])
    k.dma('pool', wlow[:], wlow_d[:, :], writes=['wlow'])
    k.dma('pool', mc[:].rearrange("p i q -> p (i q)"), mc_d[:, :], writes=['mc'])
    k.dma('pool', V1c[:, :, 65:193], ovl_d.rearrange("(nt p) j -> p nt j", p=128), writes=['V1c'])
    k.dma('sp', Tt[:], T_d[:, :], writes=['Tt'])
    k.dma('sp', kn0_b[:], kn0_d[0:1, :].to_broadcast([128, 64]), writes=['gains'])
    k.dma('sp', ropec[:], ropec_d.rearrange("(nt p) c -> p nt c", p=128), writes=['rope'])
    for i in range(2):
        k.op('dve', lambda e: e.memset(nbsh[i][:], 0.0), writes=['nbsh%d' % i])

    for (xc, xkey, w1, w1key, pe, pekey, w2, w2key, is_k) in [(xk, 'xk', w1k, 'w1k', pek, 'pek', w2k, 'w2k', True),
                                                           (xv, 'xv', w1v, 'w1v', pev, 'pev', w2v, 'w2v', False)]:
        xcv = xc[:].rearrange("p (n s) -> p n s", s=16)
        for i in range(32):
            k.op('pe', lambda e, i=i: e.matmul(po[0][:, 0:1], w1[:, i, :], pe[:, i:i + 1], start=(i == 0), stop=(i == 31)),
                 reads=[w1key, pekey], writes=['po0'], pe_acc=(i > 0))
        k.op('act', lambda e: e.copy(out=cbias[:], in_=po[0][:, 0:1]), reads=['po0'], writes=['cbias'])
        for i in range(32):
            o_, s_ = i // 16, i % 16
            k.op('pe', lambda e, i=i: e.matmul(ps_s[0][:, :], w1[:, i, :], xcv[:, o_:o_ + 512, s_], start=(i == 0), stop=(i == 31)),
                 reads=[w1key, xkey], writes=['ps_s0'], pe_acc=(i > 0))
        k.op('act', lambda e: e.activation(out=xb[:], in_=ps_s[0][:, :], func=AF.Identity, bias=cbias[:, 0:1]),
             reads=['ps_s0', 'cbias'], writes=['xb'])
        k.op('dve', lambda e: e.tensor_tensor(out=gt[:], in0=xb[:], in1=xb[:], op=ALU.mult), reads=['xb'], writes=['gt'])
        k.op('dve', lambda e: e.tensor_scalar(out=gt[:], in0=gt[:], scalar1=0.044715, scalar2=1.0, op0=ALU.mult, op1=ALU.add),
             reads=['gt'], writes=['gt'])
        k.op('dve', lambda e: e.tensor_tensor(out=gt[:], in0=gt[:], in1=xb[:], op=ALU.mult), reads=['gt', 'xb'], writes=['gt'])
        k.op('act', lambda e: e.activation(out=gt[:], in_=gt[:], func=AF.Sigmoid, scale=1.5957691216057308),
             reads=['gt'], writes=['gt'])
        k.op('dve', lambda e: e.tensor_tensor(out=hid[:], in0=gt[:], in1=xb[:], op=ALU.mult), reads=['gt', 'xb'], writes=['hid'])
        for nt in range(4):
            k.op('pe', lambda e: e.matmul(po[1][:, 0:64], hid[:, nt * 128:(nt + 1) * 128], w2[:, :], start=True, stop=True),
                 reads=['hid', w2key], writes=['po1'])
            if is_k:
                k.op('act', lambda e: e.copy(out=kcf[:], in_=po[1][:, 0:64]), reads=['po1'], writes=['kcf'])
                emit_headnorm_rope(cx, kcf, 'kcf', 0, 1, kn0_b[:, :], ropec[:, nt, 0:8], ropec[:, nt, 8:16], hn_tmp, hn_sq)
                k.op('dve', lambda e: e.tensor_copy(kcb[:], kcf[:]), reads=['kcf'], writes=['kcb'])
                k.op('pe', lambda e: e.transpose(pm[0:64, 0:128], kcb[:, :], cx.ident[:]), reads=['kcb', 'ident'], writes=['pm'])
                k.op('act', lambda e: e.copy(out=kcT[:, nt * 128:(nt + 1) * 128], in_=pm[0:64, 0:128]), reads=['pm'], writes=['kcT'])
            else:
                k.op('act', lambda e: e.copy(out=V1c[:, nt, 0:64], in_=po[1][:, 0:64]), reads=['po1'], writes=['V1c'])

    step = [0]

    def attn_tile(lhsT, lkeys, rhs, rkeys, ncol, nh, mask, vtile, vkey, vw, first, last):
        i = step[0] % 2
        j = step[0] % 3
        step[0] += 1
        k.op('pe', lambda e: e.matmul(ps_s[i][:, 0:ncol], lhsT, rhs, start=True, stop=True),
             reads=lkeys + rkeys, writes=['ps_s%d' % i])
        k.op('act', lambda e: e.activation(out=pt[j][:, 0:ncol], in_=ps_s[i][:, 0:ncol], func=AF.Exp),
             reads=['ps_s%d' % i], writes=['pt%d' % j])
        if mask is not None:
            mk, mkey = mask
            k.op('dve', lambda e: e.tensor_tensor(out=pt[j][:, 0:ncol].rearrange("p (r q) -> p r q", q=128),
                                                  in0=pt[j][:, 0:ncol].rearrange("p (r q) -> p r q", q=128),
                                                  in1=mk.unsqueeze(1).to_broadcast([128, nh, 128]), op=ALU.mult),
                 reads=['pt%d' % j, mkey], writes=['pt%d' % j])
        for r in range(nh):
            k.op('pe', lambda e, r=r: e.matmul(po[r][:, 0:vw], pt[j][:, r * 128:(r + 1) * 128], vtile, start=first, stop=last),
                 reads=['pt%d' % j, vkey], writes=['po%d' % r], pe_acc=(not first))

    def branch_epilogue(ob, obkey, jb, br, first):
        for r in range(2):
            k.op('dve', lambda e, r=r: e.tensor_scalar(out=lr[:, r:r + 1], in0=po[r][:, 64:65], scalar1=1e-30, scalar2=None, op0=ALU.max),
                 reads=['po%d' % r], writes=['lr'])
        k.op('dve', lambda e: e.reciprocal(out=lr[:, 0:2], in_=lr[:, 0:2]), reads=['lr'], writes=['lr'])
        gv = gat[:, jb, :].rearrange("p (r b) -> p r b", b=3)[:, :, br]
        k.op('dve', lambda e: e.tensor_tensor(out=lr[:, 4:6], in0=lr[:, 0:2], in1=gv, op=ALU.mult), reads=['lr', 'gat'], writes=['lr'])
        for r in range(2):
            if first:
                k.op('dve', lambda e, r=r: e.tensor_scalar(out=ob[:, r, :], in0=po[r][:, 0:64], scalar1=lr[:, 4 + r:5 + r], scalar2=None, op0=ALU.mult),
                     reads=['po%d' % r, 'lr'], writes=[obkey])
            else:
                k.op('dve', lambda e, r=r: e.scalar_tensor_tensor(out=ob[:, r, :], in0=po[r][:, 0:64], scalar=lr[:, 4 + r:5 + r], in1=ob[:, r, :],
                                                                  op0=ALU.mult, op1=ALU.add),
                     reads=['po%d' % r, 'lr', obkey], writes=[obkey])

    for jb in range(64):
        b2 = jb % 2
        Qlo, Qhi = Qa[b2]
        qlk, qhk = 'Qa%d_0' % b2, 'Qa%d_1' % b2
        ob, obkey = oacc[b2], 'oacc%d' % b2
        k.dma('pool', Qlo[0:64, :], qT_d[jb, :, :], writes=[qlk + 'q'])
        k.dma('pool', Qhi[0:64, 0:256], qT_d[jb, :, 0:256], writes=[qhk + 'q'])
        tiles = [nt for nt in range(4) if 128 * nt <= 8 * jb + 6]
        for ii, nt in enumerate(tiles):
            masked = not (128 * nt + 127 <= 8 * jb - 2)
            mask = (mc[:, (8 * jb - 128 * nt) // 8, :], 'mc') if masked else None
            attn_tile(kcT[:, nt * 128:(nt + 1) * 128], ['kcT'], Qlo[0:64, :], [qlk + 'q'], 512, 4, mask,
                      V1c[:, nt, :], 'V1c', 193, ii == 0, ii == len(tiles) - 1)
        for r in range(4):
            k.op('dve', lambda e, r=r: e.tensor_scalar(out=lr[:, r:r + 1], in0=po[r][:, 64:65], scalar1=1e-30, scalar2=None, op0=ALU.max),
                 reads=['po%d' % r], writes=['lr'])
        k.op('dve', lambda e: e.reciprocal(out=lr[:, 0:4], in_=lr[:, 0:4]), reads=['lr'], writes=['lr'])
        k.op('dve', lambda e: e.scalar_tensor_tensor(out=imp[:], in0=po[0][:, 65:193], scalar=lr[:, 0:1], in1=Tt[:, 126 - 2 * jb:254 - 2 * jb],
                                                     op0=ALU.mult, op1=ALU.add), reads=['po0', 'lr', 'Tt'], writes=['imp'])
        for r in range(1, 4):
            k.op('dve', lambda e, r=r: e.scalar_tensor_tensor(out=imp[:], in0=po[r][:, 65:193], scalar=lr[:, r:r + 1], in1=imp[:],
                                                              op0=ALU.mult, op1=ALU.add), reads=['po%d' % r, 'lr', 'imp'], writes=['imp'])
        k.op('dve', lambda e: e.tensor_scalar(out=imp[:, 0:1], in0=imp[:, 0:1], scalar1=1000.0, scalar2=None, op0=ALU.add),
             reads=['imp'], writes=['imp'])
        gv = gat[:, jb, :].rearrange("p (r b) -> p r b", b=3)[:, :, 0]
        k.op('dve', lambda e: e.tensor_tensor(out=lr[:, 4:6], in0=lr[:, 0:2], in1=gv, op=ALU.mult), reads=['lr', 'gat'], writes=['lr'])
        for r in range(2):
            k.op('dve', lambda e, r=r: e.tensor_scalar(out=ob[:, r, :], in0=po[r][:, 0:64], scalar1=lr[:, 4 + r:5 + r], scalar2=None, op0=ALU.mult),
                 reads=['po%d' % r, 'lr'], writes=[obkey])
        k.op('dve', lambda e: e.max(out=m8[:, 0:8], in_=imp[:]), reads=['imp'], writes=['m8'])
        k.op('dve', lambda e: e.match_replace(out=imp2[:], in_to_replace=m8[:, 0:8], in_values=imp[:], imm_value=-3.0e38),
             reads=['imp', 'm8'], writes=['imp2'])
        k.op('dve', lambda e: e.max(out=m8[:, 8:16], in_=imp2[:]), reads=['imp2'], writes=['m8'])
        k.op('dve', lambda e: e.tensor_scalar(out=imp2[:], in0=imp[:], scalar1=m8[:, 15:16], scalar2=1.0, op0=ALU.is_ge, op1=ALU.subtract),
             reads=['imp', 'm8'], writes=['imp2'])
        for i in range(2):
            k.op('dve', lambda e, i=i: e.tensor_scalar(out=nbsh[i][:, 64:128], in0=imp2[:, i * 64:(i + 1) * 64], scalar1=-NB_NEG, scalar2=None, op0=ALU.mult),
                 reads=['imp2'], writes=['nbsh%d' % i])
            k.op('pe', lambda e, i=i: e.transpose(pm[:, i * 128:(i + 1) * 128], nbsh[i][:, :], cx.ident[:]),
                 reads=['nbsh%d' % i, 'ident'], writes=['pm'])
        k.op('act', lambda e: e.copy(out=Qlo[64:128, 0:256].rearrange("p (r q) -> p r q", q=128),
                                     in_=pm[64:128, 0:128].unsqueeze(1).to_broadcast([64, 2, 128])),
             reads=['pm'], writes=[qlk + 'b'])
        k.op('act', lambda e: e.copy(out=Qhi[64:128, 0:256].rearrange("p (r q) -> p r q", q=128),
                                     in_=pm[64:128, 128:256].unsqueeze(1).to_broadcast([64, 2, 128])),
             reads=['pm'], writes=[qhk + 'b'])
        kts = list(range(max(0, jb - 4), jb + 1))
        for ii, kt in enumerate(kts):
            mask = None
            if kt == jb:
                mask = (caus[:, :], 'caus')
            elif kt == jb - 4:
                mask = (wlow[:, :], 'wlow')
            attn_tile(kwT[:, kt * 128:(kt + 1) * 128], ['kwT'], Qlo[0:64, 0:256], [qlk + 'q'], 256, 2, mask,
                      vw1[:, kt, :], 'vw1', 65, ii == 0, ii == len(kts) - 1)
        branch_epilogue(ob, obkey, jb, 2, False)
        for kt in range(jb + 1):
            Q, qk = (Qlo, qlk) if kt < 32 else (Qhi, qhk)
            mask = (caus[:, :], 'caus') if kt == jb else None
            attn_tile(ks_aug[:, kt * 128:(kt + 1) * 128], ['ks_aug'], Q[:, 0:256], [qk + 'q', qk + 'b'], 256, 2, mask,
                      vs1[:, kt, :], 'vs1', 65, kt == 0, kt == jb)
        branch_epilogue(ob, obkey, jb, 1, False)
        cx.out_toks.append(k.dma('sp', o_d[jb * 128:(jb + 1) * 128, :], ob[:].rearrange("p r d -> p (r d)"), reads=[obkey]))
    return cx.done()


def prep_B2(U, g, hh, P, tabs):
    f = np.float32
    heads = [2 * hh, 2 * hh + 1, 2 * (1 - hh), 2 * (1 - hh) + 1]
    q = U[:, 256:768].reshape(64, 128, 2, 4, 64)[:, :, g][:, :, heads]
    qT = np.ascontiguousarray(q.transpose(0, 3, 2, 1)).reshape(64, 64, 512)
    kv = U[:, 768:1536].reshape(S, 6, 2, 64)[:, :, g]
    gat = U[:, 1536:1560].reshape(S, 2, 4, 3)[:, g][:, heads[:2]].reshape(S, 6)
    w1k = P['cmp_k_w1'].reshape(32, 64, 128).transpose(1, 0, 2).reshape(64, 32 * 128)
    w1v = P['cmp_v_w1'].reshape(32, 64, 128).transpose(1, 0, 2).reshape(64, 32 * 128)
    m = {
        "qT": qT.astype(f),
        "kcT": np.ascontiguousarray(kv[:, 0].T), "vcT": np.ascontiguousarray(kv[:, 1].T),
        "ksT": np.ascontiguousarray(kv[:, 2].T), "vs": np.ascontiguousarray(kv[:, 3]),
        "kwT": np.ascontiguousarray(kv[:, 4].T), "vw": np.ascontiguousarray(kv[:, 5]),
        "gat": np.ascontiguousarray(gat),
        "w1k": np.ascontiguousarray(w1k), "w1v": np.ascontiguousarray(w1v),
        "pek": np.ascontiguousarray(P['cmp_pe'][0].T), "pev": np.ascontiguousarray(P['cmp_pe'][1].T),
        "w2k": np.ascontiguousarray(P['cmp_k_w2']), "w2v": np.ascontiguousarray(P['cmp_v_w2']),
        "kn0": np.ascontiguousarray(P['k_norm'][0][None, :]),
    }
    m.update(tabs)
    return {k_: np.ascontiguousarray(v, dtype=f) for k_, v in m.items()}


LC = 512
TWO_PI_LO = 6.283185


def emit_sin_turns(cx, out, r, rkey, okey, W, wkey):
    k = cx.k
    ri, rf, rg = W['i'], W['f'], W['g']
    k.op('dve', lambda e: e.tensor_copy(ri, r), reads=[rkey], writes=[wkey])
    k.op('dve', lambda e: e.tensor_copy(rf, ri), reads=[wkey], writes=[wkey])
    k.op('dve', lambda e: e.tensor_tensor(out=rf, in0=r, in1=rf, op=ALU.subtract), reads=[rkey, wkey], writes=[wkey])
    k.op('dve', lambda e: e.tensor_scalar(out=rg, in0=rf, scalar1=0.5, scalar2=None, op0=ALU.is_gt), reads=[wkey], writes=[wkey])
    k.op('dve', lambda e: e.tensor_tensor(out=rf, in0=rf, in1=rg, op=ALU.subtract), reads=[wkey], writes=[wkey])
    k.op('dve', lambda e: e.tensor_scalar(out=rg, in0=rf, scalar1=-0.5, scalar2=None, op0=ALU.is_lt), reads=[wkey], writes=[wkey])
    k.op('dve', lambda e: e.tensor_tensor(out=rf, in0=rf, in1=rg, op=ALU.add), reads=[wkey], writes=[wkey])
    k.op('act', lambda e: e.activation(out=out, in_=rf, func=AF.Sin, scale=TWO_PI_LO), reads=[wkey], writes=[okey])


def build_B1():
    cx = Ctx()
    k = cx.k
    uT_d = cx.din("uT", [64, S])
    lre_d = cx.din("lre", [128, 2])
    lim_d = cx.din("lim", [128, 2])
    ldt_d = cx.din("ldt", [128, 2])
    bre_d = cx.din("bre", [128, 32])
    bim_d = cx.din("bim", [128, 32])
    cre_d = cx.din("cre", [128, 32])
    cim_d = cx.din("cim", [128, 32])
    dsk_d = cx.din("dsk", [32, 2])
    y_d = cx.dout("yT", [64, S])

    emit_consts(cx)
    NJ = LC + 1
    lre = cx.sb("lre", [128, 2]); lim = cx.sb("lim", [128, 2]); ldt = cx.sb("ldt", [128, 2])
    bre = cx.sb("bre", [128, 2, 16]); bim = cx.sb("bim", [128, 2, 16])
    cre = cx.sb("cre", [128, 2, 16]); cim = cx.sb("cim", [128, 2, 16])
    dsk = cx.sb("dsk", [32, 2])
    sc = cx.sb("sc", [128, 32])
    cosT = [cx.sb("cosT%d" % t, [128, NJ]) for t in range(2)]
    sinT = [cx.sb("sinT%d" % t, [128, NJ]) for t in range(2)]
    iota_i = cx.sb("iota_i", [128, NJ], I32)
    rr = cx.sb("rr", [128, NJ])
    Wt = {'i': cx.sb("w_i", [128, NJ], I32)[:], 'f': cx.sb("w_f", [128, NJ])[:], 'g': cx.sb("w_g", [128, NJ])[:]}
    Bbd = [[cx.sb("Bbd%d_%d" % (t, c), [128, 32], BF16) for c in range(2)] for t in range(2)]
    BbT = [[cx.sb("BbT%d_%d" % (t, c), [32, 128], BF16) for c in range(2)] for t in range(2)]
    Cbd = [[cx.sb("Cbd%d_%d" % (t, c), [128, 32], BF16) for c in range(2)] for t in range(2)]
    tmpB = cx.sb("tmpB", [128, 4, 16])
    uf = [cx.sb("uf%d" % i, [32, LC]) for i in range(2)]
    ub = [cx.sb("ub%d" % i, [32, LC], BF16) for i in range(2)]
    t1 = cx.sb("t1", [128, LC]); t2 = cx.sb("t2", [128, LC])
    wre = cx.sb("wre", [128, LC]); wim = cx.sb("wim", [128, LC])
    vre = cx.sb("vre", [128, LC]); vim = cx.sb("vim", [128, LC])
    zre = cx.sb("zre", [128, LC], BF16); zim = cx.sb("zim", [128, LC], BF16)
    init = [cx.sb("init%d" % t, [128, 2]) for t in range(2)]
    yo = [cx.sb("yo%d" % i, [32, LC]) for i in range(2)]
    pA = cx.ps("pA", [128, 512]); pB = cx.ps("pB", [128, 512]); pC = cx.ps("pC", [128, 512])
    pm = cx.ps("pm", [128, 1024], BF16)

    for a, d_, nm in [(lre, lre_d, 'lre'), (lim, lim_d, 'lim'), (ldt, ldt_d, 'ldt'), (dsk, dsk_d, 'dsk')]:
        k.dma('sp', a[:], d_[:, :], writes=[nm])
    for a, d_, nm in [(bre, bre_d, 'bre'), (bim, bim_d, 'bim'), (cre, cre_d, 'cre'), (cim, cim_d, 'cim')]:
        k.dma('sp', a[:].rearrange("p t h -> p (t h)"), d_[:, :], writes=[nm])
    S_ = lambda a, b: sc[:, a:b]
    k.op('act', lambda e: e.activation(out=S_(0, 2), in_=ldt[:], func=AF.Exp), reads=['ldt'], writes=['sc'])
    k.op('dve', lambda e: e.tensor_tensor(out=S_(2, 4), in0=lre[:], in1=S_(0, 2), op=ALU.mult), reads=['lre', 'sc'], writes=['sc'])
    k.op('dve', lambda e: e.tensor_tensor(out=S_(4, 6), in0=lim[:], in1=S_(0, 2), op=ALU.mult), reads=['lim', 'sc'], writes=['sc'])
    k.op('act', lambda e: e.activation(out=S_(6, 8), in_=S_(2, 4), func=AF.Exp), reads=['sc'], writes=['sc'])
    k.op('dve', lambda e: e.tensor_scalar(out=S_(8, 10), in0=S_(4, 6), scalar1=1.0 / (2 * math.pi), scalar2=None, op0=ALU.mult),
         reads=['sc'], writes=['sc'])
    k.op('pool', lambda e: e.iota(iota_i[:], pattern=[[1, NJ]], base=0, channel_multiplier=0), writes=['iota_i'])
    for t in range(2):
        k.op('dve', lambda e: e.tensor_copy(rr[:], iota_i[:]), reads=['iota_i'], writes=['rr'])
        k.op('dve', lambda e: e.tensor_scalar(out=rr[:], in0=rr[:], scalar1=sc[:, 8 + t:9 + t], scalar2=None, op0=ALU.mult),
             reads=['rr', 'sc'], writes=['rr'])
        emit_sin_turns(cx, sinT[t][:], rr[:], 'rr', 'sinT%d' % t, Wt, 'wt')
        k.op('dve', lambda e: e.tensor_scalar(out=rr[:], in0=rr[:], scalar1=0.25, scalar2=None, op0=ALU.add), reads=['rr'], writes=['rr'])
        emit_sin_turns(cx, cosT[t][:], rr[:], 'rr', 'cosT%d' % t, Wt, 'wt')
    for t in range(2):
        c1, s1 = cosT[t][:, 1:2], sinT[t][:, 1:2]
        rho = sc[:, 6 + t:7 + t]
        nr, ni, den = sc[:, 10 + t:11 + t], sc[:, 12 + t:13 + t], sc[:, 14 + t:15 + t]
        cr_, ci_ = sc[:, 16 + t:17 + t], sc[:, 18 + t:19 + t]
        ta, tb = sc[:, 20:21], sc[:, 21:22]
        lr_, li_ = lre[:, t:t + 1], lim[:, t:t + 1]
        ck, sk = 'cosT%d' % t, 'sinT%d' % t
        k.op('dve', lambda e: e.tensor_tensor(out=nr, in0=rho, in1=c1, op=ALU.mult), reads=['sc', ck], writes=['sc'])
        k.op('dve', lambda e: e.tensor_scalar(out=nr, in0=nr, scalar1=-1.0, scalar2=None, op0=ALU.add), reads=['sc'], writes=['sc'])
        k.op('dve', lambda e: e.tensor_tensor(out=ni, in0=rho, in1=s1, op=ALU.mult), reads=['sc', sk], writes=['sc'])
        k.op('dve', lambda e: e.tensor_tensor(out=ta, in0=lr_, in1=lr_, op=ALU.mult), reads=['lre'], writes=['sc'])
        k.op('dve', lambda e: e.scalar_tensor_tensor(out=den, in0=li_, scalar=li_, in1=ta, op0=ALU.mult, op1=ALU.add),
             reads=['lim', 'sc'], writes=['sc'])
        k.op('dve', lambda e: e.reciprocal(out=den, in_=den), reads=['sc'], writes=['sc'])
        k.op('dve', lambda e: e.tensor_tensor(out=ta, in0=nr, in1=lr_, op=ALU.mult), reads=['sc', 'lre'], writes=['sc'])
        k.op('dve', lambda e: e.scalar_tensor_tensor(out=ta, in0=ni, scalar=li_, in1=ta, op0=ALU.mult, op1=ALU.add),
             reads=['sc', 'lim'], writes=['sc'])
        k.op('dve', lambda e: e.tensor_tensor(out=cr_, in0=ta, in1=den, op=ALU.mult), reads=['sc'], writes=['sc'])
        k.op('dve', lambda e: e.tensor_tensor(out=ta, in0=ni, in1=lr_, op=ALU.mult), reads=['sc', 'lre'], writes=['sc'])
        k.op('dve', lambda e: e.tensor_tensor(out=tb, in0=nr, in1=li_, op=ALU.mult), reads=['sc', 'lim'], writes=['sc'])
        k.op('dve', lambda e: e.tensor_tensor(out=ta, in0=ta, in1=tb, op=ALU.subtract), reads=['sc'], writes=['sc'])
        k.op('dve', lambda e: e.tensor_tensor(out=ci_, in0=ta, in1=den, op=ALU.mult), reads=['sc'], writes=['sc'])
        q0, q1, q2, q3 = tmpB[:, 0, :], tmpB[:, 1, :], tmpB[:, 2, :], tmpB[:, 3, :]
        k.op('dve', lambda e: e.tensor_scalar(out=q0, in0=bre[:, t, :], scalar1=cr_, scalar2=None, op0=ALU.mult), reads=['bre', 'sc'], writes=['tmpB'])
        k.op('dve', lambda e: e.tensor_scalar(out=q1, in0=bim[:, t, :], scalar1=ci_, scalar2=None, op0=ALU.mult), reads=['bim', 'sc'], writes=['tmpB'])
        k.op('dve', lambda e: e.tensor_scalar(out=q2, in0=bre[:, t, :], scalar1=ci_, scalar2=None, op0=ALU.mult), reads=['bre', 'sc'], writes=['tmpB'])
        k.op('dve', lambda e: e.tensor_scalar(out=q3, in0=bim[:, t, :], scalar1=cr_, scalar2=None, op0=ALU.mult), reads=['bim', 'sc'], writes=['tmpB'])
        for c in range(2):
            key = 'Bbd%d_%d' % (t, c)
            k.op('dve', lambda e, c=c: e.memset(Bbd[t][c][:], 0.0), writes=[key])
            kc_ = 'Cbd%d_%d' % (t, c)
            k.op('dve', lambda e, c=c: e.memset(Cbd[t][c][:], 0.0), writes=[kc_])
        for gi in range(2):
            rs = slice(gi * 64, (gi + 1) * 64)
            cs = slice(gi * 16, (gi + 1) * 16)
            k.op('dve', lambda e: e.tensor_tensor(out=Bbd[t][0][rs, cs], in0=tmpB[rs, 0, :], in1=tmpB[rs, 1, :], op=ALU.subtract),
                 reads=['tmpB', 'Bbd%d_0' % t], writes=['Bbd%d_0' % t])
            k.op('dve', lambda e: e.tensor_tensor(out=Bbd[t][1][rs, cs], in0=tmpB[rs, 2, :], in1=tmpB[rs, 3, :], op=ALU.add),
                 reads=['tmpB', 'Bbd%d_1' % t], writes=['Bbd%d_1' % t])
            k.op('dve', lambda e: e.tensor_copy(Cbd[t][0][rs, cs], cre[rs, t, :]), reads=['cre', 'Cbd%d_0' % t], writes=['Cbd%d_0' % t])
            k.op('dve', lambda e: e.tensor_scalar(out=Cbd[t][1][rs, cs], in0=cim[rs, t, :], scalar1=-1.0, scalar2=None, op0=ALU.mult),
                 reads=['cim', 'Cbd%d_1' % t], writes=['Cbd%d_1' % t])
        for c in range(2):
            k.op('pe', lambda e, c=c: e.transpose(pm[0:32, c * 128:(c + 1) * 128], Bbd[t][c][:, :], cx.ident[:]),
                 reads=['Bbd%d_%d' % (t, c), 'ident'], writes=['pm'])
            k.op('act', lambda e, c=c: e.copy(out=BbT[t][c][:], in_=pm[0:32, c * 128:(c + 1) * 128]), reads=['pm'], writes=['BbT%d_%d' % (t, c)])
        k.op('dve', lambda e: e.memset(init[t][:], 0.0), writes=['init%d' % t])

    nchunk = S // LC
    it = 0
    for ch in range(nchunk):
        for t in range(2):
            b = it % 2
            it += 1
            ck, sk = 'cosT%d' % t, 'sinT%d' % t
            cs_, sn_ = cosT[t][:, 0:LC], sinT[t][:, 0:LC]
            k.dma('sp', uf[b][:], uT_d[t * 32:(t + 1) * 32, ch * LC:(ch + 1) * LC], writes=['uf%d' % b])
            k.op('pool', lambda e: e.tensor_copy(ub[b][:], uf[b][:]), reads=['uf%d' % b], writes=['ub%d' % b])
            k.op('pe', lambda e: e.matmul(pA[:, :], BbT[t][0][:, :], ub[b][:, :], start=True, stop=True),
                 reads=['BbT%d_0' % t, 'ub%d' % b], writes=['pA'])
            k.op('pe', lambda e: e.matmul(pB[:, :], BbT[t][1][:, :], ub[b][:, :], start=True, stop=True),
                 reads=['BbT%d_1' % t, 'ub%d' % b], writes=['pB'])
            k.op('dve', lambda e: e.tensor_tensor(out=t1[:], in0=cs_, in1=pA[:, :], op=ALU.mult), reads=[ck, 'pA'], writes=['t1'])
            k.op('dve', lambda e: e.tensor_tensor(out=t2[:], in0=sn_, in1=pB[:, :], op=ALU.mult), reads=[sk, 'pB'], writes=['t2'])
            k.op('pool', lambda e: e.tensor_tensor(out=wre[:], in0=t1[:], in1=t2[:], op=ALU.add), reads=['t1', 't2'], writes=['wre'])
            k.op('dve', lambda e: e.tensor_tensor(out=t1[:], in0=cs_, in1=pB[:, :], op=ALU.mult), reads=[ck, 'pB'], writes=['t1'])
            k.op('dve', lambda e: e.tensor_tensor(out=t2[:], in0=sn_, in1=pA[:, :], op=ALU.mult), reads=[sk, 'pA'], writes=['t2'])
            k.op('pool', lambda e: e.tensor_tensor(out=wim[:], in0=t1[:], in1=t2[:], op=ALU.subtract), reads=['t1', 't2'], writes=['wim'])
            rho_b = sc[:, 6 + t:7 + t].to_broadcast([128, LC])
            k.op('dve', lambda e: e.tensor_tensor_scan(out=vre[:], data0=rho_b, data1=wre[:], initial=init[t][:, 0:1], op0=ALU.mult, op1=ALU.add),
                 reads=['sc', 'wre', 'init%d' % t], writes=['vre'])
            k.op('dve', lambda e: e.tensor_tensor_scan(out=vim[:], data0=rho_b, data1=wim[:], initial=init[t][:, 1:2], op0=ALU.mult, op1=ALU.add),
                 reads=['sc', 'wim', 'init%d' % t], writes=['vim'])
            cL, sL = cosT[t][:, LC:LC + 1], sinT[t][:, LC:LC + 1]
            ta, tb = sc[:, 24:25], sc[:, 25:26]
            k.op('dve', lambda e: e.tensor_tensor(out=ta, in0=vim[:, LC - 1:LC], in1=sL, op=ALU.mult), reads=['vim', sk], writes=['sc2'])
            k.op('dve', lambda e: e.tensor_tensor(out=tb, in0=vim[:, LC - 1:LC], in1=cL, op=ALU.mult), reads=['vim', ck], writes=['sc2'])
            k.op('dve', lambda e: e.scalar_tensor_tensor(out=init[t][:, 0:1], in0=vre[:, LC - 1:LC], scalar=cL, in1=ta, op0=ALU.mult, op1=ALU.subtract),
                 reads=['vre', ck, 'sc2'], writes=['init%d' % t])
            k.op('dve', lambda e: e.scalar_tensor_tensor(out=init[t][:, 1:2], in0=vre[:, LC - 1:LC], scalar=sL, in1=tb, op0=ALU.mult, op1=ALU.add),
                 reads=['vre', sk, 'sc2'], writes=['init%d' % t])
            k.op('pool', lambda e: e.tensor_tensor(out=t1[:], in0=cs_, in1=vre[:], op=ALU.mult), reads=[ck, 'vre'], writes=['t1'])
            k.op('pool', lambda e: e.tensor_tensor(out=t2[:], in0=sn_, in1=vim[:], op=ALU.mult), reads=[sk, 'vim'], writes=['t2'])
            k.op('dve', lambda e: e.tensor_tensor(out=zre[:], in0=t1[:], in1=t2[:], op=ALU.subtract), reads=['t1', 't2'], writes=['zre'])
            k.op('pool', lambda e: e.tensor_tensor(out=t1[:], in0=sn_, in1=vre[:], op=ALU.mult), reads=[sk, 'vre'], writes=['t1'])
            k.op('pool', lambda e: e.tensor_tensor(out=t2[:], in0=cs_, in1=vim[:], op=ALU.mult), reads=[ck, 'vim'], writes=['t2'])
            k.op('dve', lambda e: e.tensor_tensor(out=zim[:], in0=t1[:], in1=t2[:], op=ALU.add), reads=['t1', 't2'], writes=['zim'])
            k.op('pe', lambda e: e.matmul(pC[0:32, :], Cbd[t][0][:, :], zre[:, :], start=True, stop=False),
                 reads=['Cbd%d_0' % t, 'zre'], writes=['pC'])
            k.op('pe', lambda e: e.matmul(pC[0:32, :], Cbd[t][1][:, :], zim[:, :], start=False, stop=True),
                 reads=['Cbd%d_1' % t, 'zim'], writes=['pC'], pe_acc=True)
            k.op('dve', lambda e: e.scalar_tensor_tensor(out=yo[b][:], in0=uf[b][:], scalar=dsk[:, t:t + 1], in1=pC[0:32, :], op0=ALU.mult, op1=ALU.add),
                 reads=['uf%d' % b, 'dsk', 'pC'], writes=['yo%d' % b])
            cx.out_toks.append(k.dma('sp', y_d[t * 32:(t + 1) * 32, ch * LC:(ch + 1) * LC], yo[b][:], reads=['yo%d' % b]))
    return cx.done()


def prep_B1(u_s5, kq, P):
    f = np.float32
    gs = slice(4 * kq, 4 * kq + 4)
    def pg(a):
        return np.ascontiguousarray(a[gs].reshape(2, 128).T)
    def pgh(a):
        return np.ascontiguousarray(a[gs].reshape(2, 128, 16).transpose(1, 0, 2).reshape(128, 32))
    m = {
        "uT": np.ascontiguousarray(u_s5[:, 64 * kq:64 * kq + 64].T),
        "lre": pg(P['s5_lam_re']), "lim": pg(P['s5_lam_im']),
        "ldt": pg(np.repeat(P['s5_log_dt'][:, None], 64, axis=1)),
        "bre": pgh(P['s5_b_re']), "bim": pgh(P['s5_b_im']),
        "cre": pgh(P['s5_c_re'].transpose(0, 2, 1)), "cim": pgh(P['s5_c_im'].transpose(0, 2, 1)),
        "dsk": np.ascontiguousarray(P['s5_d'][gs].reshape(2, 32).T),
    }
    return {k_: np.ascontiguousarray(v, dtype=f) for k_, v in m.items()}


HALO = 16


def emit_rms_rows(cx, out, okey, x, xkey, W_, gain, gkey, junk):
    k = cx.k
    ss, rstd = cx.work['ss'], cx.work['rstd']
    k.op('act', lambda e: e.activation(out=junk, in_=x, func=AF.Square, accum_out=ss[:, 0:1]), reads=[xkey], writes=['rjunk', 'ss'])
    k.op('act', lambda e: e.activation(out=rstd[:, 0:1], in_=ss[:, 0:1], func=AF.Sqrt, scale=1.0 / W_, bias=cx.work['eps'][:, 0:1]),
         reads=['ss', 'eps'], writes=['rstd'])
    k.op('dve', lambda e: e.reciprocal(out=rstd[:, 0:1], in_=rstd[:, 0:1]), reads=['rstd'], writes=['rstd'])
    k.op('dve', lambda e: e.scalar_tensor_tensor(out=out, in0=x, scalar=rstd[:, 0:1], in1=gain, op0=ALU.mult, op1=ALU.mult),
         reads=[xkey, 'rstd', gkey], writes=[okey])


def build_O():
    cx = Ctx()
    k = cx.k
    x_d = cx.din("x", [NTOK, D])
    upT_d = cx.din("upT", [256, HALO + NTOK])
    corr_d = cx.din("corr", [128, 2 * HALO])
    pwbd_d = cx.din("pwbd", [128, 256])
    prow_d = cx.din("prow", [1, 4 * 256])
    onorm_d = cx.din("onorm", [1, D])
    nsa_d = cx.din("nsa", [NTOK, 512])
    ys_d = cx.din("ys", [NTOK, 256])
    gluw_d = cx.din("gluw", [256, 256])
    wo_d = cx.din("wo", [D, D])
    cT_d = cx.din("cT", [128, 8])
    adaw_d = cx.din("adaw", [D, D])
    adab_d = cx.din("adab", [1, D])
    xo_d = cx.dout("xo", [NTOK, D])

    emit_consts(cx)
    emit_work(cx)
    NP = HALO + NTOK
    v = cx.sb("v", [128, 2, NP])
    sA = cx.sb("sA", [128, NP])
    sB = cx.sb("sB", [128, NP])
    pooled = cx.sb("pooled", [128, 2, NTOK], BF16)
    corr = cx.sb("corr", [128, 2, HALO])
    pwbd = cx.sb("pwbd", [128, 2, 128], BF16)
    prow = cx.sb("prow", [128, 4, 256])
    onorm = cx.sb("onorm", [128, D])
    gluw = cx.sb("gluw", [128, 2, 256], BF16)
    Wo = cx.sb("Wo", [128, 8, D], BF16)
    xt = [cx.sb("xt%d" % i, [128, D]) for i in range(2)]
    nsa = [cx.sb("nsa%d" % i, [128, 512]) for i in range(2)]
    ys = [cx.sb("ys%d" % i, [128, 256]) for i in range(2)]
    ycat = cx.sb("ycat", [128, D], BF16)
    rjunk = cx.sb("rjunk", [128, 512])
    yp = cx.sb("yp", [128, 256])
    yg = cx.sb("yg", [128, 256])
    gt = cx.sb("gt", [128, 256])
    ygb = cx.sb("ygb", [128, 256], BF16)
    ygT = cx.sb("ygT", [128, 2, 128], BF16)
    ycT = cx.sb("ycT", [128, 8, 128], BF16)
    xo = [cx.sb("xo%d" % i, [128, D]) for i in range(2)]
    pT = cx.ps("pT", [128, 8, 128], BF16)
    pq = cx.ps("pq", [128, 512])
    py = [cx.ps("py%d" % i, [128, 512]) for i in range(2)]

    k.dma('sp', corr[:].rearrange("p t h -> p (t h)"), corr_d[:, :], writes=['corr'])
    k.dma('pool', pwbd[:].rearrange("p t d -> p (t d)"), pwbd_d[:, :], writes=['pwbd'])
    k.dma('sp', prow[:].rearrange("p a b -> p (a b)"), prow_d[0:1, :].to_broadcast([128, 1024]), writes=['prow'])
    k.dma('sp', onorm[:], onorm_d[0:1, :].to_broadcast([128, D]), writes=['onorm'])
    k.dma('pool', gluw[:], gluw_d.rearrange("(c p) n -> p c n", p=128), writes=['gluw'])
    for kc in range(8):
        k.dma('pool', Wo[:, kc, :], wo_d[kc * 128:(kc + 1) * 128, :], writes=['Wo'])
    for t in range(2):
        k.dma('sp', v[:, t, :], upT_d[t * 128:(t + 1) * 128, :], writes=['v'])
    mod_b = emit_mod(cx, cT_d, adaw_d, adab_d, 1, ['py0', 'py1'], py)
    g2_b = mod_b[:, 0, :]

    TT = ALU.add
    for t in range(2):
        vt = v[:, t, :]
        k.op('dve', lambda e: e.tensor_tensor(out=sA[:, 1:NP], in0=vt[:, 1:NP], in1=vt[:, 0:NP - 1], op=TT), reads=['v'], writes=['sA'])
        if t == 0:
            k.op('dve', lambda e: e.tensor_tensor(out=sB[64:128, 3:NP], in0=sA[64:128, 3:NP], in1=sA[64:128, 1:NP - 2], op=TT), reads=['sA'], writes=['sB'])
            srcs = [(sA, 'sA', 0.5), (sB, 'sB', 0.25)]
        else:
            k.op('dve', lambda e: e.tensor_tensor(out=sB[:, 3:NP], in0=sA[:, 3:NP], in1=sA[:, 1:NP - 2], op=TT), reads=['sA'], writes=['sB'])
            k.op('dve', lambda e: e.tensor_tensor(out=sA[:, 7:NP], in0=sB[:, 7:NP], in1=sB[:, 3:NP - 4], op=TT), reads=['sB', 'sA'], writes=['sA'])
            k.op('dve', lambda e: e.tensor_tensor(out=sB[64:128, 15:NP], in0=sA[64:128, 15:NP], in1=sA[64:128, 7:NP - 8], op=TT),
                 reads=['sA', 'sB'], writes=['sB'])
            srcs = [(sA, 'sA', 0.125), (sB, 'sB', 0.0625)]
        for gi, (src, skey, iw) in enumerate(srcs):
            rs = slice(gi * 64, (gi + 1) * 64)
            k.op('dve', lambda e: e.tensor_tensor(out=src[rs, HALO:2 * HALO], in0=src[rs, HALO:2 * HALO], in1=corr[rs, t, :], op=ALU.mult),
                 reads=[skey, 'corr'], writes=[skey])
            k.op('dve', lambda e: e.scalar_tensor_tensor(out=pooled[rs, t, :], in0=src[rs, HALO:NP], scalar=iw, in1=vt[rs, HALO:NP],
                                                         op0=ALU.mult, op1=ALU.subtract), reads=[skey, 'v'], writes=['pooled'])

    for ti in range(NT):
        b = ti % 2
        tsl = slice(ti * 128, (ti + 1) * 128)
        k.dma('sp', xt[b][:], x_d[tsl, :], writes=['xt%d' % b])
        k.dma('sp', nsa[b][:], nsa_d[tsl, :], writes=['nsa%d' % b])
        k.dma('sp', ys[b][:], ys_d[tsl, :], writes=['ys%d' % b])
        for t in range(2):
            k.op('pe', lambda e, t=t: e.matmul(pq[:, t * 128:(t + 1) * 128], pooled[:, t, tsl], pwbd[:, t, :], start=True, stop=True),
                 reads=['pooled', 'pwbd'], writes=['pq'])
        k.op('dve', lambda e: e.tensor_tensor(out=yp[:], in0=pq[:, 0:256], in1=prow[:, 0, :], op=ALU.add), reads=['pq', 'prow'], writes=['yp'])
        k.op('pool', lambda e: e.tensor_tensor(out=yp[:], in0=yp[:], in1=prow[:, 1, :], op=ALU.mult), reads=['yp', 'prow'], writes=['yp'])
        emit_rms_rows(cx, ycat[:, 0:256], 'ycat', yp[:], 'yp', 256, onorm[:, 0:256], 'onorm', rjunk[:, 0:256])
        emit_rms_rows(cx, ycat[:, 256:768], 'ycat', nsa[b][:], 'nsa%d' % b, 512, onorm[:, 256:768], 'onorm', rjunk[:, 0:512])
        y0 = ys[b]
        k.op('dve', lambda e: e.tensor_tensor(out=gt[:], in0=y0[:], in1=y0[:], op=ALU.mult), reads=['ys%d' % b], writes=['gt'])
        k.op('dve', lambda e: e.tensor_scalar(out=gt[:], in0=gt[:], scalar1=0.044715, scalar2=1.0, op0=ALU.mult, op1=ALU.add), reads=['gt'], writes=['gt'])
        k.op('dve', lambda e: e.tensor_tensor(out=gt[:], in0=gt[:], in1=y0[:], op=ALU.mult), reads=['gt', 'ys%d' % b], writes=['gt'])
        k.op('act', lambda e: e.activation(out=gt[:], in_=gt[:], func=AF.Sigmoid, scale=1.5957691216057308), reads=['gt'], writes=['gt'])
        k.op('dve', lambda e: e.tensor_tensor(out=yg[:], in0=gt[:], in1=y0[:], op=ALU.mult), reads=['gt', 'ys%d' % b], writes=['yg'])
        k.op('pool', lambda e: e.tensor_copy(ygb[:], yg[:]), reads=['yg'], writes=['ygb'])
        for c in range(2):
            k.op('pe', lambda e, c=c: e.transpose(pT[:, c, :], ygb[:, c * 128:(c + 1) * 128], cx.ident[:]), reads=['ygb', 'ident'], writes=['pT'], pe_acc=(c > 0))
        k.op('act', lambda e: e.copy(out=ygT[:], in_=pT[:, 0:2, :]), reads=['pT'], writes=['ygT'])
        for c in range(2):
            k.op('pe', lambda e, c=c: e.matmul(pq[:, 256:512], ygT[:, c, :], gluw[:, c, :], start=(c == 0), stop=(c == 1)),
                 reads=['ygT', 'gluw'], writes=['pq'], pe_acc=(c > 0))
        k.op('dve', lambda e: e.tensor_tensor(out=gt[:], in0=pq[:, 256:512], in1=prow[:, 2, :], op=ALU.add), reads=['pq', 'prow'], writes=['gt'])
        k.op('act', lambda e: e.activation(out=gt[:], in_=gt[:], func=AF.Sigmoid), reads=['gt'], writes=['gt'])
        k.op('dve', lambda e: e.tensor_tensor(out=yg[:], in0=yg[:], in1=gt[:], op=ALU.mult), reads=['yg', 'gt'], writes=['yg'])
        emit_rms_rows(cx, ycat[:, 768:1024], 'ycat', yg[:], 'yg', 256, onorm[:, 768:1024], 'onorm', rjunk[:, 0:256])
        for c in range(8):
            k.op('pe', lambda e, c=c: e.transpose(pT[:, c, :], ycat[:, c * 128:(c + 1) * 128], cx.ident[:]), reads=['ycat', 'ident'], writes=['pT'], pe_acc=(c > 0))
        k.op('act', lambda e: e.copy(out=ycT[:], in_=pT[:, :, :]), reads=['pT'], writes=['ycT'])
        for h in range(2):
            for c in range(8):
                k.op('pe', lambda e, c=c: e.matmul(py[h][:, :], ycT[:, c, :], Wo[:, c, h * 512:(h + 1) * 512], start=(c == 0), stop=(c == 7)),
                     reads=['ycT', 'Wo'], writes=['py%d' % h], pe_acc=(c > 0))
        for h in range(2):
            k.op('dve', lambda e, h=h: e.tensor_tensor(out=xo[b][:, h * 512:(h + 1) * 512], in0=py[h][:, :], in1=g2_b[:, h * 512:(h + 1) * 512], op=ALU.mult),
                 reads=['py%d' % h, 'mod_b'], writes=['xo%d' % b])
        k.op('pool', lambda e: e.tensor_tensor(out=xo[b][:], in0=xo[b][:], in1=xt[b][:], op=ALU.add), reads=['xo%d' % b, 'xt%d' % b], writes=['xo%d' % b])
        cx.out_toks.append(k.dma('sp', xo_d[tsl, :], xo[b][:], reads=['xo%d' % b]))
    return cx.done()


def prep_O(x1, u_pool_seq, t0, nsa, ys, P, c, adaw, adab):
    f = np.float32
    up = np.zeros((HALO + NTOK, 256), f)
    lo = t0 - HALO
    if lo >= 0:
        up[:] = u_pool_seq[lo:t0 + NTOK]
    else:
        up[HALO:] = u_pool_seq[t0:t0 + NTOK]
    corr = np.ones((128, 2, HALO), f)
    if t0 == 0:
        tt = np.arange(HALO) + 1.0
        for t in range(2):
            for gi in range(2):
                w = (2, 4, 8, 16)[2 * t + gi]
                corr[gi * 64:(gi + 1) * 64, t, :] = (w / np.minimum(tt, w))[None, :]
    pwbd = np.zeros((128, 2, 128), f)
    for t in range(2):
        for gi in range(2):
            pwbd[gi * 64:(gi + 1) * 64, t, gi * 64:(gi + 1) * 64] = P['pool_w'][2 * t + gi]
    prow = np.concatenate([P['pool_b'].reshape(-1), P['pool_scale'], P['glu_b'], np.zeros(256, f)])[None, :]
    m = {"x": x1, "upT": up.T, "corr": corr.reshape(128, 2 * HALO), "pwbd": pwbd.reshape(128, 256), "prow": prow,
         "onorm": P['out_norm'][None, :], "nsa": nsa, "ys": ys, "gluw": P['glu_w'], "wo": P['w_out'],
         "cT": c.reshape(8, 128).T, "adaw": adaw, "adab": adab[None, :]}
    return {k_: np.ascontiguousarray(v, dtype=f) for k_, v in m.items()}


_PROGS = {}


def _prog(name):
    if name not in _PROGS:
        _PROGS[name] = {"F": build_F, "P": build_P, "B1": build_B1, "B2": build_B2, "O": build_O}[name]()
    return _PROGS[name]


def _run(name, maps):
    res = run_bass_kernel_spmd(_prog(name), maps, core_ids=list(range(8)))
    return res.results


def _rope_table(t0):
    pos = (np.arange(NTOK) + t0).astype(np.float32)
    inv = np.exp(-math.log(500000.0) * np.arange(8, dtype=np.float32) * (2.0 / 16)).astype(np.float32)
    ang = pos[:, None] * inv[None, :]
    return np.concatenate([np.cos(ang), np.sin(ang)], 1).astype(np.float32)


def kernel(**inp):
    f = np.float32
    inp = {k_: np.asarray(v, dtype=f) for k_, v in inp.items()}
    x = inp['x']
    c = inp['c']
    L = inp['ada_w'].shape[0]
    tabs = b2_tables()
    ca = lambda a: np.ascontiguousarray(a, dtype=f)
    xs = [ca(x[ci // 4, (ci % 4) * NTOK:(ci % 4 + 1) * NTOK]) for ci in range(8)]
    for l in range(L):
        P = {k_: inp[k_][l] for k_ in inp if k_ not in ('x', 'c')}
        aw, ab = P['ada_w'], P['ada_b']
        cTs = [ca(c[b].reshape(8, 128).T) for b in range(2)]

        def ffn_maps(xs_, m0, nrm, w_in, w_out):
            adaw = ca(aw[:, m0 * D:(m0 + 3) * D]); adab = ca(ab[None, m0 * D:(m0 + 3) * D])
            return [{"x": xs_[ci], "cT": cTs[ci // 4], "adaw": adaw, "adab": adab, "nrm": ca(nrm[None, :]),
                     "w_in": ca(w_in), "w_out": ca(w_out)} for ci in range(8)]

        r = _run("F", ffn_maps(xs, 0, P['norm_ffn1'], P['ffn1_w_in'], P['ffn1_w_out']))
        x1 = [r[ci]["xo"] for ci in range(8)]
        adaw = ca(aw[:, 3 * D:5 * D]); adab = ca(ab[None, 3 * D:5 * D])
        gains = ca(np.concatenate([P['q_norm'], P['k_norm'].reshape(-1)])[None, :])
        maps = [{"x": x1[ci], "cT": cTs[ci // 4], "adaw": adaw, "adab": adab, "nrm": ca(P['norm_mix'][None, :]),
                 "w_in": ca(P['w_in']), "gains": gains, "rope": _rope_table((ci % 4) * NTOK)} for ci in range(8)]
        r = _run("P", maps)
        U = [np.concatenate([r[b * 4 + kq]["u"] for kq in range(4)], 0) for b in range(2)]
        r = _run("B1", [prep_B1(U[ci // 4][:, 1560:1816], ci % 4, P) for ci in range(8)])
        ys = [np.concatenate([r[b * 4 + kq]["yT"].T for kq in range(4)], 1) for b in range(2)]
        r = _run("B2", [prep_B2(U[ci // 4], (ci % 4) // 2, ci % 2, P, tabs) for ci in range(8)])
        nsa = []
        for b in range(2):
            o = np.zeros((S, 2, 4, 64), f)
            for g in range(2):
                for hh in range(2):
                    o[:, g, 2 * hh:2 * hh + 2, :] = r[b * 4 + g * 2 + hh]["o"].reshape(S, 2, 64)
            nsa.append(o.reshape(S, 512))
        adaw = ca(aw[:, 5 * D:6 * D]); adab = ca(ab[5 * D:6 * D])
        maps = []
        for ci in range(8):
            b, t0 = ci // 4, (ci % 4) * NTOK
            maps.append(prep_O(x1[ci], U[b][:, 0:256], t0, nsa[b][t0:t0 + NTOK], ys[b][t0:t0 + NTOK], P, c[b], adaw, adab))
        r = _run("O", maps)
        x2 = [r[ci]["xo"] for ci in range(8)]
        r = _run("F", ffn_maps(x2, 6, P['norm_ffn2'], P['ffn2_w_in'], P['ffn2_w_out']))
        xs = [r[ci]["xo"] for ci in range(8)]
    out = np.zeros_like(x)
    for ci in range(8):
        out[ci // 4, (ci % 4) * NTOK:(ci % 4 + 1) * NTOK] = xs[ci]
    return out
```

```python
import math
from contextlib import ExitStack
import numpy as np
import concourse.bass as bass
import concourse.mybir as mybir
from concourse.bass_utils import run_bass_kernel_spmd

F32 = mybir.dt.float32
BF16 = mybir.dt.bfloat16
I32 = mybir.dt.int32
AF = mybir.ActivationFunctionType
ALU = mybir.AluOpType
AX = mybir.AxisListType

D = 1024
DFF = 2816
NTOK = 2048
NT = NTOK // 128
S = 8192
EPS = 1e-6
N_IN = 1816
N_DMA_SEMS = 16


class KB:
    def __init__(self, nc):
        self.nc = nc
        self.eng = {'pe': nc.tensor, 'act': nc.scalar, 'dve': nc.vector, 'pool': nc.gpsimd, 'sp': nc.sync}
        self.sem = {e: nc.alloc_semaphore('s_' + e) for e in self.eng}
        self.cnt = {e: 0 for e in self.eng}
        self.ekey = {e: e + '#0' for e in self.eng}
        self.epoch = {e: 0 for e in self.eng}
        self.seen = {e: {} for e in self.eng}
        self.dsem = [nc.alloc_semaphore('d%d' % i) for i in range(N_DMA_SEMS)]
        self.dcnt = 0
        self.lastw = {}
        self.reads = {}
        self.n_inst = 0
        self.kp = ''

    def _wait(self, e, tok):
        key, sem, val = tok
        if self.seen[e].get(key, 0) >= val:
            return
        self.seen[e][key] = val
        self.eng[e].wait_ge(sem, val)

    def _deps(self, e, reads, writes, pe_acc=False):
        toks = []
        for b in reads:
            t = self.lastw.get(b)
            if t is not None:
                toks.append((t, True))
        for b in writes:
            t = self.lastw.get(b)
            if t is not None:
                toks.append((t, False))
            toks.extend((t2, False) for t2 in self.reads.get(b, []))
        own = e + '#'
        for t, raw in toks:
            if t[0].startswith(own):
                if e == 'pe' or not raw:
                    continue
            if pe_acc and t[0].startswith('pe#'):
                continue
            self._wait(e, t)

    def _commit(self, tok, reads, writes):
        for b in reads:
            self.reads.setdefault(b, []).append(tok)
        for b in writes:
            self.lastw[b] = tok
            self.reads[b] = []

    def _pk(self, keys):
        return [(x if x.startswith('@') else self.kp + x) for x in keys]

    def op(self, e, inst_fn, reads=(), writes=(), pe_acc=False):
        reads, writes = self._pk(reads), self._pk(writes)
        self._deps(e, reads, writes, pe_acc)
        inst = inst_fn(self.eng[e])
        if self.cnt[e] >= 60000:
            self.epoch[e] += 1
            self.ekey[e] = '%s#%d' % (e, self.epoch[e])
            self.sem[e] = self.nc.alloc_semaphore('s_%s_%d' % (e, self.epoch[e]))
            self.cnt[e] = 0
        self.cnt[e] += 1
        inst.then_inc(self.sem[e], 1)
        tok = (self.ekey[e], self.sem[e], self.cnt[e])
        self._commit(tok, reads, writes)
        self.n_inst += 1
        return inst

    def dma(self, e, out, in_, reads=(), writes=(), **kw):
        reads, writes = self._pk(reads), self._pk(writes)
        i = self.dcnt
        self.dcnt += 1
        s = self.dsem[i % N_DMA_SEMS]
        kk = i // N_DMA_SEMS
        key = 'd%d' % (i % N_DMA_SEMS)
        if kk > 0:
            self._wait(e, (key, s, 16 * kk))
        self._deps(e, reads, writes)
        inst = self.eng[e].dma_start(out=out, in_=in_, **kw)
        inst.then_inc(s, 16)
        tok = (key, s, 16 * (kk + 1))
        self._commit(tok, reads, writes)
        self.n_inst += 1
        return tok

    def finish(self, toks):
        for t in toks:
            self._wait('sp', t)

    def stage_reset(self):
        toks = []
        for e in self.eng:
            if self.cnt[e] > 0:
                toks.append((self.ekey[e], self.sem[e], self.cnt[e]))
        for j in range(N_DMA_SEMS):
            uses = (self.dcnt - j + N_DMA_SEMS - 1) // N_DMA_SEMS if self.dcnt > j else 0
            if uses > 0:
                toks.append(('d%d' % j, self.dsem[j], 16 * uses))
        for e in self.eng:
            for t in toks:
                self._wait(e, t)
        self.lastw = {}
        self.reads = {}


class Ctx:
    def __init__(self):
        self.nc = bass.Bass("TRN2", target_bir_lowering=False)
        self.es = ExitStack()
        self.k = KB(self.nc)
        self.out_toks = []
        self.pfx = ""

    def din(self, name, shape, dt=F32):
        return self.nc.dram_tensor(name, list(shape), dt, kind="ExternalInput").ap()

    def dout(self, name, shape, dt=F32):
        return self.nc.dram_tensor(name, list(shape), dt, kind="ExternalOutput").ap()

    def dscr(self, name, shape, dt=F32):
        return self.nc.dram_tensor(name, list(shape), dt, kind="Internal").ap()

    def sb(self, name, shape, dt=F32):
        return self.es.enter_context(self.nc.sbuf_tensor(self.pfx + "s_" + name, list(shape), dt))

    def ps(self, name, shape, dt=F32):
        return self.es.enter_context(self.nc.psum_tensor(self.pfx + "p_" + name, list(shape), dt))

    def begin_stage(self, pfx):
        self.pfx = pfx
        self.es = ExitStack()
        self._ropekey = 'rope'

    def end_stage(self):
        self.k.stage_reset()
        self.out_toks = []
        self.es.close()

    def done(self):
        self.k.finish(self.out_toks)
        self.es.close()
        return self.nc


def emit_consts(cx):
    k = cx.k
    ident_f = cx.sb("ident_f", [128, 128], F32)
    ident = cx.sb("ident", [128, 128], BF16)
    ones_r = cx.sb("ones_r", [1, 128], BF16)
    k.op('pool', lambda e: e.memset(ident_f[:], 1.0), writes=['ident_f'])
    k.op('pool', lambda e: e.affine_select(out=ident_f[:], in_=ident_f[:], pattern=[[-1, 128]],
                                           compare_op=ALU.is_equal, fill=0.0, base=0, channel_multiplier=1),
         reads=['ident_f'], writes=['ident_f'])
    k.op('dve', lambda e: e.tensor_copy(ident[:], ident_f[:]), reads=['ident_f'], writes=['ident'])
    k.op('dve', lambda e: e.memset(ones_r[:], 1.0), writes=['ones_r'])
    cx.ident = ident
    cx.ident_f = ident_f
    cx.ones_r = ones_r


def emit_mod(cx, cT_d, adaw_d, adab_d, nmod, ps_keys, ps_tiles):
    k = cx.k
    mod_b = cx.sb("mod_b", [128, nmod, 1024], F32)
    cT = cx.sb("cT", [128, 8], F32)
    sig = cx.sb("csig", [128, 8], F32)
    crep = cx.sb("crep", [128, 8, 128], BF16)
    adabc = [cx.sb("adab_ch%d" % i, [1, 128], BF16) for i in range(2)]
    wch = [cx.sb("adaw_ch%d" % i, [128, 8, 128], BF16) for i in range(2)]
    k.dma('sp', cT[:], cT_d[:, :], writes=['cT'])
    k.op('act', lambda e: e.activation(out=sig[:], in_=cT[:], func=AF.Sigmoid), reads=['cT'], writes=['csig'])
    k.op('dve', lambda e: e.tensor_tensor(out=sig[:], in0=sig[:], in1=cT[:], op=ALU.mult),
         reads=['csig', 'cT'], writes=['csig'])
    k.op('dve', lambda e: e.tensor_copy(crep[:], sig[:].unsqueeze(2).to_broadcast([128, 8, 128])),
         reads=['csig'], writes=['crep'])
    nch = nmod * 8
    for ch in range(nch):
        w = wch[ch % 2]
        wk = 'adaw_ch%d' % (ch % 2)
        k.dma('pool', w[:], adaw_d[:, ch * 128:(ch + 1) * 128].rearrange("(kc p) n -> p kc n", p=128),
              writes=[wk])
        k.dma('pool', adabc[ch % 2][:], adab_d[:, ch * 128:(ch + 1) * 128], writes=['adab_ch%d' % (ch % 2)])
        pk = ps_keys[ch % 2]
        pt = ps_tiles[ch % 2]
        for kc in range(8):
            k.op('pe', lambda e, kc=kc: e.matmul(pt[:, 0:128], crep[:, kc, :], w[:, kc, :], start=(kc == 0), stop=False),
                 reads=['crep', wk], writes=[pk], pe_acc=(kc > 0))
        k.op('pe', lambda e: e.matmul(pt[:, 0:128], cx.ones_r[:, :], adabc[ch % 2][:, :], start=False, stop=True),
             reads=['ones_r', 'adab_ch%d' % (ch % 2)], writes=[pk], pe_acc=True)
        k.op('act', lambda e: e.copy(out=mod_b[:, ch // 8, (ch % 8) * 128:(ch % 8 + 1) * 128], in_=pt[:, 0:128]),
             reads=[pk], writes=['mod_b'])
    return mod_b


def emit_norm_mod(cx, x_t, xkey, gs_b, sh_b, hb, hbkey):
    k = cx.k
    W = cx.work
    ss, rstd, hf = W['ss'], W['rstd'], W['hf']
    k.op('dve', lambda e: e.scalar_tensor_tensor(out=hf[:], in0=x_t[:], scalar=1.0, in1=x_t[:], op0=ALU.mult, op1=ALU.mult,
                                                 accum_out=ss[:, 0:1]),
         reads=[xkey], writes=['hf', 'ss'])
    k.op('dve', lambda e: e.tensor_scalar(out=ss[:, 0:1], in0=ss[:, 0:1], scalar1=1.0 / D, scalar2=EPS, op0=ALU.mult, op1=ALU.add),
         reads=['ss'], writes=['ss'])
    k.op('pool', lambda e: e.tensor_tensor(out=rstd[:, 0:1], in0=ss[:, 0:1], in1=W['mhalf'][:, 0:1], op=ALU.pow),
         reads=['ss', 'mhalf'], writes=['rstd'])
    k.op('dve', lambda e: e.scalar_tensor_tensor(out=hf[:], in0=x_t[:], scalar=rstd[:, 0:1], in1=gs_b,
                                                 op0=ALU.mult, op1=ALU.mult),
         reads=[xkey, 'rstd', 'modd'], writes=['hf'])
    k.op('pool', lambda e: e.tensor_tensor(out=hb[:], in0=hf[:], in1=sh_b, op=ALU.add),
         reads=['hf', 'modd'], writes=[hbkey])


def emit_T(cx, hb, hbkey, hT, hTkey, col0, pT, pTkey):
    k = cx.k
    for c in range(8):
        k.op('pe', lambda e, c=c: e.transpose(pT[:, c, :], hb[:, c * 128:(c + 1) * 128], cx.ident[:]),
             reads=[hbkey, 'ident'], writes=[pTkey], pe_acc=(c > 0))
    k.op('act', lambda e: e.copy(out=hT[:, :, col0:col0 + 128], in_=pT[:, :, :]), reads=[pTkey], writes=[hTkey])


def emit_norm_mod_T(cx, x_t, xkey, gs_b, sh_b, hT, hTkey, col0, pT, pTkey, tag):
    emit_norm_mod(cx, x_t, xkey, gs_b, sh_b, cx.work['hb'], 'hb')
    emit_T(cx, cx.work['hb'], 'hb', hT, hTkey, col0, pT, pTkey)


def emit_work(cx, with_hb=True):
    W = {}
    W['ss'] = cx.sb("ss", [128, 1], F32)
    W['rstd'] = cx.sb("rstd", [128, 1], F32)
    W['hf'] = cx.sb("hf", [128, 1024], F32)
    if with_hb:
        W['hb'] = cx.sb("hb", [128, 1024], BF16)
    W['eps'] = cx.sb("eps", [128, 1], F32)
    W['mhalf'] = cx.sb("mhalf", [128, 1], F32)
    cx.k.op('dve', lambda e: e.memset(W['eps'][:], EPS), writes=['eps'])
    cx.k.op('dve', lambda e: e.memset(W['mhalf'][:], -0.5), writes=['mhalf'])
    cx.work = W


GRP = 256


def stage_F(cx, io, ntok):
    k = cx.k
    x_d, cT_d, adaw_d, adab_d, nrm_d, win_d, wout_d, xo_d = [io[n] for n in ("x", "cT", "adaw", "adab", "nrm", "w_in", "w_out", "xo")]

    emit_consts(cx)
    emit_work(cx, with_hb=False)
    Win = cx.sb("Win", [128, 8, 2 * DFF], BF16)
    Wout = cx.sb("Wout", [128, 22, D], BF16)
    nrm_b = cx.work['hf']
    hTs = [cx.sb("hT%d" % i, [128, 8, GRP], BF16) for i in range(2)]
    hbs = [cx.sb("hb%d" % i, [128, D], BF16) for i in range(GRP // 128)]
    actT = cx.sb("actT", [128, 22, GRP], BF16)
    sg = [cx.sb("sg%d" % i, [128, GRP], F32) for i in range(2)]
    xt = [cx.sb("xt%d" % i, [128, D], F32) for i in range(6)]
    pT = cx.ps("pT", [128, 8, 128], BF16)
    pg = [cx.ps("pg%d" % i, [128, 512], F32) for i in range(2)]
    pu = [cx.ps("pu%d" % i, [128, 512], F32) for i in range(2)]
    py = [cx.ps("py%d" % i, [128, 512], F32) for i in range(2)]

    k.dma('sp', nrm_b[:], nrm_d[0:1, :].to_broadcast([128, D]), writes=['hf'])
    mod_b = emit_mod(cx, cT_d, adaw_d, adab_d, 3, ['pg0', 'pg1'], pg)
    for kc in range(8):
        k.dma('pool', Win[:, kc, :], win_d[kc * 128:(kc + 1) * 128, :], writes=['Win'])
    for kc in range(22):
        k.dma('pool', Wout[:, kc, :], wout_d[kc * 128:(kc + 1) * 128, :], writes=['Wout'])
    k.op('dve', lambda e: e.scalar_tensor_tensor(out=mod_b[:, 1, :], in0=mod_b[:, 1, :], scalar=1.0, in1=nrm_b[:],
                                                 op0=ALU.add, op1=ALU.mult),
         reads=['mod_b', 'hf'], writes=['modd'])
    k.op('dve', lambda e: e.tensor_scalar(out=mod_b[:, 2, :], in0=mod_b[:, 2, :], scalar1=0.5, scalar2=None, op0=ALU.mult),
         reads=['mod_b', 'modd'], writes=['modd'])
    sh_b, gs_b, hg_b = mod_b[:, 0, :], mod_b[:, 1, :], mod_b[:, 2, :]

    ngrp = ntok // GRP
    tpg = GRP // 128

    def prep_load(g):
        for t in range(tpg):
            ti = g * tpg + t
            xi = (g % 3) * tpg + t
            k.dma('act', xt[xi][:], x_d[ti * 128:(ti + 1) * 128, :], writes=['xt%d' % xi])

    def prep_norm_t(g, t):
        xi = (g % 3) * tpg + t
        emit_norm_mod(cx, xt[xi], 'xt%d' % xi, gs_b, sh_b, hbs[t], 'hb%d' % t)

    def prep_norm(g):
        prep_load(g)
        for t in range(tpg):
            prep_norm_t(g, t)

    def prep_T(g, t):
        emit_T(cx, hbs[t], 'hb%d' % t, hTs[g % 2], 'hT%d' % (g % 2), t * 128, pT, 'pT')

    prep_norm(0)
    for t in range(tpg):
        prep_T(0, t)
    for g in range(ngrp):
        hT, hTk = hTs[g % 2], 'hT%d' % (g % 2)
        if g + 1 < ngrp:
            prep_load(g + 1)
        for i in range(22):
            b = i % 2
            for kc in range(8):
                k.op('pe', lambda e, kc=kc: e.matmul(pg[b][:, 0:GRP], Win[:, kc, i * 128:(i + 1) * 128], hT[:, kc, :],
                                                     start=(kc == 0), stop=(kc == 7)),
                     reads=['Win', hTk], writes=['pg%d' % b], pe_acc=(kc > 0))
            for kc in range(8):
                k.op('pe', lambda e, kc=kc: e.matmul(pu[b][:, 0:GRP], Win[:, kc, DFF + i * 128:DFF + (i + 1) * 128],
                                                     hT[:, kc, :], start=(kc == 0), stop=(kc == 7)),
                     reads=['Win', hTk], writes=['pu%d' % b], pe_acc=(kc > 0))
            k.op('act', lambda e: e.activation(out=sg[b][:], in_=pg[b][:, 0:GRP], func=AF.Silu),
                 reads=['pg%d' % b], writes=['sg%d' % b])
            k.op('dve', lambda e: e.tensor_tensor(out=actT[:, i, :], in0=sg[b][:], in1=pu[b][:, 0:GRP], op=ALU.mult),
                 reads=['sg%d' % b, 'pu%d' % b], writes=['actT'])
            if g + 1 < ngrp:
                if i in (1, 5):
                    prep_norm_t(g + 1, 0 if i == 1 else 1)
                if i in (10, 16):
                    prep_T(g + 1, 0 if i == 10 else 1)
        for t in range(tpg):
            ti = g * tpg + t
            xi = (g % 3) * tpg + t
            for h in range(2):
                for i in range(22):
                    k.op('pe', lambda e, i=i: e.matmul(py[h][:, :], actT[:, i, t * 128:(t + 1) * 128],
                                                       Wout[:, i, h * 512:(h + 1) * 512], start=(i == 0), stop=(i == 21)),
                         reads=['actT', 'Wout'], writes=['py%d' % h], pe_acc=(i > 0))
            hf = cx.work['hf']
            for h in range(2):
                k.op('dve', lambda e: e.tensor_tensor(out=hf[:, h * 512:(h + 1) * 512], in0=py[h][:, :],
                                                      in1=hg_b[:, h * 512:(h + 1) * 512], op=ALU.mult),
                     reads=['py%d' % h, 'modd'], writes=['hf'])
            k.op('pool', lambda e: e.tensor_tensor(out=xt[xi][:], in0=xt[xi][:], in1=hf[:], op=ALU.add),
                 reads=['hf', 'xt%d' % xi], writes=['xt%d' % xi])
            cx.out_toks.append(k.dma('sp', xo_d[ti * 128:(ti + 1) * 128, :], xt[xi][:], reads=['xt%d' % xi]))


def emit_headnorm_rope(cx, U, ukey, c0, nh, gain_b, cos_b, sin_b, tmp, sq, per_head_gain=False):
    k = cx.k
    v = U[:, c0:c0 + nh * 64].rearrange("p (h d) -> p h d", d=64)
    t = tmp[:, 0:nh * 64].rearrange("p (h d) -> p h d", d=64)
    ssq = sq[:, 0:nh]
    rs = sq[:, 16:16 + nh]
    k.op('dve', lambda e: e.tensor_tensor(out=t, in0=v, in1=v, op=ALU.mult), reads=[ukey], writes=['hn_tmp'])
    k.op('dve', lambda e: e.tensor_reduce(out=ssq, in_=t, axis=AX.X, op=ALU.add), reads=['hn_tmp'], writes=['hn_sq'])
    k.op('act', lambda e: e.activation(out=rs, in_=ssq, func=AF.Sqrt, scale=1.0 / 64, bias=cx.work['eps'][:, 0:1]),
         reads=['hn_sq', 'eps'], writes=['hn_sq'])
    k.op('dve', lambda e: e.reciprocal(out=rs, in_=rs), reads=['hn_sq'], writes=['hn_sq'])
    k.op('dve', lambda e: e.tensor_tensor(out=v, in0=v, in1=rs.unsqueeze(2).to_broadcast([128, nh, 64]), op=ALU.mult),
         reads=[ukey, 'hn_sq'], writes=[ukey])
    k.op('dve', lambda e: e.tensor_tensor(out=v, in0=v, in1=(gain_b if per_head_gain else gain_b.unsqueeze(1).to_broadcast([128, nh, 64])), op=ALU.mult),
         reads=[ukey, 'gains'], writes=[ukey])
    x1 = v[:, :, 0:8]
    x2 = v[:, :, 8:16]
    cb = cos_b.unsqueeze(1).to_broadcast([128, nh, 8])
    sb_ = sin_b.unsqueeze(1).to_broadcast([128, nh, 8])
    a, b, c, d = t[:, :, 0:8], t[:, :, 8:16], t[:, :, 16:24], t[:, :, 24:32]
    k.op('dve', lambda e: e.tensor_tensor(out=a, in0=x1, in1=cb, op=ALU.mult), reads=[ukey, getattr(cx, '_ropekey', 'rope')], writes=['hn_tmp'])
    k.op('dve', lambda e: e.tensor_tensor(out=b, in0=x2, in1=sb_, op=ALU.mult), reads=[ukey, getattr(cx, '_ropekey', 'rope')], writes=['hn_tmp'])
    k.op('dve', lambda e: e.tensor_tensor(out=c, in0=x2, in1=cb, op=ALU.mult), reads=[ukey, getattr(cx, '_ropekey', 'rope')], writes=['hn_tmp'])
    k.op('dve', lambda e: e.tensor_tensor(out=d, in0=x1, in1=sb_, op=ALU.mult), reads=[ukey, getattr(cx, '_ropekey', 'rope')], writes=['hn_tmp'])
    k.op('dve', lambda e: e.tensor_tensor(out=x1, in0=a, in1=b, op=ALU.subtract), reads=['hn_tmp'], writes=[ukey])
    k.op('dve', lambda e: e.tensor_tensor(out=x2, in0=c, in1=d, op=ALU.add), reads=['hn_tmp'], writes=[ukey])


def stage_P(cx, io, ntok, co_setup=None):
    k = cx.k
    x_d, cT_d, adaw_d, adab_d, nrm_d, win_d, gains_d, rope_d, u_d, uT_d = [io[n] for n in (
        "x", "cT", "adaw", "adab", "nrm", "w_in", "gains", "rope", "u", "uT")]
    UTt = cx.sb("UTt", [128, 15, 128], F32)

    emit_consts(cx)
    emit_work(cx, with_hb=False)
    Win = cx.sb("Win", [128, 8, N_IN], BF16)
    nrm_b = cx.work['hf']
    gains_b = cx.sb("gains_b", [128, 768], F32)
    hTs = [cx.sb("hT%d" % i, [128, 8, 128], BF16) for i in range(2)]
    hbs = [cx.sb("hb%d" % i, [128, D], BF16) for i in range(2)]
    xt = [cx.sb("xt%d" % i, [128, D], F32) for i in range(2)]
    U = [cx.sb("U%d" % i, [128, N_IN], F32) for i in range(2)]
    rope = [cx.sb("rope%d" % i, [128, 16], F32) for i in range(2)]
    tmp = cx.sb("hn_tmp", [128, 768], F32)
    sq = cx.sb("hn_sq", [128, 32], F32)
    pT = cx.ps("pT", [128, 8, 128], BF16)
    pu = [cx.ps("pu%d" % i, [128, 512], F32) for i in range(4)]

    k.dma('sp', nrm_b[:], nrm_d[0:1, :].to_broadcast([128, D]), writes=['hf'])
    k.dma('sp', gains_b[:], gains_d[0:1, :].to_broadcast([128, 768]), writes=['gains'])
    k.op('dve', lambda e: e.tensor_scalar(out=gains_b[:, 0:512], in0=gains_b[:, 0:512], scalar1=0.125, scalar2=None, op0=ALU.mult),
         reads=['gains'], writes=['gains'])
    mod_b = emit_mod(cx, cT_d, adaw_d, adab_d, 2, ['pu0', 'pu1'], pu)
    for kc in range(8):
        k.dma('pool', Win[:, kc, :], win_d[kc * 128:(kc + 1) * 128, :], writes=['Win'])
    k.op('dve', lambda e: e.scalar_tensor_tensor(out=mod_b[:, 1, :], in0=mod_b[:, 1, :], scalar=1.0, in1=nrm_b[:],
                                                 op0=ALU.add, op1=ALU.mult),
         reads=['mod_b', 'hf'], writes=['modd'])
    sh_b, gs_b = mod_b[:, 0, :], mod_b[:, 1, :]
    chunks = [(0, 512), (512, 512), (1024, 512), (1536, 280)]
    nt_ = ntok // 128

    def P1(ti):
        b = ti % 2
        k.dma('sp', xt[b][:], x_d[ti * 128:(ti + 1) * 128, :], writes=['xt%d' % b])
        k.dma('sp', rope[ti % 2][:], rope_d[ti * 128:(ti + 1) * 128, :], writes=['rope%d' % (ti % 2)])
        emit_norm_mod(cx, xt[b], 'xt%d' % b, gs_b, sh_b, hbs[b], 'hb%d' % b)

    def P2(ti):
        b = ti % 2
        ub_, ukey = U[ti % 2], 'U%d' % (ti % 2)
        emit_T(cx, hbs[b], 'hb%d' % b, hTs[b], 'hT%d' % b, 0, pT, '@pT')
        for ci, (c0, cw) in enumerate(chunks):
            for kc in range(8):
                k.op('pe', lambda e, kc=kc: e.matmul(pu[ci][:, 0:cw], hTs[b][:, kc, :], Win[:, kc, c0:c0 + cw],
                                                     start=(kc == 0), stop=(kc == 7)),
                     reads=['hT%d' % b, 'Win'], writes=['pu%d' % ci], pe_acc=(kc > 0))
            k.op('act', lambda e: e.copy(out=ub_[:, c0:c0 + cw], in_=pu[ci][:, 0:cw]), reads=['pu%d' % ci], writes=[ukey])

    def P3d(ti):
        ub_, ukey = U[ti % 2], 'U%d' % (ti % 2)
        rp = rope[ti % 2]
        cx._ropekey = 'rope%d' % (ti % 2)
        emit_headnorm_rope(cx, ub_, ukey, 256, 12, gains_b[:, :].rearrange("p (h d) -> p h d", d=64), rp[:, 0:8], rp[:, 8:16], tmp, sq,
                           per_head_gain=True)
        k.op('act', lambda e: e.activation(out=ub_[:, 1536:1560], in_=ub_[:, 1536:1560], func=AF.Sigmoid),
             reads=[ukey], writes=[ukey])
        cx.out_toks.append(k.dma('sp', u_d[ti * 128:(ti + 1) * 128, :], ub_[:], reads=[ukey]))

    def P3p(ti):
        ub_, ukey = U[ti % 2], 'U%d' % (ti % 2)
        for i in range(15):
            cw = min(128, N_IN - i * 128)
            k.op('pe', lambda e, i=i, cw=cw: e.transpose(pu[i // 4][0:cw, (i % 4) * 128:(i % 4 + 1) * 128], ub_[:, i * 128:i * 128 + cw], cx.ident_f[:]),
                 reads=[ukey, 'ident_f'], writes=['pu%d' % (i // 4)])
        for j in range(4):
            if j < 3:
                k.op('act', lambda e, j=j: e.copy(out=UTt[:, j * 4:j * 4 + 4, :], in_=pu[j][:, :].rearrange("p (i t) -> p i t", t=128)),
                     reads=['pu%d' % j], writes=['UTt'])
            else:
                k.op('act', lambda e: e.copy(out=UTt[:, 12:14, :], in_=pu[3][:, 0:256].rearrange("p (i t) -> p i t", t=128)),
                     reads=['pu3'], writes=['UTt'])
                k.op('act', lambda e: e.copy(out=UTt[0:24, 14, :], in_=pu[3][0:24, 256:384]), reads=['pu3'], writes=['UTt'])
        tsl = slice(ti * 128, (ti + 1) * 128)
        cx.out_toks.append(k.dma('sp', uT_d[0:1792, tsl].rearrange("(i p) t -> p i t", p=128), UTt[:, 0:14, :], reads=['UTt'], writes=['@UTa%d' % ti]))
        cx.out_toks.append(k.dma('sp', uT_d[1792:1816, tsl], UTt[0:24, 14, :], reads=['UTt'], writes=['@UTb%d' % ti]))

    adv = fin = None
    if co_setup is not None:
        k.kp = 'S:'
        NJ_ = LC + 1
        al = [U[0][:, 0:NJ_], U[0][:, NJ_:2 * NJ_], U[0][:, 2 * NJ_:3 * NJ_], U[1][:, 0:NJ_], U[1][:, NJ_:2 * NJ_]]
        adv, fin = co_setup(pT, al)
        k.kp = ''
        k.stage_reset()
        k.kp = 'P:'
    P1(0)
    for i in range(nt_ + 1):
        if i + 1 < nt_:
            P1(i + 1)
        if i < nt_:
            P2(i)
            P3d(i)
        if 0 <= i - 1 < nt_:
            P3p(i - 1)
            if adv is not None:
                k.kp = 'S:'
                adv(i // 4, 8)
                k.kp = 'P:'
    if fin is not None:
        k.kp = 'S:'
        fin()
    k.kp = ''


NB_NEG = -30000.0


def b2_tables():
    p = np.arange(128)
    caus = (p[:, None] <= p[None, :]).astype(np.float32)
    wlow = (p[:, None] > p[None, :]).astype(np.float32)
    f = np.floor((p - 31) / 16.0)
    mc = np.zeros((128, 17, 128), np.float32)
    for i in range(17):
        mc[:, i, :] = ((p[:, None] - 8 * i) <= f[None, :])
    n = np.arange(512)
    ovl = np.zeros((512, 128), np.float32)
    c_start = n[:, None] * 16
    s_start = np.arange(128)[None, :] * 64
    ovl = np.clip(np.minimum(c_start + 32, s_start + 64) - np.maximum(c_start, s_start), 0, None).astype(np.float32) / 32
    ovl[511, :] = 0
    T = np.zeros((128, 254), np.float32)
    cr = (p >= 64).astype(np.int64)
    m = np.arange(254) - 126
    T[(m[None, :] == cr[:, None]) | (m[None, :] == cr[:, None] - 1)] = 1000.0
    T[m[None, :] > cr[:, None]] = -1e30
    c = np.arange(8192)
    eslot = (np.arange(64)[:, None] == ((c // 64) % 64)[None, :]).astype(np.float32)
    pos = (np.arange(512) * 16 + 31).astype(np.float32)
    inv = np.exp(-math.log(500000.0) * np.arange(8, dtype=np.float32) * (2.0 / 16)).astype(np.float32)
    ang = pos[:, None] * inv[None, :]
    ropec = np.concatenate([np.cos(ang), np.sin(ang)], 1).astype(np.float32)
    return dict(caus=caus, wlow=wlow, mc=mc.reshape(128, 17 * 128), ovl=ovl, T=T, eslot=eslot, ropec=ropec)


def stage_B2(cx, io):
    k = cx.k
    (ksT_d, kwT_d, vs_d, vw_d, kcT_d, vcT_d, gat_d, w1k_d, w1v_d, pek_d, pev_d, w2k_d, w2v_d, kn0_d, caus_d, wlow_d,
     mc_d, ovl_d, T_d, eslot_d, ropec_d, o_d) = [io[n] for n in (
        "ksT", "kwT", "vs", "vw", "kcT", "vcT", "gat", "w1k", "w1v", "pek", "pev", "w2k", "w2v", "kn0", "caus", "wlow",
        "mc", "ovl", "T", "eslot", "ropec", "o")]

    emit_consts(cx)
    emit_work(cx)
    ks_aug = cx.sb("ks_aug", [128, S], BF16)
    kwT = cx.sb("kwT", [64, S], BF16)
    vs1 = cx.sb("vs1", [128, 64, 65], BF16)
    vw1 = cx.sb("vw1", [128, 64, 65], BF16)
    xk = cx.sb("xk", [64, S + 16], BF16)
    xv = cx.sb("xv", [64, S + 16], BF16)
    w1k = cx.sb("w1k", [64, 32, 128], BF16)
    w1v = cx.sb("w1v", [64, 32, 128], BF16)
    pek = cx.sb("pek", [64, 32], BF16)
    pev = cx.sb("pev", [64, 32], BF16)
    w2k = cx.sb("w2k", [128, 64], BF16)
    w2v = cx.sb("w2v", [128, 64], BF16)
    kcT = cx.sb("kcT", [64, 512], BF16)
    V1c = cx.sb("V1c", [128, 4, 193], BF16)
    gat = cx.sb("gat", [128, 64, 12], F32)
    caus = cx.sb("caus", [128, 128], BF16)
    wlow = cx.sb("wlow", [128, 128], BF16)
    mc = cx.sb("mc", [128, 17, 128], BF16)
    Tt = cx.sb("Tt", [128, 254], F32)
    kn0_b = cx.sb("kn0_b", [128, 64], F32)
    ropec = cx.sb("ropec", [128, 4, 16], F32)
    cbias = cx.sb("cbias", [128, 1], F32)
    xb = cx.sb("xb", [128, 512], F32)
    gt = cx.sb("gt", [128, 512], F32)
    hid = cx.sb("hid", [128, 512], BF16)
    kcf = cx.sb("kcf", [128, 64], F32)
    kcb = cx.sb("kcb", [128, 64], BF16)
    hn_tmp = cx.sb("hn_tmp", [128, 64], F32)
    hn_sq = cx.sb("hn_sq", [128, 32], F32)
    Qa = [[cx.sb("Qa%d_%d" % (i, j), [128, 512], BF16) for j in range(2)] for i in range(3)]
    pt = [cx.sb("pt%d" % i, [128, 512], BF16) for i in range(3)]
    oacc = [cx.sb("oacc%d" % i, [128, 4, 64], F32) for i in range(2)]
    imp = cx.sb("imp", [128, 128], F32)
    imp2 = cx.sb("imp2", [128, 128], F32)
    m8 = cx.sb("m8", [128, 16], F32)
    lr = cx.sb("lr", [128, 8], F32)
    otmp = cx.sb("otmp", [128, 4, 64], F32)
    nbshs = [[cx.sb("nbsh%d_%d" % (i, pp), [128, 128], BF16) for i in range(2)] for pp in range(2)]
    ps_s = [cx.ps("ps_s%d" % i, [128, 512], F32) for i in range(2)]
    po = [cx.ps("po%d" % i, [128, 512], F32) for i in range(4)]
    pm = cx.ps("pm", [128, 1024], BF16)

    for c4 in range(4):
        sl = slice(c4 * 2048, (c4 + 1) * 2048)
        k.dma('pool', ks_aug[0:64, sl], ksT_d[:, sl], writes=['ks_aug'])
        k.dma('pool', ks_aug[64:128, sl], eslot_d[:, sl], writes=['ks_aug'])
        k.dma('pool', kwT[:, sl], kwT_d[:, sl], writes=['kwT'])
        k.dma('pool', xk[:, sl], kcT_d[:, sl], writes=['xk'])
        k.dma('pool', xv[:, sl], vcT_d[:, sl], writes=['xv'])
    k.op('dve', lambda e: e.memset(xk[:, S:S + 16], 0.0), writes=['xk'])
    k.op('dve', lambda e: e.memset(xv[:, S:S + 16], 0.0), writes=['xv'])
    k.op('dve', lambda e: e.memset(vs1[:, :, 64:65], 1.0), writes=['vs1'])
    k.op('dve', lambda e: e.memset(vw1[:, :, 64:65], 1.0), writes=['vw1'])
    k.op('dve', lambda e: e.memset(V1c[:, :, 64:65], 1.0), writes=['V1c'])
    for c4 in range(4):
        k.dma('pool', vs1[:, c4 * 16:(c4 + 1) * 16, 0:64],
              vs_d[c4 * 2048:(c4 + 1) * 2048, :].rearrange("(kt p) d -> p kt d", p=128), writes=['vs1'])
        k.dma('pool', vw1[:, c4 * 16:(c4 + 1) * 16, 0:64],
              vw_d[c4 * 2048:(c4 + 1) * 2048, :].rearrange("(kt p) d -> p kt d", p=128), writes=['vw1'])
        k.dma('sp', gat[:, c4 * 16:(c4 + 1) * 16, :],
              gat_d[c4 * 2048:(c4 + 1) * 2048, :].rearrange("(jb p) c -> p jb c", p=128), writes=['gat'])
    k.dma('pool', w1k[:].rearrange("p i h -> p (i h)"), w1k_d[:, :], writes=['w1k'])
    k.dma('pool', w1v[:].rearrange("p i h -> p (i h)"), w1v_d[:, :], writes=['w1v'])
    k.dma('pool', pek[:], pek_d[:, :], writes=['pek'])
    k.dma('pool', pev[:], pev_d[:, :], writes=['pev'])
    k.dma('pool', w2k[:], w2k_d[:, :], writes=['w2k'])
    k.dma('pool', w2v[:], w2v_d[:, :], writes=['w2v'])
    k.dma('pool', caus[:], caus_d[:, :], writes=['caus'])
    k.dma('pool', wlow[:], wlow_d[:, :], writes=['wlow'])
    k.dma('pool', mc[:].rearrange("p i q -> p (i q)"), mc_d[:, :], writes=['mc'])
    k.dma('pool', V1c[:, :, 65:193], ovl_d.rearrange("(nt p) j -> p nt j", p=128), writes=['V1c'])
    k.dma('sp', Tt[:], T_d[:, :], writes=['Tt'])
    k.dma('sp', kn0_b[:], kn0_d[0:1, :].to_broadcast([128, 64]), writes=['gains'])
    k.dma('sp', ropec[:], ropec_d.rearrange("(nt p) c -> p nt c", p=128), writes=['rope'])
    for pp in range(2):
        for i in range(2):
            k.op('dve', lambda e: e.memset(nbshs[pp][i][:], 0.0), writes=['nbsh%d_%d' % (i, pp)])
    caus_b = cx.sb("caus_b", [128, 4, 128], BF16)
    wlow_b = cx.sb("wlow_b", [128, 4, 128], BF16)
    for tb_, src_, sk_, tk_ in ((caus_b, caus, 'caus', 'caus_b'), (wlow_b, wlow, 'wlow', 'wlow_b')):
        k.op('dve', lambda e, tb_=tb_, src_=src_: e.tensor_scalar(out=tb_[:], in0=src_[:, :].unsqueeze(1).to_broadcast([128, 4, 128]),
                                                                  scalar1=-NB_NEG, scalar2=NB_NEG, op0=ALU.mult, op1=ALU.add),
             reads=[sk_], writes=[tk_])

    for (xc, xkey, w1, w1key, pe, pekey, w2, w2key, is_k) in [(xk, 'xk', w1k, 'w1k', pek, 'pek', w2k, 'w2k', True),
                                                           (xv, 'xv', w1v, 'w1v', pev, 'pev', w2v, 'w2v', False)]:
        xcv = xc[:].rearrange("p (n s) -> p n s", s=16)
        for i in range(32):
            k.op('pe', lambda e, i=i: e.matmul(po[0][:, 0:1], w1[:, i, :], pe[:, i:i + 1], start=(i == 0), stop=(i == 31)),
                 reads=[w1key, pekey], writes=['po0'], pe_acc=(i > 0))
        k.op('act', lambda e: e.copy(out=cbias[:], in_=po[0][:, 0:1]), reads=['po0'], writes=['cbias'])
        for i in range(32):
            o_, s_ = i // 16, i % 16
            k.op('pe', lambda e, i=i: e.matmul(ps_s[0][:, :], w1[:, i, :], xcv[:, o_:o_ + 512, s_], start=(i == 0), stop=(i == 31)),
                 reads=[w1key, xkey], writes=['ps_s0'], pe_acc=(i > 0))
        k.op('act', lambda e: e.activation(out=xb[:], in_=ps_s[0][:, :], func=AF.Identity, bias=cbias[:, 0:1]),
             reads=['ps_s0', 'cbias'], writes=['xb'])
        k.op('dve', lambda e: e.tensor_tensor(out=gt[:], in0=xb[:], in1=xb[:], op=ALU.mult), reads=['xb'], writes=['gt'])
        k.op('dve', lambda e: e.tensor_scalar(out=gt[:], in0=gt[:], scalar1=0.044715, scalar2=1.0, op0=ALU.mult, op1=ALU.add),
             reads=['gt'], writes=['gt'])
        k.op('dve', lambda e: e.tensor_tensor(out=gt[:], in0=gt[:], in1=xb[:], op=ALU.mult), reads=['gt', 'xb'], writes=['gt'])
        k.op('act', lambda e: e.activation(out=gt[:], in_=gt[:], func=AF.Sigmoid, scale=1.5957691216057308),
             reads=['gt'], writes=['gt'])
        k.op('dve', lambda e: e.tensor_tensor(out=hid[:], in0=gt[:], in1=xb[:], op=ALU.mult), reads=['gt', 'xb'], writes=['hid'])
        for nt in range(4):
            k.op('pe', lambda e: e.matmul(po[1][:, 0:64], hid[:, nt * 128:(nt + 1) * 128], w2[:, :], start=True, stop=True),
                 reads=['hid', w2key], writes=['po1'])
            if is_k:
                k.op('act', lambda e: e.copy(out=kcf[:], in_=po[1][:, 0:64]), reads=['po1'], writes=['kcf'])
                emit_headnorm_rope(cx, kcf, 'kcf', 0, 1, kn0_b[:, :], ropec[:, nt, 0:8], ropec[:, nt, 8:16], hn_tmp, hn_sq)
                k.op('dve', lambda e: e.tensor_copy(kcb[:], kcf[:]), reads=['kcf'], writes=['kcb'])
                k.op('pe', lambda e: e.transpose(pm[0:64, 0:128], kcb[:, :], cx.ident[:]), reads=['kcb', 'ident'], writes=['pm'])
                k.op('act', lambda e: e.copy(out=kcT[:, nt * 128:(nt + 1) * 128], in_=pm[0:64, 0:128]), reads=['pm'], writes=['kcT'])
            else:
                k.op('act', lambda e: e.copy(out=V1c[:, nt, 0:64], in_=po[1][:, 0:64]), reads=['po1'], writes=['V1c'])

    NH = 4
    cur_list = [None]
    cnt = [0]

    def mk_step(lhsT, lkeys, rhs_fn, rkeys, ncol, nh, mask, vtile, vkey, vw, first, last, pre=None, after=None, bsel=2):
        st = {}

        def A():
            i, j = st['i'], st['j']
            if pre is not None:
                pre()
            pe_bias = mask is not None and mask[1] in ('caus', 'wlow')
            k.op('pe', lambda e: e.matmul(ps_s[i][:, 0:ncol], lhsT, rhs_fn(), start=True, stop=(not pe_bias)),
                 reads=lkeys + rkeys, writes=['ps_s%d' % i])
            if pe_bias:
                bt = caus_b if mask[1] == 'caus' else wlow_b
                k.op('pe', lambda e: e.matmul(ps_s[i][:, 0:ncol], cx.ident[:, :], bt[:].rearrange("p r q -> p (r q)")[:, 0:ncol],
                                              start=False, stop=True),
                     reads=['ident', mask[1] + '_b'], writes=['ps_s%d' % i], pe_acc=True)
            k.op('act', lambda e: e.activation(out=pt[j][:, 0:ncol], in_=ps_s[i][:, 0:ncol], func=AF.Exp),
                 reads=['ps_s%d' % i], writes=['pt%d' % j])
            if mask is not None and not pe_bias:
                mk, mkey = mask
                k.op('dve', lambda e: e.tensor_tensor(out=pt[j][:, 0:ncol].rearrange("p (r q) -> p r q", q=128),
                                                      in0=pt[j][:, 0:ncol].rearrange("p (r q) -> p r q", q=128),
                                                      in1=mk.unsqueeze(1).to_broadcast([128, nh, 128]), op=ALU.mult),
                     reads=['pt%d' % j, mkey], writes=['pt%d' % j])

        def B():
            j = st['j']
            for r in range(nh):
                if vw == 193:
                    bank, c0, lead = r // 2, (r % 2) * 193, (r % 2 == 0)
                else:
                    bank, c0, lead = bsel, r * 65, (r == 0)
                st_ = first and lead
                k.op('pe', lambda e, r=r: e.matmul(po[bank][:, c0:c0 + vw], pt[j][:, r * 128:(r + 1) * 128], vtile,
                                                   start=st_, stop=last, skip_group_check=True),
                     reads=['pt%d' % j, vkey], writes=['po%d' % bank], pe_acc=(not st_))
        st.update(A=A, B=B, after=after)
        cur_list[0].append(st)

    def branch_epilogue(ob, obkey, jb, br, bank):
        pv = po[bank][:, 0:260].rearrange("p (r c) -> p r c", c=65)
        pk = 'po%d' % bank
        k.op('dve', lambda e: e.tensor_scalar(out=lr[:, 0:4], in0=pv[:, :, 64], scalar1=1e-30, scalar2=None, op0=ALU.max),
             reads=[pk], writes=['lr'])
        k.op('dve', lambda e: e.reciprocal(out=lr[:, 0:4], in_=lr[:, 0:4]), reads=['lr'], writes=['lr'])
        gv = gat[:, jb, :].rearrange("p (r b) -> p r b", b=3)[:, :, br]
        k.op('dve', lambda e: e.tensor_tensor(out=lr[:, 4:8], in0=lr[:, 0:4], in1=gv, op=ALU.mult), reads=['lr', 'gat'], writes=['lr'])
        k.op('dve', lambda e: e.tensor_tensor(out=otmp[:], in0=pv[:, :, 0:64], in1=lr[:, 4:8].unsqueeze(2).to_broadcast([128, 4, 64]), op=ALU.mult),
             reads=[pk, 'lr'], writes=['otmp'])
        k.op('pool', lambda e: e.tensor_tensor(out=ob[:], in0=ob[:], in1=otmp[:], op=ALU.add), reads=[obkey, 'otmp'], writes=[obkey])

    deferred = {}

    def cmp_after(jb, ob, obkey, Qlo, Qhi, qlk, qhk):
        nbsh = nbshs[jb % 2]

        def f():
            pvs = [po[bk][:, 0:386].rearrange("p (r c) -> p r c", c=193) for bk in range(2)]
            for bk in range(2):
                k.op('dve', lambda e, bk=bk: e.tensor_scalar(out=lr[:, 2 * bk:2 * bk + 2], in0=pvs[bk][:, :, 64], scalar1=1e-30, scalar2=None, op0=ALU.max),
                     reads=['po%d' % bk], writes=['lr'])
            k.op('dve', lambda e: e.reciprocal(out=lr[:, 0:4], in_=lr[:, 0:4]), reads=['lr'], writes=['lr'])
            k.op('dve', lambda e: e.scalar_tensor_tensor(out=imp[:], in0=pvs[0][:, 0, 65:193], scalar=lr[:, 0:1], in1=Tt[:, 126 - 2 * jb:254 - 2 * jb],
                                                         op0=ALU.mult, op1=ALU.add), reads=['po0', 'lr', 'Tt'], writes=['imp'])
            for r in range(1, 4):
                k.op('dve', lambda e, r=r: e.scalar_tensor_tensor(out=imp[:], in0=pvs[r // 2][:, r % 2, 65:193], scalar=lr[:, r:r + 1], in1=imp[:],
                                                                  op0=ALU.mult, op1=ALU.add), reads=['po%d' % (r // 2), 'lr', 'imp'], writes=['imp'])
            gv = gat[:, jb, :].rearrange("p (r b) -> p r b", b=3)[:, :, 0]
            k.op('dve', lambda e: e.tensor_tensor(out=lr[:, 4:8], in0=lr[:, 0:4], in1=gv, op=ALU.mult), reads=['lr', 'gat'], writes=['lr'])
            for bk in range(2):
                k.op('dve', lambda e, bk=bk: e.tensor_tensor(out=ob[:, 2 * bk:2 * bk + 2, :], in0=pvs[bk][:, :, 0:64],
                                                             in1=lr[:, 4 + 2 * bk:6 + 2 * bk].unsqueeze(2).to_broadcast([128, 2, 64]), op=ALU.mult),
                     reads=['po%d' % bk, 'lr'], writes=[obkey])
            k.op('dve', lambda e: e.tensor_scalar(out=imp[:, 0:1], in0=imp[:, 0:1], scalar1=1000.0, scalar2=None, op0=ALU.add),
                 reads=['imp'], writes=['imp'])
            k.op('dve', lambda e: e.max(out=m8[:, 0:8], in_=imp[:]), reads=['imp'], writes=['m8'])
            k.op('dve', lambda e: e.match_replace(out=imp2[:], in_to_replace=m8[:, 0:8], in_values=imp[:], imm_value=-3.0e38),
                 reads=['imp', 'm8'], writes=['imp2'])
            k.op('dve', lambda e: e.max(out=m8[:, 8:16], in_=imp2[:]), reads=['imp2'], writes=['m8'])
            k.op('dve', lambda e: e.tensor_scalar(out=imp2[:], in0=imp[:], scalar1=m8[:, 15:16], scalar2=1.0, op0=ALU.is_ge, op1=ALU.subtract),
                 reads=['imp', 'm8'], writes=['imp2'])
            for i in range(2):
                k.op('dve', lambda e, i=i: e.tensor_scalar(out=nbsh[i][:, 64:128], in0=imp2[:, i * 64:(i + 1) * 64], scalar1=-NB_NEG, scalar2=None, op0=ALU.mult),
                     reads=['imp2'], writes=['nbsh%d_%d' % (i, jb % 2)])

        def f2():
            for i in range(2):
                k.op('pe', lambda e, i=i: e.transpose(pm[:, i * 128:(i + 1) * 128], nbsh[i][:, :], cx.ident[:]),
                     reads=['nbsh%d_%d' % (i, jb % 2), 'ident'], writes=['pm'])
            k.op('dve', lambda e: e.tensor_copy(Qlo[64:128, :].rearrange("p (r q) -> p r q", q=128),
                                                pm[64:128, 0:128].unsqueeze(1).to_broadcast([64, NH, 128])),
                 reads=['pm'], writes=[qlk + 'b'])
            k.op('dve', lambda e: e.tensor_copy(Qhi[64:128, :].rearrange("p (r q) -> p r q", q=128),
                                                pm[64:128, 128:256].unsqueeze(1).to_broadcast([64, NH, 128])),
                 reads=['pm'], writes=[qhk + 'b'])
        deferred[jb] = f2
        if jb == 0:
            def f0(f=f, f2=f2):
                f()
                f2()
            return f0
        return f

    per_q = []
    for jb in range(64):
        lists = {'c': [], 'w': [], 's': []}
        b2 = jb % 2
        b3 = jb % 3
        Qlo, Qhi = Qa[b3]
        qlk, qhk = 'Qa%d_0' % b3, 'Qa%d_1' % b3
        ob, obkey = oacc[b2], 'oacc%d' % b2

        def ldq(j):
            io["load_q"](k, j, Qa[j % 3][0], Qa[j % 3][1], 'Qa%d_0q' % (j % 3), 'Qa%d_1q' % (j % 3))

        def pre(jb=jb):
            if jb == 0:
                ldq(0)
                ldq(1)

        def pre_s(jb=jb):
            if jb + 2 < 64:
                ldq(jb + 2)
        cur_list[0] = lists['c']
        tiles = [nt for nt in range(4) if 128 * nt <= 8 * jb + 6]
        for ii, nt in enumerate(tiles):
            masked = not (128 * nt + 127 <= 8 * jb - 2)
            mask = (mc[:, (8 * jb - 128 * nt) // 8, :], 'mc') if masked else None
            mk_step(kcT[:, nt * 128:(nt + 1) * 128], ['kcT'], (lambda Qlo=Qlo: Qlo[0:64, :]), [qlk + 'q'], 512, 4, mask,
                    V1c[:, nt, :], 'V1c', 193, ii == 0, ii == len(tiles) - 1, pre=(pre if ii == 0 else None),
                    after=(cmp_after(jb, ob, obkey, Qlo, Qhi, qlk, qhk) if ii == len(tiles) - 1 else None))
        cur_list[0] = lists['w']
        kts = list(range(max(0, jb - 4), jb + 1))
        for ii, kt in enumerate(kts):
            mask = None
            if kt == jb:
                mask = (caus[:, :], 'caus')
            elif kt == jb - 4:
                mask = (wlow[:, :], 'wlow')
            mk_step(kwT[:, kt * 128:(kt + 1) * 128], ['kwT'], (lambda Qlo=Qlo: Qlo[0:64, :]), [qlk + 'q'], 512, NH, mask,
                    vw1[:, kt, :], 'vw1', 65, ii == 0, ii == len(kts) - 1,
                    after=((lambda ob=ob, obkey=obkey, jb=jb: branch_epilogue(ob, obkey, jb, 2, 2)) if ii == len(kts) - 1 else None), bsel=2)
        cur_list[0] = lists['s']
        for kt in range(jb + 1):
            Q, qk = (Qlo, qlk) if kt < 32 else (Qhi, qhk)
            mask = (caus[:, :], 'caus') if kt == jb else None

            def fin(ob=ob, obkey=obkey, jb=jb):
                branch_epilogue(ob, obkey, jb, 1, 3)
                if jb + 1 in deferred:
                    deferred[jb + 1]()
                cx.out_toks.append(k.dma('sp', o_d[jb * 128:(jb + 1) * 128, :], ob[:].rearrange("p r d -> p (r d)"), reads=[obkey]))
            mk_step(ks_aug[:, kt * 128:(kt + 1) * 128], ['ks_aug'], (lambda Q=Q: Q[:, :]), [qk + 'q', qk + 'b'], 512, NH, mask,
                    vs1[:, kt, :], 'vs1', 65, kt == 0, kt == jb, pre=(pre_s if kt == 0 else None), after=(fin if kt == jb else None), bsel=3)
        per_q.append(lists)
    steps = per_q[0]['c'] + per_q[0]['w']
    for jb in range(64):
        if jb + 1 < 64:
            steps = steps + per_q[jb + 1]['c']
        steps = steps + per_q[jb]['s']
        if jb + 1 < 64:
            steps = steps + per_q[jb + 1]['w']
    for i, st in enumerate(steps):
        st['i'], st['j'] = i % 2, i % 3
        st['A']()
        if i > 0:
            steps[i - 1]['B']()
            if steps[i - 1]['after'] is not None:
                steps[i - 1]['after']()
    steps[-1]['B']()
    steps[-1]['after']()


def prep_B2(U, g, hh, P, tabs):
    f = np.float32
    heads = [2 * hh, 2 * hh + 1, 2 * (1 - hh), 2 * (1 - hh) + 1]
    q = U[:, 256:768].reshape(64, 128, 2, 4, 64)[:, :, g][:, :, heads]
    qT = np.ascontiguousarray(q.transpose(0, 3, 2, 1)).reshape(64, 64, 512)
    kv = U[:, 768:1536].reshape(S, 6, 2, 64)[:, :, g]
    gat = U[:, 1536:1560].reshape(S, 2, 4, 3)[:, g][:, heads[:2]].reshape(S, 6)
    w1k = P['cmp_k_w1'].reshape(32, 64, 128).transpose(1, 0, 2).reshape(64, 32 * 128)
    w1v = P['cmp_v_w1'].reshape(32, 64, 128).transpose(1, 0, 2).reshape(64, 32 * 128)
    m = {
        "qT": qT.astype(f),
        "kcT": np.ascontiguousarray(kv[:, 0].T), "vcT": np.ascontiguousarray(kv[:, 1].T),
        "ksT": np.ascontiguousarray(kv[:, 2].T), "vs": np.ascontiguousarray(kv[:, 3]),
        "kwT": np.ascontiguousarray(kv[:, 4].T), "vw": np.ascontiguousarray(kv[:, 5]),
        "gat": np.ascontiguousarray(gat),
        "w1k": np.ascontiguousarray(w1k), "w1v": np.ascontiguousarray(w1v),
        "pek": np.ascontiguousarray(P['cmp_pe'][0].T), "pev": np.ascontiguousarray(P['cmp_pe'][1].T),
        "w2k": np.ascontiguousarray(P['cmp_k_w2']), "w2v": np.ascontiguousarray(P['cmp_v_w2']),
        "kn0": np.ascontiguousarray(P['k_norm'][0][None, :]),
    }
    m.update(tabs)
    return {k_: np.ascontiguousarray(v, dtype=f) for k_, v in m.items()}


LC = 512
TWO_PI_LO = 6.283185


def emit_sin_turns(cx, out, r, rkey, okey, W, wkey):
    k = cx.k
    ri, rf, rg = W['i'], W['f'], W['g']
    k.op('dve', lambda e: e.tensor_copy(ri, r), reads=[rkey], writes=[wkey])
    k.op('dve', lambda e: e.tensor_copy(rf, ri), reads=[wkey], writes=[wkey])
    k.op('dve', lambda e: e.tensor_tensor(out=rf, in0=r, in1=rf, op=ALU.subtract), reads=[rkey, wkey], writes=[wkey])
    k.op('dve', lambda e: e.tensor_scalar(out=rg, in0=rf, scalar1=0.5, scalar2=None, op0=ALU.is_gt), reads=[wkey], writes=[wkey])
    k.op('dve', lambda e: e.tensor_tensor(out=rf, in0=rf, in1=rg, op=ALU.subtract), reads=[wkey], writes=[wkey])
    k.op('dve', lambda e: e.tensor_scalar(out=rg, in0=rf, scalar1=-0.5, scalar2=None, op0=ALU.is_lt), reads=[wkey], writes=[wkey])
    k.op('dve', lambda e: e.tensor_tensor(out=rf, in0=rf, in1=rg, op=ALU.add), reads=[wkey], writes=[wkey])
    k.op('act', lambda e: e.activation(out=out, in_=rf, func=AF.Sin, scale=TWO_PI_LO), reads=[wkey], writes=[okey])


def stage_B1(cx, io, nq=4, shared_pT=None, co=False, alias=None):
    k = cx.k
    uT_d, lre_d, lim_d, ldt_d, bre_d, bim_d, cre_d, cim_d, dsk_d, y_d = [io[n] for n in (
        "uT", "lre", "lim", "ldt", "bre", "bim", "cre", "cim", "dsk", "yT")]
    NTL = 2 * nq
    emit_consts(cx)
    NJ = LC + 1
    lre = cx.sb("lre", [128, NTL]); lim = cx.sb("lim", [128, NTL]); ldt = cx.sb("ldt", [128, NTL])
    bre = cx.sb("bre", [128, NTL, 16]); bim = cx.sb("bim", [128, NTL, 16])
    cre = cx.sb("cre", [128, NTL, 16]); cim = cx.sb("cim", [128, NTL, 16])
    dsk = cx.sb("dsk", [32, NTL])
    sc = cx.sb("sc", [128, 12 * NTL])
    sc2 = cx.sb("sc2", [128, 2 * NTL])
    C_ = lambda base, t: sc[:, base * NTL + t:base * NTL + t + 1]
    CA = lambda base: sc[:, base * NTL:(base + 1) * NTL]
    cosT = [cx.sb("cosT%d" % t, [128, NJ]) for t in range(NTL)]
    sinT = [cx.sb("sinT%d" % t, [128, NJ]) for t in range(NTL)]
    if alias is None:
        iota_i = cx.sb("iota_i", [128, NJ], I32)
        rr = cx.sb("rr", [128, NJ])
        Wt = {'i': cx.sb("w_i", [128, NJ], I32)[:], 'f': cx.sb("w_f", [128, NJ])[:], 'g': cx.sb("w_g", [128, NJ])[:]}
    else:
        iota_i, rr = alias[0].bitcast(I32), alias[1]
        Wt = {'i': alias[2].bitcast(I32), 'f': alias[3], 'g': alias[4]}
    Bbd = [cx.sb("Bbd%d" % c, [128, 32], BF16) for c in range(2)]
    BbT = [[cx.sb("BbT%d_%d" % (t, c), [32, 128], BF16) for c in range(2)] for t in range(NTL)]
    Cbd = [[cx.sb("Cbd%d_%d" % (t, c), [128, 32], BF16) for c in range(3)] for t in range(NTL)]
    tmpB = cx.sb("tmpB", [128, 4, 16])
    NS = 3
    NU = 3 if co else 4
    uf = [cx.sb("uf%d" % i, [32, LC]) for i in range(NU)]
    ub = [cx.sb("ub%d" % i, [32, LC], BF16) for i in range(2)]
    WK = [{n: cx.sb("%s_%d" % (n, i), [128, LC], (BF16 if n[0] == 'z' else F32))
           for n in ('t1', 't2', 't3', 't4', 'wre', 'wim', 'vre', 'vim', 'z1', 'z2', 'z3', 'z4')} for i in range(NS)]
    init = [cx.sb("init%d" % t, [128, 2]) for t in range(NTL)]
    yo = [cx.sb("yo%d" % i, [32, LC]) for i in range(2)]
    nps = 1 if co else 2
    pA = [cx.ps("pA%d" % i, [128, 512]) for i in range(nps)] * (2 // nps)
    pB = [cx.ps("pB%d" % i, [128, 512]) for i in range(nps)] * (2 // nps)
    pC = [cx.ps("pC%d" % i, [128, 512]) for i in range(nps)] * (2 // nps)
    if shared_pT is None:
        pm = cx.ps("pm", [128, 1024], BF16)
        pmk = 'pm'
    else:
        pm = shared_pT[:].rearrange("p c t -> p (c t)")
        pmk = '@pT'

    for a_, d_, nm in [(lre, lre_d, 'lre'), (lim, lim_d, 'lim'), (ldt, ldt_d, 'ldt')]:
        k.dma('sp', a_[:].rearrange("p (q t) -> p q t", t=2), d_.rearrange("(q p) t -> p q t", p=128), writes=[nm])
    k.dma('sp', dsk[:].rearrange("p (q t) -> p q t", t=2), dsk_d.rearrange("(q p) t -> p q t", p=32), writes=['dsk'])
    for a_, d_, nm in [(bre, bre_d, 'bre'), (bim, bim_d, 'bim'), (cre, cre_d, 'cre'), (cim, cim_d, 'cim')]:
        k.dma('sp', a_[:].rearrange("p (q t) h -> p q (t h)", t=2), d_.rearrange("(q p) c -> p q c", p=128), writes=[nm])
    k.op('act', lambda e: e.activation(out=CA(0), in_=ldt[:], func=AF.Exp), reads=['ldt'], writes=['sc'])
    k.op('dve', lambda e: e.tensor_tensor(out=CA(1), in0=lre[:], in1=CA(0), op=ALU.mult), reads=['lre', 'sc'], writes=['sc'])
    k.op('dve', lambda e: e.tensor_tensor(out=CA(2), in0=lim[:], in1=CA(0), op=ALU.mult), reads=['lim', 'sc'], writes=['sc'])
    k.op('act', lambda e: e.activation(out=CA(3), in_=CA(1), func=AF.Exp), reads=['sc'], writes=['sc'])
    k.op('dve', lambda e: e.tensor_scalar(out=CA(4), in0=CA(2), scalar1=1.0 / (2 * math.pi), scalar2=None, op0=ALU.mult),
         reads=['sc'], writes=['sc'])
    k.op('pool', lambda e: e.iota(iota_i[:], pattern=[[1, NJ]], base=0, channel_multiplier=0), writes=['iota_i'])
    for t in range(NTL):
        k.op('dve', lambda e: e.tensor_copy(rr[:], iota_i[:]), reads=['iota_i'], writes=['rr'])
        k.op('dve', lambda e: e.tensor_scalar(out=rr[:], in0=rr[:], scalar1=C_(4, t), scalar2=None, op0=ALU.mult),
             reads=['rr', 'sc'], writes=['rr'])
        emit_sin_turns(cx, sinT[t][:], rr[:], 'rr', 'sinT%d' % t, Wt, 'wt')
        k.op('dve', lambda e: e.tensor_scalar(out=rr[:], in0=rr[:], scalar1=0.25, scalar2=None, op0=ALU.add), reads=['rr'], writes=['rr'])
        emit_sin_turns(cx, cosT[t][:], rr[:], 'rr', 'cosT%d' % t, Wt, 'wt')
    for t in range(NTL):
        c1, s1 = cosT[t][:, 1:2], sinT[t][:, 1:2]
        rho = C_(3, t)
        nr, ni, den, cr_, ci_, ta, tb = C_(5, t), C_(6, t), C_(7, t), C_(8, t), C_(9, t), C_(10, t), C_(11, t)
        lr_, li_ = lre[:, t:t + 1], lim[:, t:t + 1]
        ck, sk = 'cosT%d' % t, 'sinT%d' % t
        k.op('dve', lambda e: e.tensor_tensor(out=nr, in0=rho, in1=c1, op=ALU.mult), reads=['sc', ck], writes=['sc'])
        k.op('dve', lambda e: e.tensor_scalar(out=nr, in0=nr, scalar1=-1.0, scalar2=None, op0=ALU.add), reads=['sc'], writes=['sc'])
        k.op('dve', lambda e: e.tensor_tensor(out=ni, in0=rho, in1=s1, op=ALU.mult), reads=['sc', sk], writes=['sc'])
        k.op('dve', lambda e: e.tensor_tensor(out=ta, in0=lr_, in1=lr_, op=ALU.mult), reads=['lre'], writes=['sc'])
        k.op('dve', lambda e: e.scalar_tensor_tensor(out=den, in0=li_, scalar=li_, in1=ta, op0=ALU.mult, op1=ALU.add),
             reads=['lim', 'sc'], writes=['sc'])
        k.op('dve', lambda e: e.reciprocal(out=den, in_=den), reads=['sc'], writes=['sc'])
        k.op('dve', lambda e: e.tensor_tensor(out=ta, in0=nr, in1=lr_, op=ALU.mult), reads=['sc', 'lre'], writes=['sc'])
        k.op('dve', lambda e: e.scalar_tensor_tensor(out=ta, in0=ni, scalar=li_, in1=ta, op0=ALU.mult, op1=ALU.add),
             reads=['sc', 'lim'], writes=['sc'])
        k.op('dve', lambda e: e.tensor_tensor(out=cr_, in0=ta, in1=den, op=ALU.mult), reads=['sc'], writes=['sc'])
        k.op('dve', lambda e: e.tensor_tensor(out=ta, in0=ni, in1=lr_, op=ALU.mult), reads=['sc', 'lre'], writes=['sc'])
        k.op('dve', lambda e: e.tensor_tensor(out=tb, in0=nr, in1=li_, op=ALU.mult), reads=['sc', 'lim'], writes=['sc'])
        k.op('dve', lambda e: e.tensor_tensor(out=ta, in0=ta, in1=tb, op=ALU.subtract), reads=['sc'], writes=['sc'])
        k.op('dve', lambda e: e.tensor_tensor(out=ci_, in0=ta, in1=den, op=ALU.mult), reads=['sc'], writes=['sc'])
        q0, q1, q2, q3 = tmpB[:, 0, :], tmpB[:, 1, :], tmpB[:, 2, :], tmpB[:, 3, :]
        k.op('dve', lambda e: e.tensor_scalar(out=q0, in0=bre[:, t, :], scalar1=cr_, scalar2=None, op0=ALU.mult), reads=['bre', 'sc'], writes=['tmpB'])
        k.op('dve', lambda e: e.tensor_scalar(out=q1, in0=bim[:, t, :], scalar1=ci_, scalar2=None, op0=ALU.mult), reads=['bim', 'sc'], writes=['tmpB'])
        k.op('dve', lambda e: e.tensor_scalar(out=q2, in0=bre[:, t, :], scalar1=ci_, scalar2=None, op0=ALU.mult), reads=['bre', 'sc'], writes=['tmpB'])
        k.op('dve', lambda e: e.tensor_scalar(out=q3, in0=bim[:, t, :], scalar1=cr_, scalar2=None, op0=ALU.mult), reads=['bim', 'sc'], writes=['tmpB'])
        for c in range(2):
            k.op('dve', lambda e, c=c: e.memset(Bbd[c][:], 0.0), writes=['Bbd%d' % c])
        for c in range(3):
            k.op('dve', lambda e, c=c: e.memset(Cbd[t][c][:], 0.0), writes=['Cbd%d_%d' % (t, c)])
        for gi in range(2):
            rs = slice(gi * 64, (gi + 1) * 64)
            cs = slice(gi * 16, (gi + 1) * 16)
            k.op('dve', lambda e: e.tensor_tensor(out=Bbd[0][rs, cs], in0=tmpB[rs, 0, :], in1=tmpB[rs, 1, :], op=ALU.subtract),
                 reads=['tmpB', 'Bbd0'], writes=['Bbd0'])
            k.op('dve', lambda e: e.tensor_tensor(out=Bbd[1][rs, cs], in0=tmpB[rs, 2, :], in1=tmpB[rs, 3, :], op=ALU.add),
                 reads=['tmpB', 'Bbd1'], writes=['Bbd1'])
            k.op('dve', lambda e: e.tensor_copy(Cbd[t][0][rs, cs], cre[rs, t, :]), reads=['cre', 'Cbd%d_0' % t], writes=['Cbd%d_0' % t])
            k.op('dve', lambda e: e.tensor_scalar(out=Cbd[t][1][rs, cs], in0=cim[rs, t, :], scalar1=-1.0, scalar2=None, op0=ALU.mult),
                 reads=['cim', 'Cbd%d_1' % t], writes=['Cbd%d_1' % t])
            k.op('dve', lambda e: e.tensor_scalar(out=Cbd[t][2][rs, cs], in0=cre[rs, t, :], scalar1=-1.0, scalar2=None, op0=ALU.mult),
                 reads=['cre', 'Cbd%d_2' % t], writes=['Cbd%d_2' % t])
        for c in range(2):
            k.op('pe', lambda e, c=c: e.transpose(pm[0:32, c * 128:(c + 1) * 128], Bbd[c][:, :], cx.ident[:]),
                 reads=['Bbd%d' % c, 'ident'], writes=[pmk])
            k.op('act', lambda e, c=c: e.copy(out=BbT[t][c][:], in_=pm[0:32, c * 128:(c + 1) * 128]), reads=[pmk], writes=['BbT%d_%d' % (t, c)])
        k.op('dve', lambda e: e.memset(init[t][:], 0.0), writes=['init%d' % t])

    nchunk = S // LC
    items = [(ch, t) for ch in range(nchunk) for t in range(NTL)]

    def names(it):
        ch, t = items[it]
        b = it % NS
        pb2 = (it % 2) if not co else 0
        W_ = WK[b]
        return ch, t, b, pb2, W_, (lambda n: '%s_%d' % (n, b)), 'cosT%d' % t, 'sinT%d' % t

    def P1(it):
        ch, t, b, pb2, W_, wk, ck, sk = names(it)
        t1, t2, t3, t4, wre, wim = [W_[n] for n in ('t1', 't2', 't3', 't4', 'wre', 'wim')]
        cs_, sn_ = cosT[t][:, 0:LC], sinT[t][:, 0:LC]
        pa, pb_ = pA[pb2], pB[pb2]
        pak, pbk = 'pA%d' % pb2, 'pB%d' % pb2
        k.dma('sp', uf[it % NU][:], uT_d[t * 32:(t + 1) * 32, ch * LC:(ch + 1) * LC], reads=io.get('chunk_keys', lambda c: [])(ch), writes=['uf%d' % (it % NU)])
        k.op('act', lambda e: e.copy(out=ub[it % 2][:], in_=uf[it % NU][:]), reads=['uf%d' % (it % NU)], writes=['ub%d' % (it % 2)])
        k.op('pe', lambda e: e.matmul(pa[:, :], BbT[t][0][:, :], ub[it % 2][:, :], start=True, stop=True),
             reads=['BbT%d_0' % t, 'ub%d' % (it % 2)], writes=[pak])
        k.op('pe', lambda e: e.matmul(pb_[:, :], BbT[t][1][:, :], ub[it % 2][:, :], start=True, stop=True),
             reads=['BbT%d_1' % t, 'ub%d' % (it % 2)], writes=[pbk])
        k.op('dve', lambda e: e.tensor_tensor(out=t1[:], in0=cs_, in1=pa[:, :], op=ALU.mult), reads=[ck, pak], writes=[wk('t1')])
        k.op('dve', lambda e: e.tensor_tensor(out=t2[:], in0=sn_, in1=pb_[:, :], op=ALU.mult), reads=[sk, pbk], writes=[wk('t2')])
        k.op('dve', lambda e: e.tensor_tensor(out=t3[:], in0=cs_, in1=pb_[:, :], op=ALU.mult), reads=[ck, pbk], writes=[wk('t3')])
        k.op('dve', lambda e: e.tensor_tensor(out=t4[:], in0=sn_, in1=pa[:, :], op=ALU.mult), reads=[sk, pak], writes=[wk('t4')])
        k.op('pool', lambda e: e.tensor_tensor(out=wre[:], in0=t1[:], in1=t2[:], op=ALU.add), reads=[wk('t1'), wk('t2')], writes=[wk('wre')])
        k.op('pool', lambda e: e.tensor_tensor(out=wim[:], in0=t3[:], in1=t4[:], op=ALU.subtract), reads=[wk('t3'), wk('t4')], writes=[wk('wim')])

    def P2(it):
        ch, t, b, pb2, W_, wk, ck, sk = names(it)
        wre, wim, vre, vim = [W_[n] for n in ('wre', 'wim', 'vre', 'vim')]
        rho_b = C_(3, t).to_broadcast([128, LC])
        k.op('dve', lambda e: e.tensor_tensor_scan(out=vre[:], data0=rho_b, data1=wre[:], initial=init[t][:, 0:1], op0=ALU.mult, op1=ALU.add),
             reads=['sc', wk('wre'), 'init%d' % t], writes=[wk('vre')])
        k.op('dve', lambda e: e.tensor_tensor_scan(out=vim[:], data0=rho_b, data1=wim[:], initial=init[t][:, 1:2], op0=ALU.mult, op1=ALU.add),
             reads=['sc', wk('wim'), 'init%d' % t], writes=[wk('vim')])
        cL, sL = cosT[t][:, LC:LC + 1], sinT[t][:, LC:LC + 1]
        ta, tb = sc2[:, 2 * t:2 * t + 1], sc2[:, 2 * t + 1:2 * t + 2]
        k.op('dve', lambda e: e.tensor_tensor(out=ta, in0=vim[:, LC - 1:LC], in1=sL, op=ALU.mult), reads=[wk('vim'), sk], writes=['sc2_%d' % t])
        k.op('dve', lambda e: e.tensor_tensor(out=tb, in0=vim[:, LC - 1:LC], in1=cL, op=ALU.mult), reads=[wk('vim'), ck], writes=['sc2_%d' % t])
        k.op('dve', lambda e: e.scalar_tensor_tensor(out=init[t][:, 0:1], in0=vre[:, LC - 1:LC], scalar=cL, in1=ta, op0=ALU.mult, op1=ALU.subtract),
             reads=[wk('vre'), ck, 'sc2_%d' % t], writes=['init%d' % t])
        k.op('dve', lambda e: e.scalar_tensor_tensor(out=init[t][:, 1:2], in0=vre[:, LC - 1:LC], scalar=sL, in1=tb, op0=ALU.mult, op1=ALU.add),
             reads=[wk('vre'), sk, 'sc2_%d' % t], writes=['init%d' % t])

    def P3(it):
        ch, t, b, pb2, W_, wk, ck, sk = names(it)
        vre, vim, z1, z2, z3, z4 = [W_[n] for n in ('vre', 'vim', 'z1', 'z2', 'z3', 'z4')]
        cs_, sn_ = cosT[t][:, 0:LC], sinT[t][:, 0:LC]
        pc, pck = pC[pb2], 'pC%d' % pb2
        k.op('pool', lambda e: e.tensor_tensor(out=z1[:], in0=cs_, in1=vre[:], op=ALU.mult), reads=[ck, wk('vre')], writes=[wk('z1')])
        k.op('pool', lambda e: e.tensor_tensor(out=z2[:], in0=sn_, in1=vim[:], op=ALU.mult), reads=[sk, wk('vim')], writes=[wk('z2')])
        k.op('pool', lambda e: e.tensor_tensor(out=z3[:], in0=sn_, in1=vre[:], op=ALU.mult), reads=[sk, wk('vre')], writes=[wk('z3')])
        k.op('dve', lambda e: e.tensor_tensor(out=z4[:], in0=cs_, in1=vim[:], op=ALU.mult), reads=[ck, wk('vim')], writes=[wk('z4')])
        for ii, (ci, z, zk) in enumerate([(0, z1, 'z1'), (2, z2, 'z2'), (1, z3, 'z3'), (1, z4, 'z4')]):
            k.op('pe', lambda e, ci=ci, z=z, ii=ii: e.matmul(pc[0:32, :], Cbd[t][ci][:, :], z[:, :], start=(ii == 0), stop=(ii == 3)),
                 reads=['Cbd%d_%d' % (t, ci), wk(zk)], writes=[pck], pe_acc=(ii > 0))

    def P4(it):
        ch, t, b, pb2, W_, wk, ck, sk = names(it)
        pc, pck = pC[pb2], 'pC%d' % pb2
        k.op('dve', lambda e: e.scalar_tensor_tensor(out=yo[it % 2][:], in0=uf[it % NU][:], scalar=dsk[:, t:t + 1], in1=pc[0:32, :], op0=ALU.mult, op1=ALU.add),
             reads=['uf%d' % (it % NU), 'dsk', pck], writes=['yo%d' % (it % 2)])
        cx.out_toks.append(k.dma('sp', y_d[t * 32:(t + 1) * 32, ch * LC:(ch + 1) * LC], yo[it % 2][:], reads=['yo%d' % (it % 2)]))

    N = len(items)
    pos = [0]

    def step(i):
        if i < N:
            P1(i)
        if 0 <= i - 1 < N:
            P2(i - 1)
        if 0 <= i - 2 < N:
            P3(i - 2)
            if co:
                P4(i - 2)
        if not co and 0 <= i - 3 < N:
            P4(i - 3)

    def adv(n_chunks_ready, max_items=10 ** 9):
        n = 0
        while pos[0] < N and items[pos[0]][0] < n_chunks_ready and n < max_items:
            step(pos[0])
            pos[0] += 1
            n += 1

    def fin():
        adv(nchunk)
        step(N)
        step(N + 1)
        step(N + 2)

    if co:
        return adv, fin
    fin()


def prep_B1(u_s5, kq, P):
    f = np.float32
    gs = slice(4 * kq, 4 * kq + 4)
    def pg(a):
        return np.ascontiguousarray(a[gs].reshape(2, 128).T)
    def pgh(a):
        return np.ascontiguousarray(a[gs].reshape(2, 128, 16).transpose(1, 0, 2).reshape(128, 32))
    m = {
        "uT": np.ascontiguousarray(u_s5[:, 64 * kq:64 * kq + 64].T),
        "lre": pg(P['s5_lam_re']), "lim": pg(P['s5_lam_im']),
        "ldt": pg(np.repeat(P['s5_log_dt'][:, None], 64, axis=1)),
        "bre": pgh(P['s5_b_re']), "bim": pgh(P['s5_b_im']),
        "cre": pgh(P['s5_c_re'].transpose(0, 2, 1)), "cim": pgh(P['s5_c_im'].transpose(0, 2, 1)),
        "dsk": np.ascontiguousarray(P['s5_d'][gs].reshape(2, 32).T),
    }
    return {k_: np.ascontiguousarray(v, dtype=f) for k_, v in m.items()}


HALO = 16


def emit_rms_rows(cx, out, okey, x, xkey, W_, gain, gkey, junk, sfx=''):
    k = cx.k
    ss, rstd = cx.work['ss' + sfx], cx.work['rstd' + sfx]
    k.op('dve', lambda e: e.scalar_tensor_tensor(out=junk, in0=x, scalar=1.0, in1=x, op0=ALU.mult, op1=ALU.mult, accum_out=ss[:, 0:1]),
         reads=[xkey], writes=['rjunk' + sfx, 'ss' + sfx])
    yield
    k.op('dve', lambda e: e.tensor_scalar(out=ss[:, 0:1], in0=ss[:, 0:1], scalar1=1.0 / W_, scalar2=EPS, op0=ALU.mult, op1=ALU.add),
         reads=['ss' + sfx], writes=['ss' + sfx])
    yield
    k.op('pool', lambda e: e.tensor_tensor(out=rstd[:, 0:1], in0=ss[:, 0:1], in1=cx.work['mhalf'][:, 0:1], op=ALU.pow),
         reads=['ss' + sfx, 'mhalf'], writes=['rstd' + sfx])
    yield
    k.op('dve', lambda e: e.scalar_tensor_tensor(out=out, in0=x, scalar=rstd[:, 0:1], in1=gain, op0=ALU.mult, op1=ALU.mult),
         reads=[xkey, 'rstd' + sfx, gkey], writes=[okey])
    yield


def stage_O(cx, io, ntok):
    k = cx.k
    (x_d, upT_d, corr_d, pwbd_d, prow_d, onorm_d, nsa_d, ysT_d, gluw_d, wo_d, cT_d, adaw_d, adab_d, xo_d) = [io[n] for n in (
        "x", "upT", "corr", "pwbd", "prow", "onorm", "nsa", "ysT", "gluw", "wo", "cT", "adaw", "adab", "xo")]
    CH = 2048
    emit_consts(cx)
    emit_work(cx)
    NP = HALO + CH
    v = cx.sb("v", [128, 2, NP])
    sA = cx.sb("sA", [128, NP])
    sB = cx.sb("sB", [128, NP])
    pooled = cx.sb("pooled", [128, 2, CH], BF16)
    corr = cx.sb("corr", [128, 2, HALO])
    pwbd = cx.sb("pwbd", [128, 2, 128], BF16)
    prow = cx.sb("prow", [128, 4, 256])
    onorm = cx.sb("onorm", [128, D])
    gluw = cx.sb("gluw", [128, 2, 256], BF16)
    Wo = cx.sb("Wo", [128, 8, D], BF16)
    xt = [cx.sb("xt%d" % i, [128, D]) for i in range(4)]
    nsa = [cx.sb("nsa%d" % i, [128, 512]) for i in range(2)]
    ysT = [cx.sb("ysT%d" % i, [128, 2, 128]) for i in range(2)]
    ys_ = [cx.sb("ys%d" % i, [128, 256]) for i in range(2)]
    ycats = [cx.sb("ycat%d" % i, [128, D], BF16) for i in range(4)]
    rjunk = [cx.sb("rjunk%d" % i, [128, 512]) for i in range(2)]
    yp = [cx.sb("yp%d" % i, [128, 256]) for i in range(2)]
    yg = [cx.sb("yg%d" % i, [128, 256]) for i in range(2)]
    gt = [cx.sb("gt%d" % i, [128, 256]) for i in range(2)]
    ygb = [cx.sb("ygb%d" % i, [128, 256], BF16) for i in range(2)]
    ygT = [cx.sb("ygT%d" % i, [128, 2, 128], BF16) for i in range(2)]
    for i in range(2):
        cx.work['ss%d' % i] = cx.sb("ss_%d" % i, [128, 1])
        cx.work['rstd%d' % i] = cx.sb("rstd_%d" % i, [128, 1])
    ycT = cx.sb("ycT", [128, 8, 128], BF16)
    xo = [cx.sb("xo%d" % i, [128, D]) for i in range(2)]
    pT = cx.ps("pT", [128, 8, 128], BF16)
    pqs = [cx.ps("pq%d" % i, [128, 512]) for i in range(2)]
    pz = cx.ps("pz", [128, 512])
    py = [cx.ps("py%d" % i, [128, 512]) for i in range(2)]

    k.dma('sp', corr[:].rearrange("p t h -> p (t h)"), corr_d[:, :], writes=['corr'])
    k.dma('pool', pwbd[:].rearrange("p t d -> p (t d)"), pwbd_d[:, :], writes=['pwbd'])
    k.dma('sp', prow[:].rearrange("p a b -> p (a b)"), prow_d[0:1, :].to_broadcast([128, 1024]), writes=['prow'])
    k.dma('sp', onorm[:], onorm_d[0:1, :].to_broadcast([128, D]), writes=['onorm'])
    k.dma('pool', gluw[:], gluw_d.rearrange("(c p) n -> p c n", p=128), writes=['gluw'])
    for kc in range(8):
        k.dma('pool', Wo[:, kc, :], wo_d[kc * 128:(kc + 1) * 128, :], writes=['Wo'])
    mod_b = emit_mod(cx, cT_d, adaw_d, adab_d, 1, ['py0', 'py1'], py)
    g2_b = mod_b[:, 0, :]
    TT = ALU.add
    for ch in range(ntok // CH):
        for t in range(2):
            if ch == 0:
                k.op('dve', lambda e, t=t: e.memset(v[:, t, 0:HALO], 0.0), writes=['v'])
                k.dma('sp', v[:, t, HALO:NP], upT_d[t * 128:(t + 1) * 128, 0:CH], writes=['v'])
            else:
                k.dma('sp', v[:, t, :], upT_d[t * 128:(t + 1) * 128, ch * CH - HALO:(ch + 1) * CH], writes=['v'])
        for t in range(2):
            vt = v[:, t, :]
            k.op('dve', lambda e: e.tensor_tensor(out=sA[:, 1:NP], in0=vt[:, 1:NP], in1=vt[:, 0:NP - 1], op=TT), reads=['v'], writes=['sA'])
            if t == 0:
                k.op('dve', lambda e: e.tensor_tensor(out=sB[64:128, 3:NP], in0=sA[64:128, 3:NP], in1=sA[64:128, 1:NP - 2], op=TT), reads=['sA'], writes=['sB'])
                srcs = [(sA, 'sA', 0.5), (sB, 'sB', 0.25)]
            else:
                k.op('dve', lambda e: e.tensor_tensor(out=sB[:, 3:NP], in0=sA[:, 3:NP], in1=sA[:, 1:NP - 2], op=TT), reads=['sA'], writes=['sB'])
                k.op('dve', lambda e: e.tensor_tensor(out=sA[:, 7:NP], in0=sB[:, 7:NP], in1=sB[:, 3:NP - 4], op=TT), reads=['sB', 'sA'], writes=['sA'])
                k.op('dve', lambda e: e.tensor_tensor(out=sB[64:128, 15:NP], in0=sA[64:128, 15:NP], in1=sA[64:128, 7:NP - 8], op=TT),
                     reads=['sA', 'sB'], writes=['sB'])
                srcs = [(sA, 'sA', 0.125), (sB, 'sB', 0.0625)]
            for gi, (src, skey, iw) in enumerate(srcs):
                rs = slice(gi * 64, (gi + 1) * 64)
                if ch == 0:
                    k.op('dve', lambda e: e.tensor_tensor(out=src[rs, HALO:2 * HALO], in0=src[rs, HALO:2 * HALO], in1=corr[rs, t, :], op=ALU.mult),
                         reads=[skey, 'corr'], writes=[skey])
                k.op('dve', lambda e: e.scalar_tensor_tensor(out=pooled[rs, t, :], in0=src[rs, HALO:NP], scalar=iw, in1=vt[rs, HALO:NP],
                                                             op0=ALU.mult, op1=ALU.subtract), reads=[skey, 'v'], writes=['pooled'])
        def genA(tl, ch=ch):
            ti = ch * (CH // 128) + tl
            p2, p4 = ti % 2, ti % 4
            sf = str(p2)
            tsl = slice(ti * 128, (ti + 1) * 128)
            lsl = slice(tl * 128, (tl + 1) * 128)
            yc, yck = ycats[p4], 'ycat%d' % p4
            k.dma('act', xt[p4][:], x_d[tsl, :], writes=['xt%d' % p4])
            k.dma('act', nsa[p2][:], nsa_d[tsl, :], writes=['nsa%d' % p2])
            k.dma('act', ysT[p2][:], ysT_d[:, tsl].rearrange("(c p) t -> p c t", p=128), writes=['ysT%d' % p2])
            pqt, pqk = pqs[p2], 'pq%d' % p2
            for t in range(2):
                k.op('pe', lambda e, t=t: e.matmul(pqt[:, t * 128:(t + 1) * 128], pooled[:, t, lsl], pwbd[:, t, :], start=True, stop=True),
                     reads=['pooled', 'pwbd'], writes=[pqk])
            yield
            k.op('dve', lambda e: e.tensor_tensor(out=yp[p2][:], in0=pqt[:, 0:256], in1=prow[:, 0, :], op=ALU.add), reads=[pqk, 'prow'], writes=['yp' + sf])
            yield
            k.op('pool', lambda e: e.tensor_tensor(out=yp[p2][:], in0=yp[p2][:], in1=prow[:, 1, :], op=ALU.mult), reads=['yp' + sf, 'prow'], writes=['yp' + sf])
            yield
            yield from emit_rms_rows(cx, yc[:, 0:256], yck, yp[p2][:], 'yp' + sf, 256, onorm[:, 0:256], 'onorm', rjunk[p2][:, 0:256], sf)
            yield from emit_rms_rows(cx, yc[:, 256:768], yck, nsa[p2][:], 'nsa%d' % p2, 512, onorm[:, 256:768], 'onorm', rjunk[p2][:, 0:512], sf)
            for c in range(2):
                k.op('pe', lambda e, c=c: e.transpose(pz[:, c * 128:(c + 1) * 128], ysT[p2][:, c, :], cx.ident_f[:]),
                     reads=['ysT%d' % p2, 'ident_f'], writes=['pz'])
            y0, y0k = ys_[p2], 'ys' + sf
            g_, gk = gt[p2], 'gt' + sf
            yg_, ygk = yg[p2], 'yg' + sf
            k.op('act', lambda e: e.copy(out=y0[:], in_=pz[:, 0:256]), reads=['pz'], writes=[y0k])
            yield
            k.op('dve', lambda e: e.tensor_tensor(out=g_[:], in0=y0[:], in1=y0[:], op=ALU.mult), reads=[y0k], writes=[gk])
            yield
            k.op('dve', lambda e: e.tensor_scalar(out=g_[:], in0=g_[:], scalar1=0.044715, scalar2=1.0, op0=ALU.mult, op1=ALU.add), reads=[gk], writes=[gk])
            yield
            k.op('pool', lambda e: e.tensor_tensor(out=g_[:], in0=g_[:], in1=y0[:], op=ALU.mult), reads=[gk, y0k], writes=[gk])
            yield
            k.op('act', lambda e: e.activation(out=g_[:], in_=g_[:], func=AF.Sigmoid, scale=1.5957691216057308), reads=[gk], writes=[gk])
            yield
            k.op('dve', lambda e: e.tensor_tensor(out=yg_[:], in0=g_[:], in1=y0[:], op=ALU.mult), reads=[gk, y0k], writes=[ygk])
            yield
            k.op('pool', lambda e: e.tensor_copy(ygb[p2][:], yg_[:]), reads=[ygk], writes=['ygb' + sf])
            yield
            for c in range(2):
                k.op('pe', lambda e, c=c: e.transpose(pT[:, c, :], ygb[p2][:, c * 128:(c + 1) * 128], cx.ident[:]), reads=['ygb' + sf, 'ident'], writes=['pT'], pe_acc=(c > 0))
            k.op('act', lambda e: e.copy(out=ygT[p2][:], in_=pT[:, 0:2, :]), reads=['pT'], writes=['ygT' + sf])
            yield
            for c in range(2):
                k.op('pe', lambda e, c=c: e.matmul(pqt[:, 256:512], ygT[p2][:, c, :], gluw[:, c, :], start=(c == 0), stop=(c == 1)),
                     reads=['ygT' + sf, 'gluw'], writes=[pqk], pe_acc=(c > 0))
            yield
            k.op('dve', lambda e: e.tensor_tensor(out=g_[:], in0=pqt[:, 256:512], in1=prow[:, 2, :], op=ALU.add), reads=[pqk, 'prow'], writes=[gk])
            yield
            k.op('act', lambda e: e.activation(out=g_[:], in_=g_[:], func=AF.Sigmoid), reads=[gk], writes=[gk])
            yield
            k.op('dve', lambda e: e.tensor_tensor(out=yg_[:], in0=yg_[:], in1=g_[:], op=ALU.mult), reads=[ygk, gk], writes=[ygk])
            yield
            yield from emit_rms_rows(cx, yc[:, 768:1024], yck, yg_[:], ygk, 256, onorm[:, 768:1024], 'onorm', rjunk[p2][:, 0:256], sf)

        def genB(tl, ch=ch):
            ti = ch * (CH // 128) + tl
            p2, p4 = ti % 2, ti % 4
            tsl = slice(ti * 128, (ti + 1) * 128)
            yc, yck = ycats[p4], 'ycat%d' % p4
            for c in range(8):
                k.op('pe', lambda e, c=c: e.transpose(pT[:, c, :], yc[:, c * 128:(c + 1) * 128], cx.ident[:]), reads=[yck, 'ident'], writes=['pT'], pe_acc=(c > 0))
            k.op('act', lambda e: e.copy(out=ycT[:], in_=pT[:, :, :]), reads=['pT'], writes=['ycT'])
            for h in range(2):
                for c in range(8):
                    k.op('pe', lambda e, c=c: e.matmul(py[h][:, :], ycT[:, c, :], Wo[:, c, h * 512:(h + 1) * 512], start=(c == 0), stop=(c == 7)),
                         reads=['ycT', 'Wo'], writes=['py%d' % h], pe_acc=(c > 0))
            for h in range(2):
                k.op('dve', lambda e, h=h: e.tensor_tensor(out=xo[p2][:, h * 512:(h + 1) * 512], in0=py[h][:, :], in1=g2_b[:, h * 512:(h + 1) * 512], op=ALU.mult),
                     reads=['py%d' % h, 'mod_b'], writes=['xo%d' % p2])
            k.op('pool', lambda e: e.tensor_tensor(out=xo[p2][:], in0=xo[p2][:], in1=xt[p4][:], op=ALU.add), reads=['xo%d' % p2, 'xt%d' % p4], writes=['xo%d' % p2])
            cx.out_toks.append(k.dma('sp', xo_d[tsl, :], xo[p2][:], reads=['xo%d' % p2]))
            yield

        ntl = CH // 128
        for pr in range(ntl // 2 + 1):
            gens = []
            if pr < ntl // 2:
                gens += [genA(2 * pr), genA(2 * pr + 1)]
            if pr >= 1:
                gens += [genB(2 * pr - 2), genB(2 * pr - 1)]
            while gens:
                for gnr in list(gens):
                    try:
                        next(gnr)
                    except StopIteration:
                        gens.remove(gnr)


NTK = S
DEPTH = 2
TAB_SHAPES = {"caus": [128, 128], "wlow": [128, 128], "mc": [128, 17 * 128], "ovl": [512, 128], "T": [128, 254],
              "eslot": [64, S], "ropec": [512, 16]}
LAYER_SHAPES = {
    "f1_adaw": [D, 3 * D], "f1_adab": [1, 3 * D], "f1_nrm": [1, D], "f1_w_in": [D, 2 * DFF], "f1_w_out": [DFF, D],
    "f2_adaw": [D, 3 * D], "f2_adab": [1, 3 * D], "f2_nrm": [1, D], "f2_w_in": [D, 2 * DFF], "f2_w_out": [DFF, D],
    "p_adaw": [D, 2 * D], "p_adab": [1, 2 * D], "p_nrm": [1, D], "p_w_in": [D, N_IN], "p_gains": [1, 768],
    "s_lre": [512, 2], "s_lim": [512, 2], "s_ldt": [512, 2], "s_bre": [512, 32], "s_bim": [512, 32],
    "s_cre": [512, 32], "s_cim": [512, 32], "s_dsk": [128, 2],
    "n_w1k": [64, 32 * 128], "n_w1v": [64, 32 * 128], "n_pek": [64, 32], "n_pev": [64, 32], "n_w2k": [128, 64],
    "n_w2v": [128, 64], "n_kn0": [1, 64],
    "o_pwbd": [128, 256], "o_prow": [1, 1024], "o_onorm": [1, D], "o_gluw": [256, 256], "o_wo": [D, D],
    "o_adaw": [D, D], "o_adab": [1, D],
}


def build_all():
    cx = Ctx()
    x_d = cx.din("x", [NTK, D])
    cT_d = cx.din("cT", [128, 8])
    rope_d = cx.din("rope", [NTK, 16])
    corr_d = cx.din("corr", [128, 2 * HALO])
    tabs = {n: cx.din(n, sh) for n, sh in TAB_SHAPES.items()}
    W = [{n: cx.din("l%d_%s" % (l, n), sh) for n, sh in LAYER_SHAPES.items()} for l in range(DEPTH)]
    out_d = cx.dout("out", [NTK, D])
    x1_s = cx.dscr("x1_s", [NTK, D])
    x2_s = cx.dscr("x2_s", [NTK, D])
    xl_s = cx.dscr("xl_s", [NTK, D])
    U_s = cx.dscr("U_s", [NTK, N_IN])
    UT_s = cx.dscr("UT_s", [N_IN, NTK])
    nsa_s = cx.dscr("nsa_s", [NTK, 512])
    ysT_s = cx.dscr("ysT_s", [256, NTK])
    xin = x_d
    for l in range(DEPTH):
        w = W[l]
        cx.begin_stage("l%df1_" % l)
        stage_F(cx, {"x": xin, "cT": cT_d, "adaw": w["f1_adaw"], "adab": w["f1_adab"], "nrm": w["f1_nrm"],
                     "w_in": w["f1_w_in"], "w_out": w["f1_w_out"], "xo": x1_s}, NTK)
        cx.end_stage()
        cx.begin_stage("l%dp_" % l)
        stage_P(cx, {"x": x1_s, "cT": cT_d, "adaw": w["p_adaw"], "adab": w["p_adab"], "nrm": w["p_nrm"], "w_in": w["p_w_in"],
                     "gains": w["p_gains"], "rope": rope_d, "u": U_s, "uT": UT_s}, NTK)
        cx.end_stage()
        cx.begin_stage("l%ds_" % l)
        stage_B1(cx, {"uT": UT_s[1560:1816, :], "lre": w["s_lre"], "lim": w["s_lim"], "ldt": w["s_ldt"], "bre": w["s_bre"],
                      "bim": w["s_bim"], "cre": w["s_cre"], "cim": w["s_cim"], "dsk": w["s_dsk"], "yT": ysT_s}, 4)
        cx.end_stage()
        for g in range(2):
            cx.begin_stage("l%dn%d_" % (l, g))

            def load_q(k, jb, Qlo, Qhi, qlk, qhk, g=g):
                tsl = slice(jb * 128, (jb + 1) * 128)
                r0 = 256 + g * 256
                src = UT_s[r0:r0 + 256, tsl].rearrange("(r d) t -> d r t", d=64)
                k.dma('pool', Qlo[0:64, :].rearrange("p (r q) -> p r q", q=128), src, writes=[qlk])
                k.dma('pool', Qhi[0:64, :].rearrange("p (r q) -> p r q", q=128), src, writes=[qhk])

            io = {"load_q": load_q,
                  "ksT": UT_s[768 + 64 * g:768 + 64 * g + 64, :], "kwT": UT_s[896 + 64 * g:896 + 64 * g + 64, :],
                  "kcT": UT_s[1024 + 64 * g:1024 + 64 * g + 64, :], "vcT": UT_s[1152 + 64 * g:1152 + 64 * g + 64, :],
                  "vs": U_s[:, 1280 + 64 * g:1280 + 64 * g + 64], "vw": U_s[:, 1408 + 64 * g:1408 + 64 * g + 64],
                  "gat": U_s[:, 1536 + 12 * g:1536 + 12 * g + 12],
                  "w1k": w["n_w1k"], "w1v": w["n_w1v"], "pek": w["n_pek"], "pev": w["n_pev"], "w2k": w["n_w2k"],
                  "w2v": w["n_w2v"], "kn0": w["n_kn0"],
                  "o": nsa_s[:, g * 256:(g + 1) * 256]}
            io.update(tabs)
            stage_B2(cx, io)
            cx.end_stage()
        cx.begin_stage("l%do_" % l)
        stage_O(cx, {"x": x1_s, "upT": UT_s[0:256, :], "corr": corr_d, "pwbd": w["o_pwbd"], "prow": w["o_prow"],
                     "onorm": w["o_onorm"], "nsa": nsa_s, "ysT": ysT_s, "gluw": w["o_gluw"], "wo": w["o_wo"], "cT": cT_d,
                     "adaw": w["o_adaw"], "adab": w["o_adab"], "xo": x2_s}, NTK)
        cx.end_stage()
        cx.begin_stage("l%df2_" % l)
        xout = out_d if l == DEPTH - 1 else xl_s
        stage_F(cx, {"x": x2_s, "cT": cT_d, "adaw": w["f2_adaw"], "adab": w["f2_adab"], "nrm": w["f2_nrm"],
                     "w_in": w["f2_w_in"], "w_out": w["f2_w_out"], "xo": xout}, NTK)
        cx.end_stage()
        xin = xl_s
    return cx.nc


def _rope_table(n):
    pos = np.arange(n).astype(np.float32)
    inv = np.exp(-math.log(500000.0) * np.arange(8, dtype=np.float32) * (2.0 / 16)).astype(np.float32)
    ang = pos[:, None] * inv[None, :]
    return np.concatenate([np.cos(ang), np.sin(ang)], 1).astype(np.float32)


def _layer_inputs(P):
    f = np.float32
    aw, ab = P['ada_w'], P['ada_b']
    m = {}
    for nm, m0, nrm, wi, wo in (("f1", 0, 'norm_ffn1', 'ffn1_w_in', 'ffn1_w_out'), ("f2", 6, 'norm_ffn2', 'ffn2_w_in', 'ffn2_w_out')):
        m[nm + "_adaw"] = aw[:, m0 * D:(m0 + 3) * D]
        m[nm + "_adab"] = ab[None, m0 * D:(m0 + 3) * D]
        m[nm + "_nrm"] = P[nrm][None, :]
        m[nm + "_w_in"] = P[wi]
        m[nm + "_w_out"] = P[wo]
    m["p_adaw"] = aw[:, 3 * D:5 * D]
    m["p_adab"] = ab[None, 3 * D:5 * D]
    m["p_nrm"] = P['norm_mix'][None, :]
    wi = P['w_in']
    kv = wi[:, 768:1536].reshape(D, 6, 128)
    m["p_w_in"] = np.concatenate([wi[:, 0:768], kv[:, 2], kv[:, 4], kv[:, 0], kv[:, 1], kv[:, 3], kv[:, 5], wi[:, 1536:]], 1)
    m["p_gains"] = np.concatenate([np.tile(P['q_norm'], 8), np.tile(P['k_norm'][1], 2), np.tile(P['k_norm'][2], 2)])[None, :]
    dummy_u = np.zeros((1, 256), f)
    quads = [prep_B1(dummy_u, q, P) for q in range(4)]
    for nm in ("lre", "lim", "ldt", "bre", "bim", "cre", "cim", "dsk"):
        m["s_" + nm] = np.concatenate([qd[nm] for qd in quads], 0)
    m["n_w1k"] = P['cmp_k_w1'].reshape(32, 64, 128).transpose(1, 0, 2).reshape(64, 32 * 128)
    m["n_w1v"] = P['cmp_v_w1'].reshape(32, 64, 128).transpose(1, 0, 2).reshape(64, 32 * 128)
    m["n_pek"] = P['cmp_pe'][0].T
    m["n_pev"] = P['cmp_pe'][1].T
    m["n_w2k"] = P['cmp_k_w2']
    m["n_w2v"] = P['cmp_v_w2']
    m["n_kn0"] = P['k_norm'][0][None, :]
    pwbd = np.zeros((128, 2, 128), f)
    for t in range(2):
        for gi in range(2):
            pwbd[gi * 64:(gi + 1) * 64, t, gi * 64:(gi + 1) * 64] = P['pool_w'][2 * t + gi]
    m["o_pwbd"] = pwbd.reshape(128, 256)
    m["o_prow"] = np.concatenate([P['pool_b'].reshape(-1), P['pool_scale'], P['glu_b'], np.zeros(256, f)])[None, :]
    m["o_onorm"] = P['out_norm'][None, :]
    m["o_gluw"] = P['glu_w']
    m["o_wo"] = P['w_out']
    m["o_adaw"] = aw[:, 5 * D:6 * D]
    m["o_adab"] = ab[None, 5 * D:6 * D]
    return {k_: np.ascontiguousarray(v, dtype=f) for k_, v in m.items()}


_NC = []


def kernel(**inp):
    f = np.float32
    inp = {k_: np.asarray(v, dtype=f) for k_, v in inp.items()}
    x, c = inp['x'], inp['c']
    base = dict(b2_tables())
    base["rope"] = _rope_table(NTK)
    corr = np.ones((128, 2, HALO), f)
    tt = np.arange(HALO) + 1.0
    for t in range(2):
        for gi in range(2):
            w = (2, 4, 8, 16)[2 * t + gi]
            corr[gi * 64:(gi + 1) * 64, t, :] = (w / np.minimum(tt, w))[None, :]
    base["corr"] = corr.reshape(128, 2 * HALO)
    for l in range(DEPTH):
        P = {k_: inp[k_][l] for k_ in inp if k_ not in ('x', 'c')}
        for n, v in _layer_inputs(P).items():
            base["l%d_%s" % (l, n)] = v
    base = {k_: np.ascontiguousarray(v, dtype=f) for k_, v in base.items()}
    maps = []
    for ci in range(8):
        b = ci % 2
        m = dict(base)
        m["x"] = np.ascontiguousarray(x[b])
        m["cT"] = np.ascontiguousarray(c[b].reshape(8, 128).T)
        maps.append(m)
    if not _NC:
        _NC.append(build_all())
    res = run_bass_kernel_spmd(_NC[0], maps, core_ids=list(range(8)))
    return np.stack([res.results[0]["out"], res.results[1]["out"]], 0).astype(f)
```

```python
import math
from contextlib import ExitStack
import numpy as np
import concourse.bass as bass
import concourse.mybir as mybir
from concourse.bass_utils import run_bass_kernel_spmd

F32 = mybir.dt.float32
BF16 = mybir.dt.bfloat16
I32 = mybir.dt.int32
AF = mybir.ActivationFunctionType
ALU = mybir.AluOpType
AX = mybir.AxisListType

D = 1024
DFF = 2816
NTOK = 2048
NT = NTOK // 128
S = 8192
EPS = 1e-6
N_IN = 1816
N_DMA_SEMS = 16


class KB:
    def __init__(self, nc):
        self.nc = nc
        self.eng = {'pe': nc.tensor, 'act': nc.scalar, 'dve': nc.vector, 'pool': nc.gpsimd, 'sp': nc.sync}
        self.sem = {e: nc.alloc_semaphore('s_' + e) for e in self.eng}
        self.cnt = {e: 0 for e in self.eng}
        self.ekey = {e: e + '#0' for e in self.eng}
        self.epoch = {e: 0 for e in self.eng}
        self.seen = {e: {} for e in self.eng}
        self.dsem = [nc.alloc_semaphore('d%d' % i) for i in range(N_DMA_SEMS)]
        self.dcnt = 0
        self.lastw = {}
        self.reads = {}
        self.n_inst = 0
        self.kp = ''

    def _wait(self, e, tok):
        key, sem, val = tok
        if self.seen[e].get(key, 0) >= val:
            return
        self.seen[e][key] = val
        self.eng[e].wait_ge(sem, val)

    def _deps(self, e, reads, writes, pe_acc=False):
        toks = []
        for b in reads:
            t = self.lastw.get(b)
            if t is not None:
                toks.append((t, True))
        for b in writes:
            t = self.lastw.get(b)
            if t is not None:
                toks.append((t, False))
            toks.extend((t2, False) for t2 in self.reads.get(b, []))
        own = e + '#'
        for t, raw in toks:
            if t[0].startswith(own):
                if e == 'pe' or not raw:
                    continue
            if pe_acc and t[0].startswith('pe#'):
                continue
            self._wait(e, t)

    def _commit(self, tok, reads, writes):
        for b in reads:
            self.reads.setdefault(b, []).append(tok)
        for b in writes:
            self.lastw[b] = tok
            self.reads[b] = []

    def _pk(self, keys):
        return [(x if x.startswith('@') else self.kp + x) for x in keys]

    def op(self, e, inst_fn, reads=(), writes=(), pe_acc=False):
        reads, writes = self._pk(reads), self._pk(writes)
        self._deps(e, reads, writes, pe_acc)
        inst = inst_fn(self.eng[e])
        if self.cnt[e] >= 60000:
            self.epoch[e] += 1
            self.ekey[e] = '%s#%d' % (e, self.epoch[e])
            self.sem[e] = self.nc.alloc_semaphore('s_%s_%d' % (e, self.epoch[e]))
            self.cnt[e] = 0
        self.cnt[e] += 1
        inst.then_inc(self.sem[e], 1)
        tok = (self.ekey[e], self.sem[e], self.cnt[e])
        self._commit(tok, reads, writes)
        self.n_inst += 1
        return inst

    def dma(self, e, out, in_, reads=(), writes=(), **kw):
        reads, writes = self._pk(reads), self._pk(writes)
        i = self.dcnt
        self.dcnt += 1
        s = self.dsem[i % N_DMA_SEMS]
        kk = i // N_DMA_SEMS
        key = 'd%d' % (i % N_DMA_SEMS)
        if kk > 0:
            self._wait(e, (key, s, 16 * kk))
        self._deps(e, reads, writes)
        inst = self.eng[e].dma_start(out=out, in_=in_, **kw)
        inst.then_inc(s, 16)
        tok = (key, s, 16 * (kk + 1))
        self._commit(tok, reads, writes)
        self.n_inst += 1
        return tok

    def finish(self, toks):
        for t in toks:
            self._wait('sp', t)

    def stage_reset(self):
        toks = []
        for e in self.eng:
            if self.cnt[e] > 0:
                toks.append((self.ekey[e], self.sem[e], self.cnt[e]))
        for j in range(N_DMA_SEMS):
            uses = (self.dcnt - j + N_DMA_SEMS - 1) // N_DMA_SEMS if self.dcnt > j else 0
            if uses > 0:
                toks.append(('d%d' % j, self.dsem[j], 16 * uses))
        for e in self.eng:
            for t in toks:
                self._wait(e, t)
        self.lastw = {}
        self.reads = {}


class Ctx:
    def __init__(self):
        self.nc = bass.Bass("TRN2", target_bir_lowering=False)
        self.es = ExitStack()
        self.k = KB(self.nc)
        self.out_toks = []
        self.pfx = ""

    def din(self, name, shape, dt=F32):
        return self.nc.dram_tensor(name, list(shape), dt, kind="ExternalInput").ap()

    def dout(self, name, shape, dt=F32):
        return self.nc.dram_tensor(name, list(shape), dt, kind="ExternalOutput").ap()

    def dscr(self, name, shape, dt=F32):
        return self.nc.dram_tensor(name, list(shape), dt, kind="Internal").ap()

    def sb(self, name, shape, dt=F32):
        return self.es.enter_context(self.nc.sbuf_tensor(self.pfx + "s_" + name, list(shape), dt))

    def ps(self, name, shape, dt=F32):
        return self.es.enter_context(self.nc.psum_tensor(self.pfx + "p_" + name, list(shape), dt))

    def begin_stage(self, pfx):
        self.pfx = pfx
        self.es = ExitStack()
        self._ropekey = 'rope'

    def end_stage(self):
        self.k.stage_reset()
        self.out_toks = []
        self.es.close()

    def done(self):
        self.k.finish(self.out_toks)
        self.es.close()
        return self.nc


def emit_consts(cx):
    k = cx.k
    ident_f = cx.sb("ident_f", [128, 128], F32)
    ident = cx.sb("ident", [128, 128], BF16)
    ones_r = cx.sb("ones_r", [1, 128], BF16)
    k.op('pool', lambda e: e.memset(ident_f[:], 1.0), writes=['ident_f'])
    k.op('pool', lambda e: e.affine_select(out=ident_f[:], in_=ident_f[:], pattern=[[-1, 128]],
                                           compare_op=ALU.is_equal, fill=0.0, base=0, channel_multiplier=1),
         reads=['ident_f'], writes=['ident_f'])
    k.op('dve', lambda e: e.tensor_copy(ident[:], ident_f[:]), reads=['ident_f'], writes=['ident'])
    k.op('dve', lambda e: e.memset(ones_r[:], 1.0), writes=['ones_r'])
    cx.ident = ident
    cx.ident_f = ident_f
    cx.ones_r = ones_r


def emit_mod(cx, cT_d, adaw_d, adab_d, nmod, ps_keys, ps_tiles):
    k = cx.k
    mod_b = cx.sb("mod_b", [128, nmod, 1024], F32)
    cT = cx.sb("cT", [128, 8], F32)
    sig = cx.sb("csig", [128, 8], F32)
    crep = cx.sb("crep", [128, 8, 128], BF16)
    adabc = [cx.sb("adab_ch%d" % i, [1, 128], BF16) for i in range(2)]
    wch = [cx.sb("adaw_ch%d" % i, [128, 8, 128], BF16) for i in range(2)]
    k.dma('sp', cT[:], cT_d[:, :], writes=['cT'])
    k.op('act', lambda e: e.activation(out=sig[:], in_=cT[:], func=AF.Sigmoid), reads=['cT'], writes=['csig'])
    k.op('dve', lambda e: e.tensor_tensor(out=sig[:], in0=sig[:], in1=cT[:], op=ALU.mult),
         reads=['csig', 'cT'], writes=['csig'])
    k.op('dve', lambda e: e.tensor_copy(crep[:], sig[:].unsqueeze(2).to_broadcast([128, 8, 128])),
         reads=['csig'], writes=['crep'])
    nch = nmod * 8
    for ch in range(nch):
        w = wch[ch % 2]
        wk = 'adaw_ch%d' % (ch % 2)
        k.dma('pool', w[:], adaw_d[:, ch * 128:(ch + 1) * 128].rearrange("(kc p) n -> p kc n", p=128),
              writes=[wk])
        k.dma('pool', adabc[ch % 2][:], adab_d[:, ch * 128:(ch + 1) * 128], writes=['adab_ch%d' % (ch % 2)])
        pk = ps_keys[ch % 2]
        pt = ps_tiles[ch % 2]
        for kc in range(8):
            k.op('pe', lambda e, kc=kc: e.matmul(pt[:, 0:128], crep[:, kc, :], w[:, kc, :], start=(kc == 0), stop=False),
                 reads=['crep', wk], writes=[pk], pe_acc=(kc > 0))
        k.op('pe', lambda e: e.matmul(pt[:, 0:128], cx.ones_r[:, :], adabc[ch % 2][:, :], start=False, stop=True),
             reads=['ones_r', 'adab_ch%d' % (ch % 2)], writes=[pk], pe_acc=True)
        k.op('act', lambda e: e.copy(out=mod_b[:, ch // 8, (ch % 8) * 128:(ch % 8 + 1) * 128], in_=pt[:, 0:128]),
             reads=[pk], writes=['mod_b'])
    return mod_b


def emit_norm_mod(cx, x_t, xkey, gs_b, sh_b, hb, hbkey):
    k = cx.k
    W = cx.work
    ss, rstd, hf = W['ss'], W['rstd'], W['hf']
    k.op('dve', lambda e: e.scalar_tensor_tensor(out=hf[:], in0=x_t[:], scalar=1.0, in1=x_t[:], op0=ALU.mult, op1=ALU.mult,
                                                 accum_out=ss[:, 0:1]),
         reads=[xkey], writes=['hf', 'ss'])
    k.op('dve', lambda e: e.tensor_scalar(out=ss[:, 0:1], in0=ss[:, 0:1], scalar1=1.0 / D, scalar2=EPS, op0=ALU.mult, op1=ALU.add),
         reads=['ss'], writes=['ss'])
    k.op('pool', lambda e: e.tensor_tensor(out=rstd[:, 0:1], in0=ss[:, 0:1], in1=W['mhalf'][:, 0:1], op=ALU.pow),
         reads=['ss', 'mhalf'], writes=['rstd'])
    k.op('dve', lambda e: e.scalar_tensor_tensor(out=hf[:], in0=x_t[:], scalar=rstd[:, 0:1], in1=gs_b,
                                                 op0=ALU.mult, op1=ALU.mult),
         reads=[xkey, 'rstd', 'modd'], writes=['hf'])
    k.op('pool', lambda e: e.tensor_tensor(out=hb[:], in0=hf[:], in1=sh_b, op=ALU.add),
         reads=['hf', 'modd'], writes=[hbkey])


def emit_T(cx, hb, hbkey, hT, hTkey, col0, pT, pTkey):
    k = cx.k
    for c in range(8):
        k.op('pe', lambda e, c=c: e.transpose(pT[:, c, :], hb[:, c * 128:(c + 1) * 128], cx.ident[:]),
             reads=[hbkey, 'ident'], writes=[pTkey], pe_acc=(c > 0))
    k.op('act', lambda e: e.copy(out=hT[:, :, col0:col0 + 128], in_=pT[:, :, :]), reads=[pTkey], writes=[hTkey])


def emit_norm_mod_T(cx, x_t, xkey, gs_b, sh_b, hT, hTkey, col0, pT, pTkey, tag):
    emit_norm_mod(cx, x_t, xkey, gs_b, sh_b, cx.work['hb'], 'hb')
    emit_T(cx, cx.work['hb'], 'hb', hT, hTkey, col0, pT, pTkey)


def emit_work(cx, with_hb=True):
    W = {}
    W['ss'] = cx.sb("ss", [128, 1], F32)
    W['rstd'] = cx.sb("rstd", [128, 1], F32)
    W['hf'] = cx.sb("hf", [128, 1024], F32)
    if with_hb:
        W['hb'] = cx.sb("hb", [128, 1024], BF16)
    W['eps'] = cx.sb("eps", [128, 1], F32)
    W['mhalf'] = cx.sb("mhalf", [128, 1], F32)
    cx.k.op('dve', lambda e: e.memset(W['eps'][:], EPS), writes=['eps'])
    cx.k.op('dve', lambda e: e.memset(W['mhalf'][:], -0.5), writes=['mhalf'])
    cx.work = W


GRP = 256


def stage_F(cx, io, ntok):
    k = cx.k
    x_d, cT_d, adaw_d, adab_d, nrm_d, win_d, wout_d, xo_d = [io[n] for n in ("x", "cT", "adaw", "adab", "nrm", "w_in", "w_out", "xo")]

    emit_consts(cx)
    emit_work(cx, with_hb=False)
    Win = cx.sb("Win", [128, 8, 2 * DFF], BF16)
    Wout = cx.sb("Wout", [128, 22, D], BF16)
    nrm_b = cx.work['hf']
    hTs = [cx.sb("hT%d" % i, [128, 8, GRP], BF16) for i in range(2)]
    hbs = [cx.sb("hb%d" % i, [128, D], BF16) for i in range(GRP // 128)]
    actT = cx.sb("actT", [128, 22, GRP], BF16)
    sg = [cx.sb("sg%d" % i, [128, GRP], F32) for i in range(2)]
    xt = [cx.sb("xt%d" % i, [128, D], F32) for i in range(6)]
    pT = cx.ps("pT", [128, 8, 128], BF16)
    pg = [cx.ps("pg%d" % i, [128, 512], F32) for i in range(2)]
    pu = [cx.ps("pu%d" % i, [128, 512], F32) for i in range(2)]
    py = [cx.ps("py%d" % i, [128, 512], F32) for i in range(2)]

    k.dma('sp', nrm_b[:], nrm_d[0:1, :].to_broadcast([128, D]), writes=['hf'])
    for kc in range(8):
        k.dma('pool', Win[:, kc, :], win_d[kc * 128:(kc + 1) * 128, :], writes=['Win'])
    for kc in range(22):
        k.dma('pool', Wout[:, kc, :], wout_d[kc * 128:(kc + 1) * 128, :], writes=['Wout'])
    mod_b = emit_mod(cx, cT_d, adaw_d, adab_d, 3, ['pg0', 'pg1'], pg)
    k.op('dve', lambda e: e.scalar_tensor_tensor(out=mod_b[:, 1, :], in0=mod_b[:, 1, :], scalar=1.0, in1=nrm_b[:],
                                                 op0=ALU.add, op1=ALU.mult),
         reads=['mod_b', 'hf'], writes=['modd'])
    k.op('dve', lambda e: e.tensor_scalar(out=mod_b[:, 2, :], in0=mod_b[:, 2, :], scalar1=0.5, scalar2=None, op0=ALU.mult),
         reads=['mod_b', 'modd'], writes=['modd'])
    sh_b, gs_b, hg_b = mod_b[:, 0, :], mod_b[:, 1, :], mod_b[:, 2, :]

    ngrp = ntok // GRP
    tpg = GRP // 128

    def prep_load(g):
        for t in range(tpg):
            ti = g * tpg + t
            xi = (g % 3) * tpg + t
            k.dma('act', xt[xi][:], x_d[ti * 128:(ti + 1) * 128, :], writes=['xt%d' % xi])

    def prep_norm_t(g, t):
        xi = (g % 3) * tpg + t
        emit_norm_mod(cx, xt[xi], 'xt%d' % xi, gs_b, sh_b, hbs[t], 'hb%d' % t)

    def prep_norm(g):
        prep_load(g)
        for t in range(tpg):
            prep_norm_t(g, t)

    def prep_T(g, t):
        emit_T(cx, hbs[t], 'hb%d' % t, hTs[g % 2], 'hT%d' % (g % 2), t * 128, pT, 'pT')

    prep_norm(0)
    for t in range(tpg):
        prep_T(0, t)
    for g in range(ngrp):
        hT, hTk = hTs[g % 2], 'hT%d' % (g % 2)
        if g + 1 < ngrp:
            prep_load(g + 1)
        for i in range(22):
            b = i % 2
            for kc in range(8):
                k.op('pe', lambda e, kc=kc: e.matmul(pg[b][:, 0:GRP], Win[:, kc, i * 128:(i + 1) * 128], hT[:, kc, :],
                                                     start=(kc == 0), stop=(kc == 7)),
                     reads=['Win', hTk], writes=['pg%d' % b], pe_acc=(kc > 0))
            for kc in range(8):
                k.op('pe', lambda e, kc=kc: e.matmul(pu[b][:, 0:GRP], Win[:, kc, DFF + i * 128:DFF + (i + 1) * 128],
                                                     hT[:, kc, :], start=(kc == 0), stop=(kc == 7)),
                     reads=['Win', hTk], writes=['pu%d' % b], pe_acc=(kc > 0))
            k.op('act', lambda e: e.activation(out=sg[b][:], in_=pg[b][:, 0:GRP], func=AF.Silu),
                 reads=['pg%d' % b], writes=['sg%d' % b])
            k.op('dve', lambda e: e.tensor_tensor(out=actT[:, i, :], in0=sg[b][:], in1=pu[b][:, 0:GRP], op=ALU.mult),
                 reads=['sg%d' % b, 'pu%d' % b], writes=['actT'])
            if g + 1 < ngrp:
                if i in (1, 5):
                    prep_norm_t(g + 1, 0 if i == 1 else 1)
                if i in (10, 16):
                    prep_T(g + 1, 0 if i == 10 else 1)
        for t in range(tpg):
            ti = g * tpg + t
            xi = (g % 3) * tpg + t
            for h in range(2):
                for i in range(22):
                    k.op('pe', lambda e, i=i: e.matmul(py[h][:, :], actT[:, i, t * 128:(t + 1) * 128],
                                                       Wout[:, i, h * 512:(h + 1) * 512], start=(i == 0), stop=(i == 21)),
                         reads=['actT', 'Wout'], writes=['py%d' % h], pe_acc=(i > 0))
            hf = cx.work['hf']
            for h in range(2):
                k.op('dve', lambda e: e.tensor_tensor(out=hf[:, h * 512:(h + 1) * 512], in0=py[h][:, :],
                                                      in1=hg_b[:, h * 512:(h + 1) * 512], op=ALU.mult),
                     reads=['py%d' % h, 'modd'], writes=['hf'])
            k.op('pool', lambda e: e.tensor_tensor(out=xt[xi][:], in0=xt[xi][:], in1=hf[:], op=ALU.add),
                 reads=['hf', 'xt%d' % xi], writes=['xt%d' % xi])
            cx.out_toks.append(k.dma('sp', xo_d[ti * 128:(ti + 1) * 128, :], xt[xi][:], reads=['xt%d' % xi]))


def emit_headnorm_rope(cx, U, ukey, c0, nh, gain_b, cos_b, sin_b, tmp, sq, per_head_gain=False):
    k = cx.k
    v = U[:, c0:c0 + nh * 64].rearrange("p (h d) -> p h d", d=64)
    t = tmp[:, 0:nh * 64].rearrange("p (h d) -> p h d", d=64)
    ssq = sq[:, 0:nh]
    rs = sq[:, 16:16 + nh]
    k.op('dve', lambda e: e.tensor_tensor(out=t, in0=v, in1=v, op=ALU.mult), reads=[ukey], writes=['hn_tmp'])
    k.op('dve', lambda e: e.tensor_reduce(out=ssq, in_=t, axis=AX.X, op=ALU.add), reads=['hn_tmp'], writes=['hn_sq'])
    k.op('act', lambda e: e.activation(out=rs, in_=ssq, func=AF.Sqrt, scale=1.0 / 64, bias=cx.work['eps'][:, 0:1]),
         reads=['hn_sq', 'eps'], writes=['hn_sq'])
    k.op('dve', lambda e: e.reciprocal(out=rs, in_=rs), reads=['hn_sq'], writes=['hn_sq'])
    k.op('dve', lambda e: e.tensor_tensor(out=v, in0=v, in1=rs.unsqueeze(2).to_broadcast([128, nh, 64]), op=ALU.mult),
         reads=[ukey, 'hn_sq'], writes=[ukey])
    k.op('dve', lambda e: e.tensor_tensor(out=v, in0=v, in1=(gain_b if per_head_gain else gain_b.unsqueeze(1).to_broadcast([128, nh, 64])), op=ALU.mult),
         reads=[ukey, 'gains'], writes=[ukey])
    x1 = v[:, :, 0:8]
    x2 = v[:, :, 8:16]
    cb = cos_b.unsqueeze(1).to_broadcast([128, nh, 8])
    sb_ = sin_b.unsqueeze(1).to_broadcast([128, nh, 8])
    a, b, c, d = t[:, :, 0:8], t[:, :, 8:16], t[:, :, 16:24], t[:, :, 24:32]
    k.op('dve', lambda e: e.tensor_tensor(out=a, in0=x1, in1=cb, op=ALU.mult), reads=[ukey, getattr(cx, '_ropekey', 'rope')], writes=['hn_tmp'])
    k.op('dve', lambda e: e.tensor_tensor(out=b, in0=x2, in1=sb_, op=ALU.mult), reads=[ukey, getattr(cx, '_ropekey', 'rope')], writes=['hn_tmp'])
    k.op('dve', lambda e: e.tensor_tensor(out=c, in0=x2, in1=cb, op=ALU.mult), reads=[ukey, getattr(cx, '_ropekey', 'rope')], writes=['hn_tmp'])
    k.op('dve', lambda e: e.tensor_tensor(out=d, in0=x1, in1=sb_, op=ALU.mult), reads=[ukey, getattr(cx, '_ropekey', 'rope')], writes=['hn_tmp'])
    k.op('dve', lambda e: e.tensor_tensor(out=x1, in0=a, in1=b, op=ALU.subtract), reads=['hn_tmp'], writes=[ukey])
    k.op('dve', lambda e: e.tensor_tensor(out=x2, in0=c, in1=d, op=ALU.add), reads=['hn_tmp'], writes=[ukey])


def stage_P(cx, io, ntok, co_setup=None):
    k = cx.k
    x_d, cT_d, adaw_d, adab_d, nrm_d, win_d, gains_d, rope_d, u_d, uT_d = [io[n] for n in (
        "x", "cT", "adaw", "adab", "nrm", "w_in", "gains", "rope", "u", "uT")]
    UTt = cx.sb("UTt", [128, 15, 128], F32)

    emit_consts(cx)
    emit_work(cx, with_hb=False)
    Win = cx.sb("Win", [128, 8, N_IN], BF16)
    nrm_b = cx.work['hf']
    gains_b = cx.sb("gains_b", [128, 768], F32)
    hTs = [cx.sb("hT%d" % i, [128, 8, 128], BF16) for i in range(2)]
    hbs = [cx.sb("hb%d" % i, [128, D], BF16) for i in range(2)]
    xt = [cx.sb("xt%d" % i, [128, D], F32) for i in range(2)]
    U = [cx.sb("U%d" % i, [128, N_IN], F32) for i in range(2)]
    rope = [cx.sb("rope%d" % i, [128, 16], F32) for i in range(2)]
    tmp = cx.sb("hn_tmp", [128, 768], F32)
    sq = cx.sb("hn_sq", [128, 32], F32)
    pT = cx.ps("pT", [128, 8, 128], BF16)
    pu = [cx.ps("pu%d" % i, [128, 512], F32) for i in range(4)]

    k.dma('sp', nrm_b[:], nrm_d[0:1, :].to_broadcast([128, D]), writes=['hf'])
    k.dma('sp', gains_b[:], gains_d[0:1, :].to_broadcast([128, 768]), writes=['gains'])
    k.op('dve', lambda e: e.tensor_scalar(out=gains_b[:, 0:512], in0=gains_b[:, 0:512], scalar1=0.125, scalar2=None, op0=ALU.mult),
         reads=['gains'], writes=['gains'])
    for kc in range(8):
        k.dma('pool', Win[:, kc, :], win_d[kc * 128:(kc + 1) * 128, :], writes=['Win'])
    mod_b = emit_mod(cx, cT_d, adaw_d, adab_d, 2, ['pu0', 'pu1'], pu)
    k.op('dve', lambda e: e.scalar_tensor_tensor(out=mod_b[:, 1, :], in0=mod_b[:, 1, :], scalar=1.0, in1=nrm_b[:],
                                                 op0=ALU.add, op1=ALU.mult),
         reads=['mod_b', 'hf'], writes=['modd'])
    sh_b, gs_b = mod_b[:, 0, :], mod_b[:, 1, :]
    chunks = [(0, 512), (512, 512), (1024, 512), (1536, 280)]
    nt_ = ntok // 128

    def P1(ti):
        b = ti % 2
        k.dma('sp', xt[b][:], x_d[ti * 128:(ti + 1) * 128, :], writes=['xt%d' % b])
        k.dma('sp', rope[ti % 2][:], rope_d[ti * 128:(ti + 1) * 128, :], writes=['rope%d' % (ti % 2)])
        emit_norm_mod(cx, xt[b], 'xt%d' % b, gs_b, sh_b, hbs[b], 'hb%d' % b)

    def P2(ti):
        b = ti % 2
        ub_, ukey = U[ti % 2], 'U%d' % (ti % 2)
        emit_T(cx, hbs[b], 'hb%d' % b, hTs[b], 'hT%d' % b, 0, pT, '@pT')
        for ci, (c0, cw) in enumerate(chunks):
            for kc in range(8):
                k.op('pe', lambda e, kc=kc: e.matmul(pu[ci][:, 0:cw], hTs[b][:, kc, :], Win[:, kc, c0:c0 + cw],
                                                     start=(kc == 0), stop=(kc == 7)),
                     reads=['hT%d' % b, 'Win'], writes=['pu%d' % ci], pe_acc=(kc > 0))
            k.op('act', lambda e: e.copy(out=ub_[:, c0:c0 + cw], in_=pu[ci][:, 0:cw]), reads=['pu%d' % ci], writes=[ukey])

    def P3d(ti):
        ub_, ukey = U[ti % 2], 'U%d' % (ti % 2)
        rp = rope[ti % 2]
        cx._ropekey = 'rope%d' % (ti % 2)
        emit_headnorm_rope(cx, ub_, ukey, 256, 12, gains_b[:, :].rearrange("p (h d) -> p h d", d=64), rp[:, 0:8], rp[:, 8:16], tmp, sq,
                           per_head_gain=True)
        k.op('act', lambda e: e.activation(out=ub_[:, 1536:1560], in_=ub_[:, 1536:1560], func=AF.Sigmoid),
             reads=[ukey], writes=[ukey])
        cx.out_toks.append(k.dma('sp', u_d[ti * 128:(ti + 1) * 128, :], ub_[:], reads=[ukey]))

    def P3p(ti):
        ub_, ukey = U[ti % 2], 'U%d' % (ti % 2)
        for i in range(15):
            cw = min(128, N_IN - i * 128)
            k.op('pe', lambda e, i=i, cw=cw: e.transpose(pu[i // 4][0:cw, (i % 4) * 128:(i % 4 + 1) * 128], ub_[:, i * 128:i * 128 + cw], cx.ident_f[:]),
                 reads=[ukey, 'ident_f'], writes=['pu%d' % (i // 4)])
        for j in range(4):
            if j < 3:
                k.op('act', lambda e, j=j: e.copy(out=UTt[:, j * 4:j * 4 + 4, :], in_=pu[j][:, :].rearrange("p (i t) -> p i t", t=128)),
                     reads=['pu%d' % j], writes=['UTt'])
            else:
                k.op('act', lambda e: e.copy(out=UTt[:, 12:14, :], in_=pu[3][:, 0:256].rearrange("p (i t) -> p i t", t=128)),
                     reads=['pu3'], writes=['UTt'])
                k.op('act', lambda e: e.copy(out=UTt[0:24, 14, :], in_=pu[3][0:24, 256:384]), reads=['pu3'], writes=['UTt'])
        tsl = slice(ti * 128, (ti + 1) * 128)
        cx.out_toks.append(k.dma('sp', uT_d[0:1792, tsl].rearrange("(i p) t -> p i t", p=128), UTt[:, 0:14, :], reads=['UTt'], writes=['@UTa%d' % ti]))
        cx.out_toks.append(k.dma('sp', uT_d[1792:1816, tsl], UTt[0:24, 14, :], reads=['UTt'], writes=['@UTb%d' % ti]))

    adv = fin = None
    if co_setup is not None:
        k.kp = 'S:'
        NJ_ = LC + 1
        al = [U[0][:, 0:NJ_], U[0][:, NJ_:2 * NJ_], U[0][:, 2 * NJ_:3 * NJ_], U[1][:, 0:NJ_], U[1][:, NJ_:2 * NJ_]]
        adv, fin = co_setup(pT, al)
        k.kp = ''
        k.stage_reset()
        k.kp = 'P:'
    P1(0)
    for i in range(nt_ + 1):
        if i + 1 < nt_:
            P1(i + 1)
        if i < nt_:
            P2(i)
            P3d(i)
        if 0 <= i - 1 < nt_:
            P3p(i - 1)
            if adv is not None:
                k.kp = 'S:'
                adv(i // 4, 8)
                k.kp = 'P:'
    if fin is not None:
        k.kp = 'S:'
        fin()
    k.kp = ''


NB_NEG = -30000.0


def b2_tables():
    p = np.arange(128)
    caus = (p[:, None] <= p[None, :]).astype(np.float32)
    wlow = (p[:, None] > p[None, :]).astype(np.float32)
    f = np.floor((p - 31) / 16.0)
    mc = np.zeros((128, 17, 128), np.float32)
    for i in range(17):
        mc[:, i, :] = ((p[:, None] - 8 * i) <= f[None, :])
    n = np.arange(512)
    ovl = np.zeros((512, 128), np.float32)
    c_start = n[:, None] * 16
    s_start = np.arange(128)[None, :] * 64
    ovl = np.clip(np.minimum(c_start + 32, s_start + 64) - np.maximum(c_start, s_start), 0, None).astype(np.float32) / 32
    ovl[511, :] = 0
    T = np.zeros((128, 254), np.float32)
    cr = (p >= 64).astype(np.int64)
    m = np.arange(254) - 126
    T[(m[None, :] == cr[:, None]) | (m[None, :] == cr[:, None] - 1)] = 1000.0
    T[m[None, :] > cr[:, None]] = -1e30
    c = np.arange(8192)
    eslot = (np.arange(64)[:, None] == ((c // 64) % 64)[None, :]).astype(np.float32)
    pos = (np.arange(512) * 16 + 31).astype(np.float32)
    inv = np.exp(-math.log(500000.0) * np.arange(8, dtype=np.float32) * (2.0 / 16)).astype(np.float32)
    ang = pos[:, None] * inv[None, :]
    ropec = np.concatenate([np.cos(ang), np.sin(ang)], 1).astype(np.float32)
    return dict(caus=caus, wlow=wlow, mc=mc.reshape(128, 17 * 128), ovl=ovl, T=T, eslot=eslot, ropec=ropec)


def stage_B2(cx, io):
    k = cx.k
    (ksT_d, kwT_d, vs_d, vw_d, kcT_d, vcT_d, gat_d, w1k_d, w1v_d, pek_d, pev_d, w2k_d, w2v_d, kn0_d, caus_d, wlow_d,
     mc_d, ovl_d, T_d, eslot_d, ropec_d, o_d) = [io[n] for n in (
        "ksT", "kwT", "vs", "vw", "kcT", "vcT", "gat", "w1k", "w1v", "pek", "pev", "w2k", "w2v", "kn0", "caus", "wlow",
        "mc", "ovl", "T", "eslot", "ropec", "o")]

    emit_consts(cx)
    emit_work(cx)
    ks_aug = cx.sb("ks_aug", [128, S], BF16)
    kwT = cx.sb("kwT", [64, S], BF16)
    vs1 = cx.sb("vs1", [128, 64, 65], BF16)
    vw1 = cx.sb("vw1", [128, 64, 65], BF16)
    xk = cx.sb("xk", [64, S + 16], BF16)
    xv = cx.sb("xv", [64, S + 16], BF16)
    w1k = cx.sb("w1k", [64, 32, 128], BF16)
    w1v = cx.sb("w1v", [64, 32, 128], BF16)
    pek = cx.sb("pek", [64, 32], BF16)
    pev = cx.sb("pev", [64, 32], BF16)
    w2k = cx.sb("w2k", [128, 64], BF16)
    w2v = cx.sb("w2v", [128, 64], BF16)
    kcT = cx.sb("kcT", [64, 512], BF16)
    V1c = cx.sb("V1c", [128, 4, 193], BF16)
    gat = cx.sb("gat", [128, 64, 12], F32)
    caus = cx.sb("caus", [128, 128], BF16)
    wlow = cx.sb("wlow", [128, 128], BF16)
    mc = cx.sb("mc", [128, 17, 128], BF16)
    Tt = cx.sb("Tt", [128, 254], F32)
    kn0_b = cx.sb("kn0_b", [128, 64], F32)
    ropec = cx.sb("ropec", [128, 4, 16], F32)
    cbias = cx.sb("cbias", [128, 1], F32)
    xb = cx.sb("xb", [128, 512], F32)
    gt = cx.sb("gt", [128, 512], F32)
    hid = cx.sb("hid", [128, 512], BF16)
    kcf = cx.sb("kcf", [128, 64], F32)
    kcb = cx.sb("kcb", [128, 64], BF16)
    hn_tmp = cx.sb("hn_tmp", [128, 64], F32)
    hn_sq = cx.sb("hn_sq", [128, 32], F32)
    Qa = [[cx.sb("Qa%d_%d" % (i, j), [128, 512], BF16) for j in range(2)] for i in range(3)]
    pt = [cx.sb("pt%d" % i, [128, 512], BF16) for i in range(3)]
    oacc = [cx.sb("oacc%d" % i, [128, 4, 64], F32) for i in range(2)]
    imp = cx.sb("imp", [128, 128], F32)
    imp2 = cx.sb("imp2", [128, 128], F32)
    m8 = cx.sb("m8", [128, 16], F32)
    lr = cx.sb("lr", [128, 8], F32)
    otmp = cx.sb("otmp", [128, 4, 64], F32)
    nbshs = [[cx.sb("nbsh%d_%d" % (i, pp), [128, 128], BF16) for i in range(2)] for pp in range(2)]
    ps_s = [cx.ps("ps_s%d" % i, [128, 512], F32) for i in range(2)]
    po = [cx.ps("po%d" % i, [128, 512], F32) for i in range(4)]
    pm = cx.ps("pm", [128, 1024], BF16)

    for c4 in range(4):
        sl = slice(c4 * 2048, (c4 + 1) * 2048)
        k.dma('pool', ks_aug[0:64, sl], ksT_d[:, sl], writes=['ks_aug'])
        k.dma('pool', ks_aug[64:128, sl], eslot_d[:, sl], writes=['ks_aug'])
        k.dma('pool', kwT[:, sl], kwT_d[:, sl], writes=['kwT'])
        k.dma('pool', xk[:, sl], kcT_d[:, sl], writes=['xk'])
        k.dma('pool', xv[:, sl], vcT_d[:, sl], writes=['xv'])
    k.op('dve', lambda e: e.memset(xk[:, S:S + 16], 0.0), writes=['xk'])
    k.op('dve', lambda e: e.memset(xv[:, S:S + 16], 0.0), writes=['xv'])
    k.op('dve', lambda e: e.memset(vs1[:, :, 64:65], 1.0), writes=['vs1'])
    k.op('dve', lambda e: e.memset(vw1[:, :, 64:65], 1.0), writes=['vw1'])
    k.op('dve', lambda e: e.memset(V1c[:, :, 64:65], 1.0), writes=['V1c'])
    for c4 in range(4):
        k.dma('pool', vs1[:, c4 * 16:(c4 + 1) * 16, 0:64],
              vs_d[c4 * 2048:(c4 + 1) * 2048, :].rearrange("(kt p) d -> p kt d", p=128), writes=['vs1'])
        k.dma('pool', vw1[:, c4 * 16:(c4 + 1) * 16, 0:64],
              vw_d[c4 * 2048:(c4 + 1) * 2048, :].rearrange("(kt p) d -> p kt d", p=128), writes=['vw1'])
        k.dma('sp', gat[:, c4 * 16:(c4 + 1) * 16, :],
              gat_d[c4 * 2048:(c4 + 1) * 2048, :].rearrange("(jb p) c -> p jb c", p=128), writes=['gat'])
    k.dma('pool', w1k[:].rearrange("p i h -> p (i h)"), w1k_d[:, :], writes=['w1k'])
    k.dma('pool', w1v[:].rearrange("p i h -> p (i h)"), w1v_d[:, :], writes=['w1v'])
    k.dma('pool', pek[:], pek_d[:, :], writes=['pek'])
    k.dma('pool', pev[:], pev_d[:, :], writes=['pev'])
    k.dma('pool', w2k[:], w2k_d[:, :], writes=['w2k'])
    k.dma('pool', w2v[:], w2v_d[:, :], writes=['w2v'])
    k.dma('pool', caus[:], caus_d[:, :], writes=['caus'])
    k.dma('pool', wlow[:], wlow_d[:, :], writes=['wlow'])
    k.dma('pool', mc[:].rearrange("p i q -> p (i q)"), mc_d[:, :], writes=['mc'])
    k.dma('pool', V1c[:, :, 65:193], ovl_d.rearrange("(nt p) j -> p nt j", p=128), writes=['V1c'])
    k.dma('sp', Tt[:], T_d[:, :], writes=['Tt'])
    k.dma('sp', kn0_b[:], kn0_d[0:1, :].to_broadcast([128, 64]), writes=['gains'])
    k.dma('sp', ropec[:], ropec_d.rearrange("(nt p) c -> p nt c", p=128), writes=['rope'])
    for pp in range(2):
        for i in range(2):
            k.op('dve', lambda e: e.memset(nbshs[pp][i][:], 0.0), writes=['nbsh%d_%d' % (i, pp)])
    caus_b = cx.sb("caus_b", [128, 4, 128], BF16)
    wlow_b = cx.sb("wlow_b", [128, 4, 128], BF16)
    for tb_, src_, sk_, tk_ in ((caus_b, caus, 'caus', 'caus_b'), (wlow_b, wlow, 'wlow', 'wlow_b')):
        k.op('dve', lambda e, tb_=tb_, src_=src_: e.tensor_scalar(out=tb_[:], in0=src_[:, :].unsqueeze(1).to_broadcast([128, 4, 128]),
                                                                  scalar1=-NB_NEG, scalar2=NB_NEG, op0=ALU.mult, op1=ALU.add),
             reads=[sk_], writes=[tk_])

    for (xc, xkey, w1, w1key, pe, pekey, w2, w2key, is_k) in [(xk, 'xk', w1k, 'w1k', pek, 'pek', w2k, 'w2k', True),
                                                           (xv, 'xv', w1v, 'w1v', pev, 'pev', w2v, 'w2v', False)]:
        xcv = xc[:].rearrange("p (n s) -> p n s", s=16)
        for i in range(32):
            k.op('pe', lambda e, i=i: e.matmul(po[0][:, 0:1], w1[:, i, :], pe[:, i:i + 1], start=(i == 0), stop=(i == 31)),
                 reads=[w1key, pekey], writes=['po0'], pe_acc=(i > 0))
        k.op('act', lambda e: e.copy(out=cbias[:], in_=po[0][:, 0:1]), reads=['po0'], writes=['cbias'])
        for i in range(32):
            o_, s_ = i // 16, i % 16
            k.op('pe', lambda e, i=i: e.matmul(ps_s[0][:, :], w1[:, i, :], xcv[:, o_:o_ + 512, s_], start=(i == 0), stop=(i == 31)),
                 reads=[w1key, xkey], writes=['ps_s0'], pe_acc=(i > 0))
        k.op('act', lambda e: e.activation(out=xb[:], in_=ps_s[0][:, :], func=AF.Identity, bias=cbias[:, 0:1]),
             reads=['ps_s0', 'cbias'], writes=['xb'])
        k.op('dve', lambda e: e.tensor_tensor(out=gt[:], in0=xb[:], in1=xb[:], op=ALU.mult), reads=['xb'], writes=['gt'])
        k.op('dve', lambda e: e.tensor_scalar(out=gt[:], in0=gt[:], scalar1=0.044715, scalar2=1.0, op0=ALU.mult, op1=ALU.add),
             reads=['gt'], writes=['gt'])
        k.op('dve', lambda e: e.tensor_tensor(out=gt[:], in0=gt[:], in1=xb[:], op=ALU.mult), reads=['gt', 'xb'], writes=['gt'])
        k.op('act', lambda e: e.activation(out=gt[:], in_=gt[:], func=AF.Sigmoid, scale=1.5957691216057308),
             reads=['gt'], writes=['gt'])
        k.op('dve', lambda e: e.tensor_tensor(out=hid[:], in0=gt[:], in1=xb[:], op=ALU.mult), reads=['gt', 'xb'], writes=['hid'])
        for nt in range(4):
            k.op('pe', lambda e: e.matmul(po[1][:, 0:64], hid[:, nt * 128:(nt + 1) * 128], w2[:, :], start=True, stop=True),
                 reads=['hid', w2key], writes=['po1'])
            if is_k:
                k.op('act', lambda e: e.copy(out=kcf[:], in_=po[1][:, 0:64]), reads=['po1'], writes=['kcf'])
                emit_headnorm_rope(cx, kcf, 'kcf', 0, 1, kn0_b[:, :], ropec[:, nt, 0:8], ropec[:, nt, 8:16], hn_tmp, hn_sq)
                k.op('dve', lambda e: e.tensor_copy(kcb[:], kcf[:]), reads=['kcf'], writes=['kcb'])
                k.op('pe', lambda e: e.transpose(pm[0:64, 0:128], kcb[:, :], cx.ident[:]), reads=['kcb', 'ident'], writes=['pm'])
                k.op('act', lambda e: e.copy(out=kcT[:, nt * 128:(nt + 1) * 128], in_=pm[0:64, 0:128]), reads=['pm'], writes=['kcT'])
            else:
                k.op('act', lambda e: e.copy(out=V1c[:, nt, 0:64], in_=po[1][:, 0:64]), reads=['po1'], writes=['V1c'])

    NH = 4
    cur_list = [None]
    cnt = [0]

    def mk_step(lhsT, lkeys, rhs_fn, rkeys, ncol, nh, mask, vtile, vkey, vw, first, last, pre=None, after=None, bsel=2):
        st = {}

        def A():
            i, j = st['i'], st['j']
            if pre is not None:
                pre()
            pe_bias = mask is not None and mask[1] in ('caus', 'wlow')
            k.op('pe', lambda e: e.matmul(ps_s[i][:, 0:ncol], lhsT, rhs_fn(), start=True, stop=(not pe_bias)),
                 reads=lkeys + rkeys, writes=['ps_s%d' % i])
            if pe_bias:
                bt = caus_b if mask[1] == 'caus' else wlow_b
                k.op('pe', lambda e: e.matmul(ps_s[i][:, 0:ncol], cx.ident[:, :], bt[:].rearrange("p r q -> p (r q)")[:, 0:ncol],
                                              start=False, stop=True),
                     reads=['ident', mask[1] + '_b'], writes=['ps_s%d' % i], pe_acc=True)
            k.op('act', lambda e: e.activation(out=pt[j][:, 0:ncol], in_=ps_s[i][:, 0:ncol], func=AF.Exp),
                 reads=['ps_s%d' % i], writes=['pt%d' % j])
            if mask is not None and not pe_bias:
                mk, mkey = mask
                k.op('dve', lambda e: e.tensor_tensor(out=pt[j][:, 0:ncol].rearrange("p (r q) -> p r q", q=128),
                                                      in0=pt[j][:, 0:ncol].rearrange("p (r q) -> p r q", q=128),
                                                      in1=mk.unsqueeze(1).to_broadcast([128, nh, 128]), op=ALU.mult),
                     reads=['pt%d' % j, mkey], writes=['pt%d' % j])

        def B():
            j = st['j']
            for r in range(nh):
                if vw == 193:
                    bank, c0, lead = r // 2, (r % 2) * 193, (r % 2 == 0)
                else:
                    bank, c0, lead = bsel, r * 65, (r == 0)
                st_ = first and lead
                k.op('pe', lambda e, r=r: e.matmul(po[bank][:, c0:c0 + vw], pt[j][:, r * 128:(r + 1) * 128], vtile,
                                                   start=st_, stop=last, skip_group_check=True),
                     reads=['pt%d' % j, vkey], writes=['po%d' % bank], pe_acc=(not st_))
        st.update(A=A, B=B, after=after)
        cur_list[0].append(st)

    def branch_epilogue(ob, obkey, jb, br, bank):
        pv = po[bank][:, 0:260].rearrange("p (r c) -> p r c", c=65)
        pk = 'po%d' % bank
        k.op('dve', lambda e: e.tensor_scalar(out=lr[:, 0:4], in0=pv[:, :, 64], scalar1=1e-30, scalar2=None, op0=ALU.max),
             reads=[pk], writes=['lr'])
        k.op('dve', lambda e: e.reciprocal(out=lr[:, 0:4], in_=lr[:, 0:4]), reads=['lr'], writes=['lr'])
        gv = gat[:, jb, :].rearrange("p (r b) -> p r b", b=3)[:, :, br]
        k.op('dve', lambda e: e.tensor_tensor(out=lr[:, 4:8], in0=lr[:, 0:4], in1=gv, op=ALU.mult), reads=['lr', 'gat'], writes=['lr'])
        k.op('dve', lambda e: e.tensor_tensor(out=otmp[:], in0=pv[:, :, 0:64], in1=lr[:, 4:8].unsqueeze(2).to_broadcast([128, 4, 64]), op=ALU.mult),
             reads=[pk, 'lr'], writes=['otmp'])
        k.op('pool', lambda e: e.tensor_tensor(out=ob[:], in0=ob[:], in1=otmp[:], op=ALU.add), reads=[obkey, 'otmp'], writes=[obkey])

    deferred = {}

    def cmp_after(jb, ob, obkey, Qlo, Qhi, qlk, qhk):
        nbsh = nbshs[jb % 2]

        def f():
            pvs = [po[bk][:, 0:386].rearrange("p (r c) -> p r c", c=193) for bk in range(2)]
            for bk in range(2):
                k.op('dve', lambda e, bk=bk: e.tensor_scalar(out=lr[:, 2 * bk:2 * bk + 2], in0=pvs[bk][:, :, 64], scalar1=1e-30, scalar2=None, op0=ALU.max),
                     reads=['po%d' % bk], writes=['lr'])
            k.op('dve', lambda e: e.reciprocal(out=lr[:, 0:4], in_=lr[:, 0:4]), reads=['lr'], writes=['lr'])
            k.op('dve', lambda e: e.scalar_tensor_tensor(out=imp[:], in0=pvs[0][:, 0, 65:193], scalar=lr[:, 0:1], in1=Tt[:, 126 - 2 * jb:254 - 2 * jb],
                                                         op0=ALU.mult, op1=ALU.add), reads=['po0', 'lr', 'Tt'], writes=['imp'])
            for r in range(1, 4):
                k.op('dve', lambda e, r=r: e.scalar_tensor_tensor(out=imp[:], in0=pvs[r // 2][:, r % 2, 65:193], scalar=lr[:, r:r + 1], in1=imp[:],
                                                                  op0=ALU.mult, op1=ALU.add), reads=['po%d' % (r // 2), 'lr', 'imp'], writes=['imp'])
            gv = gat[:, jb, :].rearrange("p (r b) -> p r b", b=3)[:, :, 0]
            k.op('dve', lambda e: e.tensor_tensor(out=lr[:, 4:8], in0=lr[:, 0:4], in1=gv, op=ALU.mult), reads=['lr', 'gat'], writes=['lr'])
            for bk in range(2):
                k.op('dve', lambda e, bk=bk: e.tensor_tensor(out=ob[:, 2 * bk:2 * bk + 2, :], in0=pvs[bk][:, :, 0:64],
                                                             in1=lr[:, 4 + 2 * bk:6 + 2 * bk].unsqueeze(2).to_broadcast([128, 2, 64]), op=ALU.mult),
                     reads=['po%d' % bk, 'lr'], writes=[obkey])
            k.op('dve', lambda e: e.tensor_scalar(out=imp[:, 0:1], in0=imp[:, 0:1], scalar1=1000.0, scalar2=None, op0=ALU.add),
                 reads=['imp'], writes=['imp'])
            k.op('dve', lambda e: e.max(out=m8[:, 0:8], in_=imp[:]), reads=['imp'], writes=['m8'])
            k.op('dve', lambda e: e.match_replace(out=imp2[:], in_to_replace=m8[:, 0:8], in_values=imp[:], imm_value=-3.0e38),
                 reads=['imp', 'm8'], writes=['imp2'])
            k.op('dve', lambda e: e.max(out=m8[:, 8:16], in_=imp2[:]), reads=['imp2'], writes=['m8'])
            k.op('dve', lambda e: e.tensor_scalar(out=imp2[:], in0=imp[:], scalar1=m8[:, 15:16], scalar2=1.0, op0=ALU.is_ge, op1=ALU.subtract),
                 reads=['imp', 'm8'], writes=['imp2'])
            for i in range(2):
                k.op('dve', lambda e, i=i: e.tensor_scalar(out=nbsh[i][:, 64:128], in0=imp2[:, i * 64:(i + 1) * 64], scalar1=-NB_NEG, scalar2=None, op0=ALU.mult),
                     reads=['imp2'], writes=['nbsh%d_%d' % (i, jb % 2)])

        def f2():
            for i in range(2):
                k.op('pe', lambda e, i=i: e.transpose(pm[:, i * 128:(i + 1) * 128], nbsh[i][:, :], cx.ident[:]),
                     reads=['nbsh%d_%d' % (i, jb % 2), 'ident'], writes=['pm'])
            k.op('dve', lambda e: e.tensor_copy(Qlo[64:128, :].rearrange("p (r q) -> p r q", q=128),
                                                pm[64:128, 0:128].unsqueeze(1).to_broadcast([64, NH, 128])),
                 reads=['pm'], writes=[qlk + 'b'])
            k.op('dve', lambda e: e.tensor_copy(Qhi[64:128, :].rearrange("p (r q) -> p r q", q=128),
                                                pm[64:128, 128:256].unsqueeze(1).to_broadcast([64, NH, 128])),
                 reads=['pm'], writes=[qhk + 'b'])
        deferred[jb] = f2
        if jb == 0:
            def f0(f=f, f2=f2):
                f()
                f2()
            return f0
        return f

    per_q = []
    for jb in range(64):
        lists = {'c': [], 'w': [], 's': []}
        b2 = jb % 2
        b3 = jb % 3
        Qlo, Qhi = Qa[b3]
        qlk, qhk = 'Qa%d_0' % b3, 'Qa%d_1' % b3
        ob, obkey = oacc[b2], 'oacc%d' % b2

        def ldq(j):
            io["load_q"](k, j, Qa[j % 3][0], Qa[j % 3][1], 'Qa%d_0q' % (j % 3), 'Qa%d_1q' % (j % 3))

        def pre(jb=jb):
            if jb == 0:
                ldq(0)
                ldq(1)

        def pre_s(jb=jb):
            if jb + 2 < 64:
                ldq(jb + 2)
        cur_list[0] = lists['c']
        tiles = [nt for nt in range(4) if 128 * nt <= 8 * jb + 6]
        for ii, nt in enumerate(tiles):
            masked = not (128 * nt + 127 <= 8 * jb - 2)
            mask = (mc[:, (8 * jb - 128 * nt) // 8, :], 'mc') if masked else None
            mk_step(kcT[:, nt * 128:(nt + 1) * 128], ['kcT'], (lambda Qlo=Qlo: Qlo[0:64, :]), [qlk + 'q'], 512, 4, mask,
                    V1c[:, nt, :], 'V1c', 193, ii == 0, ii == len(tiles) - 1, pre=(pre if ii == 0 else None),
                    after=(cmp_after(jb, ob, obkey, Qlo, Qhi, qlk, qhk) if ii == len(tiles) - 1 else None))
        cur_list[0] = lists['w']
        kts = list(range(max(0, jb - 4), jb + 1))
        for ii, kt in enumerate(kts):
            mask = None
            if kt == jb:
                mask = (caus[:, :], 'caus')
            elif kt == jb - 4:
                mask = (wlow[:, :], 'wlow')
            mk_step(kwT[:, kt * 128:(kt + 1) * 128], ['kwT'], (lambda Qlo=Qlo: Qlo[0:64, :]), [qlk + 'q'], 512, NH, mask,
                    vw1[:, kt, :], 'vw1', 65, ii == 0, ii == len(kts) - 1,
                    after=((lambda ob=ob, obkey=obkey, jb=jb: branch_epilogue(ob, obkey, jb, 2, 2)) if ii == len(kts) - 1 else None), bsel=2)
        cur_list[0] = lists['s']
        for kt in range(jb + 1):
            Q, qk = (Qlo, qlk) if kt < 32 else (Qhi, qhk)
            mask = (caus[:, :], 'caus') if kt == jb else None

            def fin(ob=ob, obkey=obkey, jb=jb):
                branch_epilogue(ob, obkey, jb, 1, 3)
                if jb + 1 in deferred:
                    deferred[jb + 1]()
                cx.out_toks.append(k.dma('sp', o_d[jb * 128:(jb + 1) * 128, :], ob[:].rearrange("p r d -> p (r d)"), reads=[obkey]))
            mk_step(ks_aug[:, kt * 128:(kt + 1) * 128], ['ks_aug'], (lambda Q=Q: Q[:, :]), [qk + 'q', qk + 'b'], 512, NH, mask,
                    vs1[:, kt, :], 'vs1', 65, kt == 0, kt == jb, pre=(pre_s if kt == 0 else None), after=(fin if kt == jb else None), bsel=3)
        per_q.append(lists)
    steps = per_q[0]['c'] + per_q[0]['w']
    for jb in range(64):
        if jb + 1 < 64:
            steps = steps + per_q[jb + 1]['c']
        steps = steps + per_q[jb]['s']
        if jb + 1 < 64:
            steps = steps + per_q[jb + 1]['w']
    for i, st in enumerate(steps):
        st['i'], st['j'] = i % 2, i % 3
        st['A']()
        if i > 0:
            steps[i - 1]['B']()
            if steps[i - 1]['after'] is not None:
                steps[i - 1]['after']()
    steps[-1]['B']()
    steps[-1]['after']()


def prep_B2(U, g, hh, P, tabs):
    f = np.float32
    heads = [2 * hh, 2 * hh + 1, 2 * (1 - hh), 2 * (1 - hh) + 1]
    q = U[:, 256:768].reshape(64, 128, 2, 4, 64)[:, :, g][:, :, heads]
    qT = np.ascontiguousarray(q.transpose(0, 3, 2, 1)).reshape(64, 64, 512)
    kv = U[:, 768:1536].reshape(S, 6, 2, 64)[:, :, g]
    gat = U[:, 1536:1560].reshape(S, 2, 4, 3)[:, g][:, heads[:2]].reshape(S, 6)
    w1k = P['cmp_k_w1'].reshape(32, 64, 128).transpose(1, 0, 2).reshape(64, 32 * 128)
    w1v = P['cmp_v_w1'].reshape(32, 64, 128).transpose(1, 0, 2).reshape(64, 32 * 128)
    m = {
        "qT": qT.astype(f),
        "kcT": np.ascontiguousarray(kv[:, 0].T), "vcT": np.ascontiguousarray(kv[:, 1].T),
        "ksT": np.ascontiguousarray(kv[:, 2].T), "vs": np.ascontiguousarray(kv[:, 3]),
        "kwT": np.ascontiguousarray(kv[:, 4].T), "vw": np.ascontiguousarray(kv[:, 5]),
        "gat": np.ascontiguousarray(gat),
        "w1k": np.ascontiguousarray(w1k), "w1v": np.ascontiguousarray(w1v),
        "pek": np.ascontiguousarray(P['cmp_pe'][0].T), "pev": np.ascontiguousarray(P['cmp_pe'][1].T),
        "w2k": np.ascontiguousarray(P['cmp_k_w2']), "w2v": np.ascontiguousarray(P['cmp_v_w2']),
        "kn0": np.ascontiguousarray(P['k_norm'][0][None, :]),
    }
    m.update(tabs)
    return {k_: np.ascontiguousarray(v, dtype=f) for k_, v in m.items()}


LC = 512
TWO_PI_LO = 6.283185


def emit_sin_turns(cx, out, r, rkey, okey, W, wkey):
    k = cx.k
    ri, rf, rg = W['i'], W['f'], W['g']
    k.op('dve', lambda e: e.tensor_copy(ri, r), reads=[rkey], writes=[wkey])
    k.op('dve', lambda e: e.tensor_copy(rf, ri), reads=[wkey], writes=[wkey])
    k.op('dve', lambda e: e.tensor_tensor(out=rf, in0=r, in1=rf, op=ALU.subtract), reads=[rkey, wkey], writes=[wkey])
    k.op('dve', lambda e: e.tensor_scalar(out=rg, in0=rf, scalar1=0.5, scalar2=None, op0=ALU.is_gt), reads=[wkey], writes=[wkey])
    k.op('dve', lambda e: e.tensor_tensor(out=rf, in0=rf, in1=rg, op=ALU.subtract), reads=[wkey], writes=[wkey])
    k.op('dve', lambda e: e.tensor_scalar(out=rg, in0=rf, scalar1=-0.5, scalar2=None, op0=ALU.is_lt), reads=[wkey], writes=[wkey])
    k.op('dve', lambda e: e.tensor_tensor(out=rf, in0=rf, in1=rg, op=ALU.add), reads=[wkey], writes=[wkey])
    k.op('act', lambda e: e.activation(out=out, in_=rf, func=AF.Sin, scale=TWO_PI_LO), reads=[wkey], writes=[okey])


def stage_B1(cx, io, nq=4, shared_pT=None, co=False, alias=None):
    k = cx.k
    uT_d, lre_d, lim_d, ldt_d, bre_d, bim_d, cre_d, cim_d, dsk_d, y_d = [io[n] for n in (
        "uT", "lre", "lim", "ldt", "bre", "bim", "cre", "cim", "dsk", "yT")]
    NTL = 2 * nq
    emit_consts(cx)
    NJ = LC + 1
    lre = cx.sb("lre", [128, NTL]); lim = cx.sb("lim", [128, NTL]); ldt = cx.sb("ldt", [128, NTL])
    bre = cx.sb("bre", [128, NTL, 16]); bim = cx.sb("bim", [128, NTL, 16])
    cre = cx.sb("cre", [128, NTL, 16]); cim = cx.sb("cim", [128, NTL, 16])
    dsk = cx.sb("dsk", [32, NTL])
    sc = cx.sb("sc", [128, 12 * NTL])
    sc2 = cx.sb("sc2", [128, 2 * NTL])
    C_ = lambda base, t: sc[:, base * NTL + t:base * NTL + t + 1]
    CA = lambda base: sc[:, base * NTL:(base + 1) * NTL]
    cosT = [cx.sb("cosT%d" % t, [128, NJ]) for t in range(NTL)]
    sinT = [cx.sb("sinT%d" % t, [128, NJ]) for t in range(NTL)]
    if alias is None:
        iota_i = cx.sb("iota_i", [128, NJ], I32)
        rr = cx.sb("rr", [128, NJ])
        Wt = {'i': cx.sb("w_i", [128, NJ], I32)[:], 'f': cx.sb("w_f", [128, NJ])[:], 'g': cx.sb("w_g", [128, NJ])[:]}
    else:
        iota_i, rr = alias[0].bitcast(I32), alias[1]
        Wt = {'i': alias[2].bitcast(I32), 'f': alias[3], 'g': alias[4]}
    Bbd = [cx.sb("Bbd%d" % c, [128, 32], BF16) for c in range(2)]
    BbT = [[cx.sb("BbT%d_%d" % (t, c), [32, 128], BF16) for c in range(2)] for t in range(NTL)]
    Cbd = [[cx.sb("Cbd%d_%d" % (t, c), [128, 32], BF16) for c in range(3)] for t in range(NTL)]
    tmpB = cx.sb("tmpB", [128, 4, 16])
    NS = 3
    NU = 3 if co else 4
    uf = [cx.sb("uf%d" % i, [32, LC]) for i in range(NU)]
    ub = [cx.sb("ub%d" % i, [32, LC], BF16) for i in range(2)]
    WK = [{n: cx.sb("%s_%d" % (n, i), [128, LC], (BF16 if n[0] == 'z' else F32))
           for n in ('t1', 't2', 't3', 't4', 'wre', 'wim', 'vre', 'vim', 'z1', 'z2', 'z3', 'z4')} for i in range(NS)]
    init = [cx.sb("init%d" % t, [128, 2]) for t in range(NTL)]
    yo = [cx.sb("yo%d" % i, [32, LC]) for i in range(2)]
    nps = 1 if co else 2
    pA = [cx.ps("pA%d" % i, [128, 512]) for i in range(nps)] * (2 // nps)
    pB = [cx.ps("pB%d" % i, [128, 512]) for i in range(nps)] * (2 // nps)
    pC = [cx.ps("pC%d" % i, [128, 512]) for i in range(nps)] * (2 // nps)
    if shared_pT is None:
        pm = cx.ps("pm", [128, 1024], BF16)
        pmk = 'pm'
    else:
        pm = shared_pT[:].rearrange("p c t -> p (c t)")
        pmk = '@pT'

    for a_, d_, nm in [(lre, lre_d, 'lre'), (lim, lim_d, 'lim'), (ldt, ldt_d, 'ldt')]:
        k.dma('sp', a_[:].rearrange("p (q t) -> p q t", t=2), d_.rearrange("(q p) t -> p q t", p=128), writes=[nm])
    k.dma('sp', dsk[:].rearrange("p (q t) -> p q t", t=2), dsk_d.rearrange("(q p) t -> p q t", p=32), writes=['dsk'])
    for a_, d_, nm in [(bre, bre_d, 'bre'), (bim, bim_d, 'bim'), (cre, cre_d, 'cre'), (cim, cim_d, 'cim')]:
        k.dma('sp', a_[:].rearrange("p (q t) h -> p q (t h)", t=2), d_.rearrange("(q p) c -> p q c", p=128), writes=[nm])
    k.op('act', lambda e: e.activation(out=CA(0), in_=ldt[:], func=AF.Exp), reads=['ldt'], writes=['sc'])
    k.op('dve', lambda e: e.tensor_tensor(out=CA(1), in0=lre[:], in1=CA(0), op=ALU.mult), reads=['lre', 'sc'], writes=['sc'])
    k.op('dve', lambda e: e.tensor_tensor(out=CA(2), in0=lim[:], in1=CA(0), op=ALU.mult), reads=['lim', 'sc'], writes=['sc'])
    k.op('act', lambda e: e.activation(out=CA(3), in_=CA(1), func=AF.Exp), reads=['sc'], writes=['sc'])
    k.op('dve', lambda e: e.tensor_scalar(out=CA(4), in0=CA(2), scalar1=1.0 / (2 * math.pi), scalar2=None, op0=ALU.mult),
         reads=['sc'], writes=['sc'])
    k.op('pool', lambda e: e.iota(iota_i[:], pattern=[[1, NJ]], base=0, channel_multiplier=0), writes=['iota_i'])
    for t in range(NTL):
        k.op('dve', lambda e: e.tensor_copy(rr[:], iota_i[:]), reads=['iota_i'], writes=['rr'])
        k.op('dve', lambda e: e.tensor_scalar(out=rr[:], in0=rr[:], scalar1=C_(4, t), scalar2=None, op0=ALU.mult),
             reads=['rr', 'sc'], writes=['rr'])
        emit_sin_turns(cx, sinT[t][:], rr[:], 'rr', 'sinT%d' % t, Wt, 'wt')
        k.op('dve', lambda e: e.tensor_scalar(out=rr[:], in0=rr[:], scalar1=0.25, scalar2=None, op0=ALU.add), reads=['rr'], writes=['rr'])
        emit_sin_turns(cx, cosT[t][:], rr[:], 'rr', 'cosT%d' % t, Wt, 'wt')
    for t in range(NTL):
        c1, s1 = cosT[t][:, 1:2], sinT[t][:, 1:2]
        rho = C_(3, t)
        nr, ni, den, cr_, ci_, ta, tb = C_(5, t), C_(6, t), C_(7, t), C_(8, t), C_(9, t), C_(10, t), C_(11, t)
        lr_, li_ = lre[:, t:t + 1], lim[:, t:t + 1]
        ck, sk = 'cosT%d' % t, 'sinT%d' % t
        k.op('dve', lambda e: e.tensor_tensor(out=nr, in0=rho, in1=c1, op=ALU.mult), reads=['sc', ck], writes=['sc'])
        k.op('dve', lambda e: e.tensor_scalar(out=nr, in0=nr, scalar1=-1.0, scalar2=None, op0=ALU.add), reads=['sc'], writes=['sc'])
        k.op('dve', lambda e: e.tensor_tensor(out=ni, in0=rho, in1=s1, op=ALU.mult), reads=['sc', sk], writes=['sc'])
        k.op('dve', lambda e: e.tensor_tensor(out=ta, in0=lr_, in1=lr_, op=ALU.mult), reads=['lre'], writes=['sc'])
        k.op('dve', lambda e: e.scalar_tensor_tensor(out=den, in0=li_, scalar=li_, in1=ta, op0=ALU.mult, op1=ALU.add),
             reads=['lim', 'sc'], writes=['sc'])
        k.op('dve', lambda e: e.reciprocal(out=den, in_=den), reads=['sc'], writes=['sc'])
        k.op('dve', lambda e: e.tensor_tensor(out=ta, in0=nr, in1=lr_, op=ALU.mult), reads=['sc', 'lre'], writes=['sc'])
        k.op('dve', lambda e: e.scalar_tensor_tensor(out=ta, in0=ni, scalar=li_, in1=ta, op0=ALU.mult, op1=ALU.add),
             reads=['sc', 'lim'], writes=['sc'])
        k.op('dve', lambda e: e.tensor_tensor(out=cr_, in0=ta, in1=den, op=ALU.mult), reads=['sc'], writes=['sc'])
        k.op('dve', lambda e: e.tensor_tensor(out=ta, in0=ni, in1=lr_, op=ALU.mult), reads=['sc', 'lre'], writes=['sc'])
        k.op('dve', lambda e: e.tensor_tensor(out=tb, in0=nr, in1=li_, op=ALU.mult), reads=['sc', 'lim'], writes=['sc'])
        k.op('dve', lambda e: e.tensor_tensor(out=ta, in0=ta, in1=tb, op=ALU.subtract), reads=['sc'], writes=['sc'])
        k.op('dve', lambda e: e.tensor_tensor(out=ci_, in0=ta, in1=den, op=ALU.mult), reads=['sc'], writes=['sc'])
        q0, q1, q2, q3 = tmpB[:, 0, :], tmpB[:, 1, :], tmpB[:, 2, :], tmpB[:, 3, :]
        k.op('dve', lambda e: e.tensor_scalar(out=q0, in0=bre[:, t, :], scalar1=cr_, scalar2=None, op0=ALU.mult), reads=['bre', 'sc'], writes=['tmpB'])
        k.op('dve', lambda e: e.tensor_scalar(out=q1, in0=bim[:, t, :], scalar1=ci_, scalar2=None, op0=ALU.mult), reads=['bim', 'sc'], writes=['tmpB'])
        k.op('dve', lambda e: e.tensor_scalar(out=q2, in0=bre[:, t, :], scalar1=ci_, scalar2=None, op0=ALU.mult), reads=['bre', 'sc'], writes=['tmpB'])
        k.op('dve', lambda e: e.tensor_scalar(out=q3, in0=bim[:, t, :], scalar1=cr_, scalar2=None, op0=ALU.mult), reads=['bim', 'sc'], writes=['tmpB'])
        for c in range(2):
            k.op('dve', lambda e, c=c: e.memset(Bbd[c][:], 0.0), writes=['Bbd%d' % c])
        for c in range(3):
            k.op('dve', lambda e, c=c: e.memset(Cbd[t][c][:], 0.0), writes=['Cbd%d_%d' % (t, c)])
        for gi in range(2):
            rs = slice(gi * 64, (gi + 1) * 64)
            cs = slice(gi * 16, (gi + 1) * 16)
            k.op('dve', lambda e: e.tensor_tensor(out=Bbd[0][rs, cs], in0=tmpB[rs, 0, :], in1=tmpB[rs, 1, :], op=ALU.subtract),
                 reads=['tmpB', 'Bbd0'], writes=['Bbd0'])
            k.op('dve', lambda e: e.tensor_tensor(out=Bbd[1][rs, cs], in0=tmpB[rs, 2, :], in1=tmpB[rs, 3, :], op=ALU.add),
                 reads=['tmpB', 'Bbd1'], writes=['Bbd1'])
            k.op('dve', lambda e: e.tensor_copy(Cbd[t][0][rs, cs], cre[rs, t, :]), reads=['cre', 'Cbd%d_0' % t], writes=['Cbd%d_0' % t])
            k.op('dve', lambda e: e.tensor_scalar(out=Cbd[t][1][rs, cs], in0=cim[rs, t, :], scalar1=-1.0, scalar2=None, op0=ALU.mult),
                 reads=['cim', 'Cbd%d_1' % t], writes=['Cbd%d_1' % t])
            k.op('dve', lambda e: e.tensor_scalar(out=Cbd[t][2][rs, cs], in0=cre[rs, t, :], scalar1=-1.0, scalar2=None, op0=ALU.mult),
                 reads=['cre', 'Cbd%d_2' % t], writes=['Cbd%d_2' % t])
        for c in range(2):
            k.op('pe', lambda e, c=c: e.transpose(pm[0:32, c * 128:(c + 1) * 128], Bbd[c][:, :], cx.ident[:]),
                 reads=['Bbd%d' % c, 'ident'], writes=[pmk])
            k.op('act', lambda e, c=c: e.copy(out=BbT[t][c][:], in_=pm[0:32, c * 128:(c + 1) * 128]), reads=[pmk], writes=['BbT%d_%d' % (t, c)])
        k.op('dve', lambda e: e.memset(init[t][:], 0.0), writes=['init%d' % t])

    nchunk = S // LC
    items = [(ch, t) for ch in range(nchunk) for t in range(NTL)]

    def names(it):
        ch, t = items[it]
        b = it % NS
        pb2 = (it % 2) if not co else 0
        W_ = WK[b]
        return ch, t, b, pb2, W_, (lambda n: '%s_%d' % (n, b)), 'cosT%d' % t, 'sinT%d' % t

    def P1(it):
        ch, t, b, pb2, W_, wk, ck, sk = names(it)
        t1, t2, t3, t4, wre, wim = [W_[n] for n in ('t1', 't2', 't3', 't4', 'wre', 'wim')]
        cs_, sn_ = cosT[t][:, 0:LC], sinT[t][:, 0:LC]
        pa, pb_ = pA[pb2], pB[pb2]
        pak, pbk = 'pA%d' % pb2, 'pB%d' % pb2
        k.dma('sp', uf[it % NU][:], uT_d[t * 32:(t + 1) * 32, ch * LC:(ch + 1) * LC], reads=io.get('chunk_keys', lambda c: [])(ch), writes=['uf%d' % (it % NU)])
        k.op('act', lambda e: e.copy(out=ub[it % 2][:], in_=uf[it % NU][:]), reads=['uf%d' % (it % NU)], writes=['ub%d' % (it % 2)])
        k.op('pe', lambda e: e.matmul(pa[:, :], BbT[t][0][:, :], ub[it % 2][:, :], start=True, stop=True),
             reads=['BbT%d_0' % t, 'ub%d' % (it % 2)], writes=[pak])
        k.op('pe', lambda e: e.matmul(pb_[:, :], BbT[t][1][:, :], ub[it % 2][:, :], start=True, stop=True),
             reads=['BbT%d_1' % t, 'ub%d' % (it % 2)], writes=[pbk])
        k.op('dve', lambda e: e.tensor_tensor(out=t1[:], in0=cs_, in1=pa[:, :], op=ALU.mult), reads=[ck, pak], writes=[wk('t1')])
        k.op('dve', lambda e: e.tensor_tensor(out=t2[:], in0=sn_, in1=pb_[:, :], op=ALU.mult), reads=[sk, pbk], writes=[wk('t2')])
        k.op('dve', lambda e: e.tensor_tensor(out=t3[:], in0=cs_, in1=pb_[:, :], op=ALU.mult), reads=[ck, pbk], writes=[wk('t3')])
        k.op('dve', lambda e: e.tensor_tensor(out=t4[:], in0=sn_, in1=pa[:, :], op=ALU.mult), reads=[sk, pak], writes=[wk('t4')])
        k.op('pool', lambda e: e.tensor_tensor(out=wre[:], in0=t1[:], in1=t2[:], op=ALU.add), reads=[wk('t1'), wk('t2')], writes=[wk('wre')])
        k.op('pool', lambda e: e.tensor_tensor(out=wim[:], in0=t3[:], in1=t4[:], op=ALU.subtract), reads=[wk('t3'), wk('t4')], writes=[wk('wim')])

    def P2(it):
        ch, t, b, pb2, W_, wk, ck, sk = names(it)
        wre, wim, vre, vim = [W_[n] for n in ('wre', 'wim', 'vre', 'vim')]
        rho_b = C_(3, t).to_broadcast([128, LC])
        k.op('dve', lambda e: e.tensor_tensor_scan(out=vre[:], data0=rho_b, data1=wre[:], initial=init[t][:, 0:1], op0=ALU.mult, op1=ALU.add),
             reads=['sc', wk('wre'), 'init%d' % t], writes=[wk('vre')])
        k.op('dve', lambda e: e.tensor_tensor_scan(out=vim[:], data0=rho_b, data1=wim[:], initial=init[t][:, 1:2], op0=ALU.mult, op1=ALU.add),
             reads=['sc', wk('wim'), 'init%d' % t], writes=[wk('vim')])
        cL, sL = cosT[t][:, LC:LC + 1], sinT[t][:, LC:LC + 1]
        ta, tb = sc2[:, 2 * t:2 * t + 1], sc2[:, 2 * t + 1:2 * t + 2]
        k.op('dve', lambda e: e.tensor_tensor(out=ta, in0=vim[:, LC - 1:LC], in1=sL, op=ALU.mult), reads=[wk('vim'), sk], writes=['sc2_%d' % t])
        k.op('dve', lambda e: e.tensor_tensor(out=tb, in0=vim[:, LC - 1:LC], in1=cL, op=ALU.mult), reads=[wk('vim'), ck], writes=['sc2_%d' % t])
        k.op('dve', lambda e: e.scalar_tensor_tensor(out=init[t][:, 0:1], in0=vre[:, LC - 1:LC], scalar=cL, in1=ta, op0=ALU.mult, op1=ALU.subtract),
             reads=[wk('vre'), ck, 'sc2_%d' % t], writes=['init%d' % t])
        k.op('dve', lambda e: e.scalar_tensor_tensor(out=init[t][:, 1:2], in0=vre[:, LC - 1:LC], scalar=sL, in1=tb, op0=ALU.mult, op1=ALU.add),
             reads=[wk('vre'), sk, 'sc2_%d' % t], writes=['init%d' % t])

    def P3(it):
        ch, t, b, pb2, W_, wk, ck, sk = names(it)
        vre, vim, z1, z2, z3, z4 = [W_[n] for n in ('vre', 'vim', 'z1', 'z2', 'z3', 'z4')]
        cs_, sn_ = cosT[t][:, 0:LC], sinT[t][:, 0:LC]
        pc, pck = pC[pb2], 'pC%d' % pb2
        k.op('pool', lambda e: e.tensor_tensor(out=z1[:], in0=cs_, in1=vre[:], op=ALU.mult), reads=[ck, wk('vre')], writes=[wk('z1')])
        k.op('pool', lambda e: e.tensor_tensor(out=z2[:], in0=sn_, in1=vim[:], op=ALU.mult), reads=[sk, wk('vim')], writes=[wk('z2')])
        k.op('pool', lambda e: e.tensor_tensor(out=z3[:], in0=sn_, in1=vre[:], op=ALU.mult), reads=[sk, wk('vre')], writes=[wk('z3')])
        k.op('dve', lambda e: e.tensor_tensor(out=z4[:], in0=cs_, in1=vim[:], op=ALU.mult), reads=[ck, wk('vim')], writes=[wk('z4')])
        for ii, (ci, z, zk) in enumerate([(0, z1, 'z1'), (2, z2, 'z2'), (1, z3, 'z3'), (1, z4, 'z4')]):
            k.op('pe', lambda e, ci=ci, z=z, ii=ii: e.matmul(pc[0:32, :], Cbd[t][ci][:, :], z[:, :], start=(ii == 0), stop=(ii == 3)),
                 reads=['Cbd%d_%d' % (t, ci), wk(zk)], writes=[pck], pe_acc=(ii > 0))

    def P4(it):
        ch, t, b, pb2, W_, wk, ck, sk = names(it)
        pc, pck = pC[pb2], 'pC%d' % pb2
        k.op('dve', lambda e: e.scalar_tensor_tensor(out=yo[it % 2][:], in0=uf[it % NU][:], scalar=dsk[:, t:t + 1], in1=pc[0:32, :], op0=ALU.mult, op1=ALU.add),
             reads=['uf%d' % (it % NU), 'dsk', pck], writes=['yo%d' % (it % 2)])
        cx.out_toks.append(k.dma('sp', y_d[t * 32:(t + 1) * 32, ch * LC:(ch + 1) * LC], yo[it % 2][:], reads=['yo%d' % (it % 2)]))

    N = len(items)
    pos = [0]

    def step(i):
        if i < N:
            P1(i)
        if 0 <= i - 1 < N:
            P2(i - 1)
        if 0 <= i - 2 < N:
            P3(i - 2)
            if co:
                P4(i - 2)
        if not co and 0 <= i - 3 < N:
            P4(i - 3)

    def adv(n_chunks_ready, max_items=10 ** 9):
        n = 0
        while pos[0] < N and items[pos[0]][0] < n_chunks_ready and n < max_items:
            step(pos[0])
            pos[0] += 1
            n += 1

    def fin():
        adv(nchunk)
        step(N)
        step(N + 1)
        step(N + 2)

    if co:
        return adv, fin
    fin()


def prep_B1(u_s5, kq, P):
    f = np.float32
    gs = slice(4 * kq, 4 * kq + 4)
    def pg(a):
        return np.ascontiguousarray(a[gs].reshape(2, 128).T)
    def pgh(a):
        return np.ascontiguousarray(a[gs].reshape(2, 128, 16).transpose(1, 0, 2).reshape(128, 32))
    m = {
        "uT": np.ascontiguousarray(u_s5[:, 64 * kq:64 * kq + 64].T),
        "lre": pg(P['s5_lam_re']), "lim": pg(P['s5_lam_im']),
        "ldt": pg(np.repeat(P['s5_log_dt'][:, None], 64, axis=1)),
        "bre": pgh(P['s5_b_re']), "bim": pgh(P['s5_b_im']),
        "cre": pgh(P['s5_c_re'].transpose(0, 2, 1)), "cim": pgh(P['s5_c_im'].transpose(0, 2, 1)),
        "dsk": np.ascontiguousarray(P['s5_d'][gs].reshape(2, 32).T),
    }
    return {k_: np.ascontiguousarray(v, dtype=f) for k_, v in m.items()}


HALO = 16


def emit_rms_rows(cx, out, okey, x, xkey, W_, gain, gkey, junk, sfx=''):
    k = cx.k
    ss, rstd = cx.work['ss' + sfx], cx.work['rstd' + sfx]
    k.op('dve', lambda e: e.scalar_tensor_tensor(out=junk, in0=x, scalar=1.0, in1=x, op0=ALU.mult, op1=ALU.mult, accum_out=ss[:, 0:1]),
         reads=[xkey], writes=['rjunk' + sfx, 'ss' + sfx])
    yield
    k.op('dve', lambda e: e.tensor_scalar(out=ss[:, 0:1], in0=ss[:, 0:1], scalar1=1.0 / W_, scalar2=EPS, op0=ALU.mult, op1=ALU.add),
         reads=['ss' + sfx], writes=['ss' + sfx])
    yield
    k.op('pool', lambda e: e.tensor_tensor(out=rstd[:, 0:1], in0=ss[:, 0:1], in1=cx.work['mhalf'][:, 0:1], op=ALU.pow),
         reads=['ss' + sfx, 'mhalf'], writes=['rstd' + sfx])
    yield
    k.op('dve', lambda e: e.scalar_tensor_tensor(out=out, in0=x, scalar=rstd[:, 0:1], in1=gain, op0=ALU.mult, op1=ALU.mult),
         reads=[xkey, 'rstd' + sfx, gkey], writes=[okey])
    yield


def stage_O(cx, io, ntok):
    k = cx.k
    (x_d, upT_d, corr_d, pwbd_d, prow_d, onorm_d, nsa_d, ysT_d, gluw_d, wo_d, cT_d, adaw_d, adab_d, xo_d) = [io[n] for n in (
        "x", "upT", "corr", "pwbd", "prow", "onorm", "nsa", "ysT", "gluw", "wo", "cT", "adaw", "adab", "xo")]
    CH = 2048
    emit_consts(cx)
    emit_work(cx)
    NP = HALO + CH
    v = cx.sb("v", [128, 2, NP])
    sA = cx.sb("sA", [128, NP])
    sB = cx.sb("sB", [128, NP])
    pooled = cx.sb("pooled", [128, 2, CH], BF16)
    corr = cx.sb("corr", [128, 2, HALO])
    pwbd = cx.sb("pwbd", [128, 2, 128], BF16)
    prow = cx.sb("prow", [128, 4, 256])
    onorm = cx.sb("onorm", [128, D])
    gluw = cx.sb("gluw", [128, 2, 256], BF16)
    Wo = cx.sb("Wo", [128, 8, D], BF16)
    xt = [cx.sb("xt%d" % i, [128, D]) for i in range(4)]
    nsa = [cx.sb("nsa%d" % i, [128, 512]) for i in range(2)]
    ysT = [cx.sb("ysT%d" % i, [128, 2, 128]) for i in range(2)]
    ys_ = [cx.sb("ys%d" % i, [128, 256]) for i in range(2)]
    ycats = [cx.sb("ycat%d" % i, [128, D], BF16) for i in range(4)]
    rjunk = [cx.sb("rjunk%d" % i, [128, 512]) for i in range(2)]
    yp = [cx.sb("yp%d" % i, [128, 256]) for i in range(2)]
    yg = [cx.sb("yg%d" % i, [128, 256]) for i in range(2)]
    gt = [cx.sb("gt%d" % i, [128, 256]) for i in range(2)]
    ygb = [cx.sb("ygb%d" % i, [128, 256], BF16) for i in range(2)]
    ygT = [cx.sb("ygT%d" % i, [128, 2, 128], BF16) for i in range(2)]
    for i in range(2):
        cx.work['ss%d' % i] = cx.sb("ss_%d" % i, [128, 1])
        cx.work['rstd%d' % i] = cx.sb("rstd_%d" % i, [128, 1])
    ycT = cx.sb("ycT", [128, 8, 128], BF16)
    xo = [cx.sb("xo%d" % i, [128, D]) for i in range(2)]
    pT = cx.ps("pT", [128, 8, 128], BF16)
    pqs = [cx.ps("pq%d" % i, [128, 512]) for i in range(2)]
    pz = cx.ps("pz", [128, 512])
    py = [cx.ps("py%d" % i, [128, 512]) for i in range(2)]

    k.dma('sp', corr[:].rearrange("p t h -> p (t h)"), corr_d[:, :], writes=['corr'])
    k.dma('pool', pwbd[:].rearrange("p t d -> p (t d)"), pwbd_d[:, :], writes=['pwbd'])
    k.dma('sp', prow[:].rearrange("p a b -> p (a b)"), prow_d[0:1, :].to_broadcast([128, 1024]), writes=['prow'])
    k.dma('sp', onorm[:], onorm_d[0:1, :].to_broadcast([128, D]), writes=['onorm'])
    k.dma('pool', gluw[:], gluw_d.rearrange("(c p) n -> p c n", p=128), writes=['gluw'])
    for kc in range(8):
        k.dma('pool', Wo[:, kc, :], wo_d[kc * 128:(kc + 1) * 128, :], writes=['Wo'])
    mod_b = emit_mod(cx, cT_d, adaw_d, adab_d, 1, ['py0', 'py1'], py)
    g2_b = mod_b[:, 0, :]
    TT = ALU.add
    for ch in range(ntok // CH):
        for t in range(2):
            if ch == 0:
                k.op('dve', lambda e, t=t: e.memset(v[:, t, 0:HALO], 0.0), writes=['v'])
                k.dma('sp', v[:, t, HALO:NP], upT_d[t * 128:(t + 1) * 128, 0:CH], writes=['v'])
            else:
                k.dma('sp', v[:, t, :], upT_d[t * 128:(t + 1) * 128, ch * CH - HALO:(ch + 1) * CH], writes=['v'])
        for t in range(2):
            vt = v[:, t, :]
            k.op('dve', lambda e: e.tensor_tensor(out=sA[:, 1:NP], in0=vt[:, 1:NP], in1=vt[:, 0:NP - 1], op=TT), reads=['v'], writes=['sA'])
            if t == 0:
                k.op('dve', lambda e: e.tensor_tensor(out=sB[64:128, 3:NP], in0=sA[64:128, 3:NP], in1=sA[64:128, 1:NP - 2], op=TT), reads=['sA'], writes=['sB'])
                srcs = [(sA, 'sA', 0.5), (sB, 'sB', 0.25)]
            else:
                k.op('dve', lambda e: e.tensor_tensor(out=sB[:, 3:NP], in0=sA[:, 3:NP], in1=sA[:, 1:NP - 2], op=TT), reads=['sA'], writes=['sB'])
                k.op('dve', lambda e: e.tensor_tensor(out=sA[:, 7:NP], in0=sB[:, 7:NP], in1=sB[:, 3:NP - 4], op=TT), reads=['sB', 'sA'], writes=['sA'])
                k.op('dve', lambda e: e.tensor_tensor(out=sB[64:128, 15:NP], in0=sA[64:128, 15:NP], in1=sA[64:128, 7:NP - 8], op=TT),
                     reads=['sA', 'sB'], writes=['sB'])
                srcs = [(sA, 'sA', 0.125), (sB, 'sB', 0.0625)]
            for gi, (src, skey, iw) in enumerate(srcs):
                rs = slice(gi * 64, (gi + 1) * 64)
                if ch == 0:
                    k.op('dve', lambda e: e.tensor_tensor(out=src[rs, HALO:2 * HALO], in0=src[rs, HALO:2 * HALO], in1=corr[rs, t, :], op=ALU.mult),
                         reads=[skey, 'corr'], writes=[skey])
                k.op('dve', lambda e: e.scalar_tensor_tensor(out=pooled[rs, t, :], in0=src[rs, HALO:NP], scalar=iw, in1=vt[rs, HALO:NP],
                                                             op0=ALU.mult, op1=ALU.subtract), reads=[skey, 'v'], writes=['pooled'])
        def genA(tl, ch=ch):
            ti = ch * (CH // 128) + tl
            p2, p4 = ti % 2, ti % 4
            sf = str(p2)
            tsl = slice(ti * 128, (ti + 1) * 128)
            lsl = slice(tl * 128, (tl + 1) * 128)
            yc, yck = ycats[p4], 'ycat%d' % p4
            k.dma('act', xt[p4][:], x_d[tsl, :], writes=['xt%d' % p4])
            k.dma('act', nsa[p2][:], nsa_d[tsl, :], writes=['nsa%d' % p2])
            k.dma('act', ysT[p2][:], ysT_d[:, tsl].rearrange("(c p) t -> p c t", p=128), writes=['ysT%d' % p2])
            pqt, pqk = pqs[p2], 'pq%d' % p2
            for t in range(2):
                k.op('pe', lambda e, t=t: e.matmul(pqt[:, t * 128:(t + 1) * 128], pooled[:, t, lsl], pwbd[:, t, :], start=True, stop=True),
                     reads=['pooled', 'pwbd'], writes=[pqk])
            yield
            k.op('dve', lambda e: e.tensor_tensor(out=yp[p2][:], in0=pqt[:, 0:256], in1=prow[:, 0, :], op=ALU.add), reads=[pqk, 'prow'], writes=['yp' + sf])
            yield
            k.op('pool', lambda e: e.tensor_tensor(out=yp[p2][:], in0=yp[p2][:], in1=prow[:, 1, :], op=ALU.mult), reads=['yp' + sf, 'prow'], writes=['yp' + sf])
            yield
            yield from emit_rms_rows(cx, yc[:, 0:256], yck, yp[p2][:], 'yp' + sf, 256, onorm[:, 0:256], 'onorm', rjunk[p2][:, 0:256], sf)
            yield from emit_rms_rows(cx, yc[:, 256:768], yck, nsa[p2][:], 'nsa%d' % p2, 512, onorm[:, 256:768], 'onorm', rjunk[p2][:, 0:512], sf)
            for c in range(2):
                k.op('pe', lambda e, c=c: e.transpose(pz[:, c * 128:(c + 1) * 128], ysT[p2][:, c, :], cx.ident_f[:]),
                     reads=['ysT%d' % p2, 'ident_f'], writes=['pz'])
            y0, y0k = ys_[p2], 'ys' + sf
            g_, gk = gt[p2], 'gt' + sf
            yg_, ygk = yg[p2], 'yg' + sf
            k.op('act', lambda e: e.copy(out=y0[:], in_=pz[:, 0:256]), reads=['pz'], writes=[y0k])
            yield
            k.op('dve', lambda e: e.tensor_tensor(out=g_[:], in0=y0[:], in1=y0[:], op=ALU.mult), reads=[y0k], writes=[gk])
            yield
            k.op('dve', lambda e: e.tensor_scalar(out=g_[:], in0=g_[:], scalar1=0.044715, scalar2=1.0, op0=ALU.mult, op1=ALU.add), reads=[gk], writes=[gk])
            yield
            k.op('pool', lambda e: e.tensor_tensor(out=g_[:], in0=g_[:], in1=y0[:], op=ALU.mult), reads=[gk, y0k], writes=[gk])
            yield
            k.op('act', lambda e: e.activation(out=g_[:], in_=g_[:], func=AF.Sigmoid, scale=1.5957691216057308), reads=[gk], writes=[gk])
            yield
            k.op('dve', lambda e: e.tensor_tensor(out=yg_[:], in0=g_[:], in1=y0[:], op=ALU.mult), reads=[gk, y0k], writes=[ygk])
            yield
            k.op('pool', lambda e: e.tensor_copy(ygb[p2][:], yg_[:]), reads=[ygk], writes=['ygb' + sf])
            yield
            for c in range(2):
                k.op('pe', lambda e, c=c: e.transpose(pT[:, c, :], ygb[p2][:, c * 128:(c + 1) * 128], cx.ident[:]), reads=['ygb' + sf, 'ident'], writes=['pT'], pe_acc=(c > 0))
            k.op('act', lambda e: e.copy(out=ygT[p2][:], in_=pT[:, 0:2, :]), reads=['pT'], writes=['ygT' + sf])
            yield
            for c in range(2):
                k.op('pe', lambda e, c=c: e.matmul(pqt[:, 256:512], ygT[p2][:, c, :], gluw[:, c, :], start=(c == 0), stop=(c == 1)),
                     reads=['ygT' + sf, 'gluw'], writes=[pqk], pe_acc=(c > 0))
            yield
            k.op('dve', lambda e: e.tensor_tensor(out=g_[:], in0=pqt[:, 256:512], in1=prow[:, 2, :], op=ALU.add), reads=[pqk, 'prow'], writes=[gk])
            yield
            k.op('act', lambda e: e.activation(out=g_[:], in_=g_[:], func=AF.Sigmoid), reads=[gk], writes=[gk])
            yield
            k.op('dve', lambda e: e.tensor_tensor(out=yg_[:], in0=yg_[:], in1=g_[:], op=ALU.mult), reads=[ygk, gk], writes=[ygk])
            yield
            yield from emit_rms_rows(cx, yc[:, 768:1024], yck, yg_[:], ygk, 256, onorm[:, 768:1024], 'onorm', rjunk[p2][:, 0:256], sf)

        def genB(tl, ch=ch):
            ti = ch * (CH // 128) + tl
            p2, p4 = ti % 2, ti % 4
            tsl = slice(ti * 128, (ti + 1) * 128)
            yc, yck = ycats[p4], 'ycat%d' % p4
            for c in range(8):
                k.op('pe', lambda e, c=c: e.transpose(pT[:, c, :], yc[:, c * 128:(c + 1) * 128], cx.ident[:]), reads=[yck, 'ident'], writes=['pT'], pe_acc=(c > 0))
            k.op('act', lambda e: e.copy(out=ycT[:], in_=pT[:, :, :]), reads=['pT'], writes=['ycT'])
            for h in range(2):
                for c in range(8):
                    k.op('pe', lambda e, c=c: e.matmul(py[h][:, :], ycT[:, c, :], Wo[:, c, h * 512:(h + 1) * 512], start=(c == 0), stop=(c == 7)),
                         reads=['ycT', 'Wo'], writes=['py%d' % h], pe_acc=(c > 0))
            for h in range(2):
                k.op('dve', lambda e, h=h: e.tensor_tensor(out=xo[p2][:, h * 512:(h + 1) * 512], in0=py[h][:, :], in1=g2_b[:, h * 512:(h + 1) * 512], op=ALU.mult),
                     reads=['py%d' % h, 'mod_b'], writes=['xo%d' % p2])
            k.op('pool', lambda e: e.tensor_tensor(out=xo[p2][:], in0=xo[p2][:], in1=xt[p4][:], op=ALU.add), reads=['xo%d' % p2, 'xt%d' % p4], writes=['xo%d' % p2])
            cx.out_toks.append(k.dma('sp', xo_d[tsl, :], xo[p2][:], reads=['xo%d' % p2]))
            yield

        ntl = CH // 128
        for pr in range(ntl // 2 + 1):
            gens = []
            if pr < ntl // 2:
                gens += [genA(2 * pr), genA(2 * pr + 1)]
            if pr >= 1:
                gens += [genB(2 * pr - 2), genB(2 * pr - 1)]
            while gens:
                for gnr in list(gens):
                    try:
                        next(gnr)
                    except StopIteration:
                        gens.remove(gnr)


NTK = S
DEPTH = 2
TAB_SHAPES = {"caus": [128, 128], "wlow": [128, 128], "mc": [128, 17 * 128], "ovl": [512, 128], "T": [128, 254],
              "eslot": [64, S], "ropec": [512, 16]}
LAYER_SHAPES = {
    "f1_adaw": [D, 3 * D], "f1_adab": [1, 3 * D], "f1_nrm": [1, D], "f1_w_in": [D, 2 * DFF], "f1_w_out": [DFF, D],
    "f2_adaw": [D, 3 * D], "f2_adab": [1, 3 * D], "f2_nrm": [1, D], "f2_w_in": [D, 2 * DFF], "f2_w_out": [DFF, D],
    "p_adaw": [D, 2 * D], "p_adab": [1, 2 * D], "p_nrm": [1, D], "p_w_in": [D, N_IN], "p_gains": [1, 768],
    "s_lre": [512, 2], "s_lim": [512, 2], "s_ldt": [512, 2], "s_bre": [512, 32], "s_bim": [512, 32],
    "s_cre": [512, 32], "s_cim": [512, 32], "s_dsk": [128, 2],
    "n_w1k": [64, 32 * 128], "n_w1v": [64, 32 * 128], "n_pek": [64, 32], "n_pev": [64, 32], "n_w2k": [128, 64],
    "n_w2v": [128, 64], "n_kn0": [1, 64],
    "o_pwbd": [128, 256], "o_prow": [1, 1024], "o_onorm": [1, D], "o_gluw": [256, 256], "o_wo": [D, D],
    "o_adaw": [D, D], "o_adab": [1, D],
}


def build_all():
    cx = Ctx()
    x_d = cx.din("x", [NTK, D])
    cT_d = cx.din("cT", [128, 8])
    rope_d = cx.din("rope", [NTK, 16])
    corr_d = cx.din("corr", [128, 2 * HALO])
    tabs = {n: cx.din(n, sh) for n, sh in TAB_SHAPES.items()}
    W = [{n: cx.din("l%d_%s" % (l, n), sh) for n, sh in LAYER_SHAPES.items()} for l in range(DEPTH)]
    out_d = cx.dout("out", [NTK, D])
    x1_s = cx.dscr("x1_s", [NTK, D])
    x2_s = cx.dscr("x2_s", [NTK, D])
    xl_s = cx.dscr("xl_s", [NTK, D])
    U_s = cx.dscr("U_s", [NTK, N_IN])
    UT_s = cx.dscr("UT_s", [N_IN, NTK])
    nsa_s = cx.dscr("nsa_s", [NTK, 512])
    ysT_s = cx.dscr("ysT_s", [256, NTK])
    xin = x_d
    for l in range(DEPTH):
        w = W[l]
        cx.begin_stage("l%df1_" % l)
        stage_F(cx, {"x": xin, "cT": cT_d, "adaw": w["f1_adaw"], "adab": w["f1_adab"], "nrm": w["f1_nrm"],
                     "w_in": w["f1_w_in"], "w_out": w["f1_w_out"], "xo": x1_s}, NTK)
        cx.end_stage()
        cx.begin_stage("l%dp_" % l)
        stage_P(cx, {"x": x1_s, "cT": cT_d, "adaw": w["p_adaw"], "adab": w["p_adab"], "nrm": w["p_nrm"], "w_in": w["p_w_in"],
                     "gains": w["p_gains"], "rope": rope_d, "u": U_s, "uT": UT_s}, NTK)
        cx.end_stage()
        cx.begin_stage("l%ds_" % l)
        stage_B1(cx, {"uT": UT_s[1560:1816, :], "lre": w["s_lre"], "lim": w["s_lim"], "ldt": w["s_ldt"], "bre": w["s_bre"],
                      "bim": w["s_bim"], "cre": w["s_cre"], "cim": w["s_cim"], "dsk": w["s_dsk"], "yT": ysT_s}, 4)
        cx.end_stage()
        for g in range(2):
            cx.begin_stage("l%dn%d_" % (l, g))

            def load_q(k, jb, Qlo, Qhi, qlk, qhk, g=g):
                tsl = slice(jb * 128, (jb + 1) * 128)
                r0 = 256 + g * 256
                src = UT_s[r0:r0 + 256, tsl].rearrange("(r d) t -> d r t", d=64)
                k.dma('pool', Qlo[0:64, :].rearrange("p (r q) -> p r q", q=128), src, writes=[qlk])
                k.dma('pool', Qhi[0:64, :].rearrange("p (r q) -> p r q", q=128), src, writes=[qhk])

            io = {"load_q": load_q,
                  "ksT": UT_s[768 + 64 * g:768 + 64 * g + 64, :], "kwT": UT_s[896 + 64 * g:896 + 64 * g + 64, :],
                  "kcT": UT_s[1024 + 64 * g:1024 + 64 * g + 64, :], "vcT": UT_s[1152 + 64 * g:1152 + 64 * g + 64, :],
                  "vs": U_s[:, 1280 + 64 * g:1280 + 64 * g + 64], "vw": U_s[:, 1408 + 64 * g:1408 + 64 * g + 64],
                  "gat": U_s[:, 1536 + 12 * g:1536 + 12 * g + 12],
                  "w1k": w["n_w1k"], "w1v": w["n_w1v"], "pek": w["n_pek"], "pev": w["n_pev"], "w2k": w["n_w2k"],
                  "w2v": w["n_w2v"], "kn0": w["n_kn0"],
                  "o": nsa_s[:, g * 256:(g + 1) * 256]}
            io.update(tabs)
            stage_B2(cx, io)
            cx.end_stage()
        cx.begin_stage("l%do_" % l)
        stage_O(cx, {"x": x1_s, "upT": UT_s[0:256, :], "corr": corr_d, "pwbd": w["o_pwbd"], "prow": w["o_prow"],
                     "onorm": w["o_onorm"], "nsa": nsa_s, "ysT": ysT_s, "gluw": w["o_gluw"], "wo": w["o_wo"], "cT": cT_d,
                     "adaw": w["o_adaw"], "adab": w["o_adab"], "xo": x2_s}, NTK)
        cx.end_stage()
        cx.begin_stage("l%df2_" % l)
        xout = out_d if l == DEPTH - 1 else xl_s
        stage_F(cx, {"x": x2_s, "cT": cT_d, "adaw": w["f2_adaw"], "adab": w["f2_adab"], "nrm": w["f2_nrm"],
                     "w_in": w["f2_w_in"], "w_out": w["f2_w_out"], "xo": xout}, NTK)
        cx.end_stage()
        xin = xl_s
    return cx.nc


def _rope_table(n):
    pos = np.arange(n).astype(np.float32)
    inv = np.exp(-math.log(500000.0) * np.arange(8, dtype=np.float32) * (2.0 / 16)).astype(np.float32)
    ang = pos[:, None] * inv[None, :]
    return np.concatenate([np.cos(ang), np.sin(ang)], 1).astype(np.float32)


def _layer_inputs(P):
    f = np.float32
    aw, ab = P['ada_w'], P['ada_b']
    m = {}
    for nm, m0, nrm, wi, wo in (("f1", 0, 'norm_ffn1', 'ffn1_w_in', 'ffn1_w_out'), ("f2", 6, 'norm_ffn2', 'ffn2_w_in', 'ffn2_w_out')):
        m[nm + "_adaw"] = aw[:, m0 * D:(m0 + 3) * D]
        m[nm + "_adab"] = ab[None, m0 * D:(m0 + 3) * D]
        m[nm + "_nrm"] = P[nrm][None, :]
        m[nm + "_w_in"] = P[wi]
        m[nm + "_w_out"] = P[wo]
    m["p_adaw"] = aw[:, 3 * D:5 * D]
    m["p_adab"] = ab[None, 3 * D:5 * D]
    m["p_nrm"] = P['norm_mix'][None, :]
    wi = P['w_in']
    kv = wi[:, 768:1536].reshape(D, 6, 128)
    m["p_w_in"] = np.concatenate([wi[:, 0:768], kv[:, 2], kv[:, 4], kv[:, 0], kv[:, 1], kv[:, 3], kv[:, 5], wi[:, 1536:]], 1)
    m["p_gains"] = np.concatenate([np.tile(P['q_norm'], 8), np.tile(P['k_norm'][1], 2), np.tile(P['k_norm'][2], 2)])[None, :]
    dummy_u = np.zeros((1, 256), f)
    quads = [prep_B1(dummy_u, q, P) for q in range(4)]
    for nm in ("lre", "lim", "ldt", "bre", "bim", "cre", "cim", "dsk"):
        m["s_" + nm] = np.concatenate([qd[nm] for qd in quads], 0)
    m["n_w1k"] = P['cmp_k_w1'].reshape(32, 64, 128).transpose(1, 0, 2).reshape(64, 32 * 128)
    m["n_w1v"] = P['cmp_v_w1'].reshape(32, 64, 128).transpose(1, 0, 2).reshape(64, 32 * 128)
    m["n_pek"] = P['cmp_pe'][0].T
    m["n_pev"] = P['cmp_pe'][1].T
    m["n_w2k"] = P['cmp_k_w2']
    m["n_w2v"] = P['cmp_v_w2']
    m["n_kn0"] = P['k_norm'][0][None, :]
    pwbd = np.zeros((128, 2, 128), f)
    for t in range(2):
        for gi in range(2):
            pwbd[gi * 64:(gi + 1) * 64, t, gi * 64:(gi + 1) * 64] = P['pool_w'][2 * t + gi]
    m["o_pwbd"] = pwbd.reshape(128, 256)
    m["o_prow"] = np.concatenate([P['pool_b'].reshape(-1), P['pool_scale'], P['glu_b'], np.zeros(256, f)])[None, :]
    m["o_onorm"] = P['out_norm'][None, :]
    m["o_gluw"] = P['glu_w']
    m["o_wo"] = P['w_out']
    m["o_adaw"] = aw[:, 5 * D:6 * D]
    m["o_adab"] = ab[None, 5 * D:6 * D]
    return {k_: np.ascontiguousarray(v, dtype=f) for k_, v in m.items()}


_NC = []


def kernel(**inp):
    f = np.float32
    inp = {k_: np.asarray(v, dtype=f) for k_, v in inp.items()}
    x, c = inp['x'], inp['c']
    base = dict(b2_tables())
    base["rope"] = _rope_table(NTK)
    corr = np.ones((128, 2, HALO), f)
    tt = np.arange(HALO) + 1.0
    for t in range(2):
        for gi in range(2):
            w = (2, 4, 8, 16)[2 * t + gi]
            corr[gi * 64:(gi + 1) * 64, t, :] = (w / np.minimum(tt, w))[None, :]
    base["corr"] = corr.reshape(128, 2 * HALO)
    for l in range(DEPTH):
        P = {k_: inp[k_][l] for k_ in inp if k_ not in ('x', 'c')}
        for n, v in _layer_inputs(P).items():
            base["l%d_%s" % (l, n)] = v
    base = {k_: np.ascontiguousarray(v, dtype=f) for k_, v in base.items()}
    maps = []
    for ci in range(8):
        b = ci % 2
        m = dict(base)
        m["x"] = np.ascontiguousarray(x[b])
        m["cT"] = np.ascontiguousarray(c[b].reshape(8, 128).T)
        maps.append(m)
    if not _NC:
        _NC.append(build_all())
    res = run_bass_kernel_spmd(_NC[0], maps, core_ids=list(range(8)))
    return np.stack([res.results[0]["out"], res.results[1]["out"]], 0).astype(f)
```
